# Optimizing a Trainium2 kernel written in Bass

```python
import jax, jax.numpy as jnp
from jax import lax
import numpy as np

D_MODEL = 1024
BATCH = 8
SEQ = 2048
DEPTH = 4

CHUNK = 64
D_POOL = D_MODEL // 2
POOL_WINDOWS = (2, 4, 8, 16)
N_POOL_GROUPS = len(POOL_WINDOWS)
POOL_GROUP_DIM = D_POOL // N_POOL_GROUPS
D_ATTN = D_MODEL - D_POOL
N_HEADS = 8
HEAD_DIM = D_ATTN // N_HEADS
LEFT_CHUNKS = 8
BAND = LEFT_CHUNKS + 1
REL_MAX = 128
REL_MIN = CHUNK - 1
N_REL = REL_MIN + REL_MAX + 1
D_FF = 4 * D_MODEL
D_IN = D_POOL + 3 * D_ATTN
N_MOD = 6
ALPHA = (2.0 * DEPTH) ** 0.25
BETA = (8.0 * DEPTH) ** -0.25
LN_EPS = 1e-5
MASK_VALUE = -1e30

kernel_name = "hybrid_pool_chunkattn_deepnorm_adaln"


def layer_norm(x, g, b):
    x32 = x.astype(jnp.float32)
    mu = jnp.mean(x32, axis=-1, keepdims=True)
    var = jnp.mean(jnp.square(x32 - mu), axis=-1, keepdims=True)
    y = (x32 - mu) * lax.rsqrt(var + LN_EPS)
    return (y * g.astype(jnp.float32) + b.astype(jnp.float32)).astype(x.dtype)


def pool_mixer(u, w_pool, pool_scale):
    S = u.shape[1]
    u32 = u.astype(jnp.float32)
    cs = jnp.pad(jnp.cumsum(u32, axis=1), ((0, 0), (1, 0), (0, 0)))
    t = jnp.arange(S)
    outs = []
    for g, w in enumerate(POOL_WINDOWS):
        lo, hi = g * POOL_GROUP_DIM, (g + 1) * POOL_GROUP_DIM
        start = jnp.maximum(t + 1 - w, 0)
        count = (t + 1 - start).astype(jnp.float32)
        window_sum = cs[:, 1:, lo:hi] - cs[:, start, lo:hi]
        pooled = window_sum / count[None, :, None] - u32[..., lo:hi]
        outs.append(jnp.einsum('bsc,cd->bsd', pooled.astype(u.dtype), w_pool[g]))
    return jnp.concatenate(outs, axis=-1) * pool_scale


def chunked_attention(q, k, v, rel_bias):
    B, S, H, Dh = q.shape
    NC = S // CHUNK
    qc = q.reshape(B, NC, CHUNK, H, Dh)
    pad = ((0, 0), (LEFT_CHUNKS, 0), (0, 0), (0, 0), (0, 0))
    kc = jnp.pad(k.reshape(B, NC, CHUNK, H, Dh), pad)
    vc = jnp.pad(v.reshape(B, NC, CHUNK, H, Dh), pad)
    band_idx = jnp.arange(NC)[:, None] + jnp.arange(BAND)[None, :]
    kb = kc[:, band_idx].reshape(B, NC, BAND * CHUNK, H, Dh)
    vb = vc[:, band_idx].reshape(B, NC, BAND * CHUNK, H, Dh)
    scores = jnp.einsum('bnqhd,bnkhd->bnhqk', qc, kb).astype(jnp.float32) * (Dh ** -0.5)
    q_pos = jnp.arange(CHUNK) + LEFT_CHUNKS * CHUNK
    k_pos = jnp.arange(BAND * CHUNK)
    rel = jnp.clip(q_pos[:, None] - k_pos[None, :], -REL_MIN, REL_MAX) + REL_MIN
    bias = rel_bias.astype(jnp.float32)[:, rel]
    valid = jnp.repeat(band_idx >= LEFT_CHUNKS, CHUNK, axis=1)
    scores = jnp.where(valid[None, :, None, None, :], scores + bias[None, None], MASK_VALUE)
    probs = jax.nn.softmax(scores, axis=-1).astype(v.dtype)
    out = jnp.einsum('bnhqk,bnkhd->bnqhd', probs, vb)
    return out.reshape(B, S, H * Dh)


def setup_inputs(seed: int = 0) -> dict:
    key = jax.random.key(seed)
    ks = jax.random.split(key, 16)
    f32 = jnp.float32
    nrm = lambda k, shape, s: jax.random.normal(k, shape, f32) * s
    return {
        "x": nrm(ks[0], (BATCH, SEQ, D_MODEL), 1.0),
        "c": nrm(ks[1], (BATCH, D_MODEL), 1.0),
        "w_ada": nrm(ks[2], (DEPTH, D_MODEL, N_MOD * D_MODEL), 0.1 * D_MODEL ** -0.5),
        "b_ada": nrm(ks[3], (DEPTH, N_MOD * D_MODEL), 0.02),
        "w_in": nrm(ks[4], (DEPTH, D_MODEL, D_IN), D_MODEL ** -0.5),
        "w_pool": nrm(ks[5], (DEPTH, N_POOL_GROUPS, POOL_GROUP_DIM, POOL_GROUP_DIM), POOL_GROUP_DIM ** -0.5),
        "pool_scale": 1.0 + nrm(ks[6], (DEPTH, D_POOL), 0.1),
        "rel_bias": nrm(ks[7], (DEPTH, N_HEADS, N_REL), 0.1),
        "w_out": nrm(ks[8], (DEPTH, D_MODEL, D_MODEL), BETA * D_MODEL ** -0.5),
        "ln1_g": 1.0 + nrm(ks[9], (DEPTH, D_MODEL), 0.02),
        "ln1_b": nrm(ks[10], (DEPTH, D_MODEL), 0.02),
        "w_ff1": nrm(ks[11], (DEPTH, D_MODEL, D_FF), D_MODEL ** -0.5),
        "w_ff2": nrm(ks[12], (DEPTH, D_FF, D_MODEL), BETA * D_FF ** -0.5),
        "ln2_g": 1.0 + nrm(ks[13], (DEPTH, D_MODEL), 0.02),
        "ln2_b": nrm(ks[14], (DEPTH, D_MODEL), 0.02),
    }


def reference(x, c, w_ada, b_ada, w_in, w_pool, pool_scale, rel_bias, w_out,
              ln1_g, ln1_b, w_ff1, w_ff2, ln2_g, ln2_b):
    B, S, _ = x.shape
    c_act = jax.nn.silu(c)
    for l in range(DEPTH):
        mod = jnp.einsum('bd,de->be', c_act, w_ada[l]) + b_ada[l]
        sh1, sc1, g1, sh2, sc2, g2 = [m[:, None, :] for m in jnp.split(mod, N_MOD, axis=-1)]
        h = x * (1.0 + sc1) + sh1
        proj = jnp.einsum('bsd,de->bse', h, w_in[l])
        u = proj[..., :D_POOL]
        q, k, v = jnp.split(proj[..., D_POOL:], 3, axis=-1)
        q = q.reshape(B, S, N_HEADS, HEAD_DIM)
        k = k.reshape(B, S, N_HEADS, HEAD_DIM)
        v = v.reshape(B, S, N_HEADS, HEAD_DIM)
        y_pool = pool_mixer(u, w_pool[l], pool_scale[l])
        y_attn = chunked_attention(q, k, v, rel_bias[l])
        y = jnp.einsum('bse,ed->bsd', jnp.concatenate([y_pool, y_attn], axis=-1), w_out[l])
        x = layer_norm(ALPHA * x + (1.0 + g1) * y, ln1_g[l], ln1_b[l])
        h = x * (1.0 + sc2) + sh2
        f = jnp.square(jax.nn.relu(jnp.einsum('bsd,df->bsf', h, w_ff1[l])))
        f = jnp.einsum('bsf,fd->bsd', f, w_ff2[l])
        x = layer_norm(ALPHA * x + (1.0 + g2) * f, ln2_g[l], ln2_b[l])
    return x
```

```python
import numpy as np
import concourse.bass as bass
import concourse.mybir as mybir
from concourse.bass_utils import run_bass_kernel_spmd

F32 = mybir.dt.float32
BF16 = mybir.dt.bfloat16
ALU = mybir.AluOpType
AF = mybir.ActivationFunctionType

D = 1024
S = 2048
DEPTH = 4
NCORES = 8
ALPHA = (2.0 * DEPTH) ** 0.25
LN_EPS = 1e-5
EPS_P = LN_EPS / (ALPHA * ALPHA)
MASKV = -100.0
NPP = 92
POOL_W = (2, 4, 8, 16)


class _Seg:
    __slots__ = ("w", "r")

    def __init__(self, w=None, r=None):
        self.w = w
        self.r = dict(r) if r else {}


class _Region:
    def __init__(self, size):
        self.b = [0, size]
        self.s = [_Seg()]

    def _split(self, x):
        import bisect
        i = bisect.bisect_left(self.b, x)
        if self.b[i] == x:
            return
        old = self.s[i - 1]
        self.b.insert(i, x)
        self.s.insert(i, _Seg(old.w, old.r))

    def segs(self, lo, hi):
        import bisect
        assert 0 <= lo < hi <= self.b[-1], (lo, hi, self.b[-1])
        self._split(lo)
        self._split(hi)
        i0 = bisect.bisect_left(self.b, lo)
        i1 = bisect.bisect_left(self.b, hi)
        return self.s[i0:i1]


class Sched:
    ENG = ("pe", "act", "dve", "pool", "sp")

    def __init__(self, nc):
        self.nc = nc
        self.sems = {}
        self.count = {}
        self.prog = {e: [] for e in self.ENG}
        self.seen = {e: {} for e in self.ENG}
        self.regions = {}
        for e in self.ENG:
            self._sem(e)

    def _sem(self, key):
        if key not in self.sems:
            self.sems[key] = self.nc.alloc_semaphore("s_" + str(key))
            self.count[key] = 0
        return self.sems[key]

    def region(self, name, size):
        self.regions[name] = _Region(size)

    def _deps(self, eng, reads, writes):
        deps = {}

        def add(d):
            if d is None:
                return
            k, v = d
            if deps.get(k, 0) < v:
                deps[k] = v

        for (rg, lo, hi) in reads:
            for sg in self.regions[rg].segs(lo, hi):
                add(sg.w)
        for (rg, lo, hi) in writes:
            for sg in self.regions[rg].segs(lo, hi):
                add(sg.w)
                for k, v in sg.r.items():
                    add((k, v))
        return deps

    def _emit_waits(self, eng, deps):
        seen = self.seen[eng]
        for k, v in deps.items():
            if k == "pe" and eng == "pe":
                continue
            if seen.get(k, 0) >= v:
                continue
            seen[k] = v
            self.prog[eng].append(("wait", k, v))

    def _mark(self, reads, writes, key, val):
        for (rg, lo, hi) in reads:
            for sg in self.regions[rg].segs(lo, hi):
                if sg.r.get(key, 0) < val:
                    sg.r[key] = val
        for (rg, lo, hi) in writes:
            for sg in self.regions[rg].segs(lo, hi):
                sg.w = (key, val)
                sg.r = {}

    @staticmethod
    def _bank(rs):
        return [((rg, 0, 512) if rg.startswith("PS") else (rg, lo, hi)) for (rg, lo, hi) in rs]

    def record(self, fn):
        self._rec = []
        try:
            fn()
        finally:
            rec, self._rec = self._rec, None
        steps, cur = [], []
        for call in rec:
            cur.append(call)
            if not (call[0] == "op" and call[1][0] == "pe"):
                steps.append(cur)
                cur = []
        if cur:
            steps.append(cur)
        return steps

    def replay(self, step):
        for kind, a, k in step:
            (self.op if kind == "op" else self.dma)(*a, **k)

    def zip_emit(self, sa, sb):
        na, nb = len(sa), len(sb)
        ia = ib = 0
        while ia < na or ib < nb:
            if ib >= nb or (ia < na and ia * nb <= ib * na):
                self.replay(sa[ia]); ia += 1
            else:
                self.replay(sb[ib]); ib += 1

    def op(self, eng, fn, reads=(), writes=(), signal=True, drain=False):
        if getattr(self, "_rec", None) is not None:
            self._rec.append(("op", (eng, fn), dict(reads=reads, writes=writes, signal=signal, drain=drain)))
            return
        reads, writes = self._bank(reads), self._bank(writes)
        deps = self._deps(eng, reads, writes)
        self._emit_waits(eng, deps)
        if drain and self.count[eng] > 0:
            self.prog[eng].append(("wait", eng, self.count[eng]))
        if signal:
            self.count[eng] += 1
            val = self.count[eng]
            self.prog[eng].append(("op", fn, eng, 1))
        else:
            val = self.count[eng] + 1
            self.prog[eng].append(("op", fn, None, 0))
        self._mark(reads, writes, eng, val)

    def dma(self, eng, chan, out, in_, reads=(), writes=()):
        if getattr(self, "_rec", None) is not None:
            self._rec.append(("dma", (eng, chan, out, in_), dict(reads=reads, writes=writes)))
            return
        self._sem(chan)
        deps = self._deps(eng, reads, writes)
        self._emit_waits(eng, deps)
        self.count[chan] += 16
        val = self.count[chan]
        self.prog[eng].append(("op", _I("dma_start", out=out, in_=in_), chan, 16))
        self._mark(reads, writes, chan, val)

    def final_wait(self, eng, chans):
        for c in chans:
            self.prog[eng].append(("wait", c, self.count[c]))

    def emit(self, block):
        nc = self.nc
        sems = self.sems

        def run(engobj, name):
            for it in self.prog[name]:
                if it[0] == "wait":
                    engobj.wait_ge(sems[it[1]], it[2])
                else:
                    ins = it[1](engobj)
                    if it[2] is not None:
                        ins.then_inc(sems[it[2]], it[3])

        @block.tensor
        def _(e):
            run(e, "pe")

        @block.scalar
        def _(e):
            run(e, "act")

        @block.vector
        def _(e):
            run(e, "dve")

        @block.gpsimd
        def _(e):
            run(e, "pool")

        @block.sync
        def _(e):
            run(e, "sp")


def _I(name, *a, **k):
    return lambda e: getattr(e, name)(*a, **k)


def _ap3(base2d, dims):
    return bass.AP(base2d.tensor, base2d.offset, [list(base2d.ap[0])] + [list(d) for d in dims])


class _Stop(Exception):
    pass


DBG_STOP = None


def build_nc(L=DEPTH):
    nc = bass.Bass("TRN2", target_bir_lowering=False)
    dram = {}

    def din(name, shape):
        dram[name] = nc.dram_tensor(name, list(shape), F32, kind="ExternalInput").ap()
        return dram[name]

    xT_d = din("xT", [D, S])
    cT_d = din("cT", [128, 8])
    pp_d = din("pp", [128, DEPTH * NPP])
    bt_d = din("bt", [DEPTH, 128, 8 * 3 * 128])
    ident_d = din("ident", [128, 128])
    w_ada_d = din("w_ada", [DEPTH, D, 6 * D])
    w_in_d = din("w_in", [DEPTH, D, 2048])
    w_pool_d = din("w_pool", [DEPTH, 4, 128, 128])
    w_out_d = din("w_out", [DEPTH, D, D])
    w_ff1_d = din("w_ff1", [DEPTH, D, 4 * D])
    w_ff2_d = din("w_ff2", [DEPTH, 4 * D, D])
    oT_d = nc.dram_tensor("oT", [D, S], F32, kind="ExternalOutput").ap()

    sb = {}
    SB_SPECS = [
        ("X", 16384, F32), ("A", 16384, BF16), ("Y", 16384, BF16), ("B", 16384, BF16),
        ("E", 4096, BF16), ("BT", 3072, BF16), ("PP", DEPTH * NPP, F32), ("T", 4608, F32),
        ("MOD", 48, F32), ("COEF", 64, F32), ("CT", 8, F32), ("CACT", 8, BF16),
        ("ONES", 128, BF16), ("INVC", 16, F32), ("WP", 512, BF16),
        ("ZQ", 4096, BF16), ("EPS", 1, F32), ("ROW", 512, F32), ("ONE1", 1, F32), ("PF", 16, F32), ("IDENT", 128, BF16),
    ]
    for name, n, dt in SB_SPECS:
        sb[name] = nc.alloc_sbuf_tensor("sb_" + name, [128, n], dt)
    ps = [nc.alloc_psum_tensor("ps%d" % i, [128, 512], F32) for i in range(8)]

    sc = Sched(nc)
    for name, n, dt in SB_SPECS:
        sc.region(name, n)
    for i in range(8):
        sc.region("PS%d" % i, 512)

    X, A, Y, B, E, BT, PP, T = (sb[k] for k in ("X", "A", "Y", "B", "E", "BT", "PP", "T"))
    MOD, COEF, CT, CACT, ONES, INVC, WP = (sb[k] for k in ("MOD", "COEF", "CT", "CACT", "ONES", "INVC", "WP"))
    ZQ, EPSC, ROW, ONE1, PF, IDENT = sb["ZQ"], sb["EPS"], sb["ROW"], sb["ONE1"], sb["PF"], sb["IDENT"]
    coef_eps = EPSC[:, 0:1]
    Tb = T[:, :].bitcast(BF16)

    def coef(i):
        return COEF[:, i * 8:(i + 1) * 8]

    def rC(i):
        return ("COEF", i * 8, i * 8 + 8)

    def ppv(l, off, n):
        return PP[:, l * NPP + off: l * NPP + off + n]

    rPP = ("PP", 0, DEPTH * NPP)

    for c in range(8):
        sc.dma("sp", "x%d" % c, X[:, c * S:(c + 1) * S], xT_d[c * 128:(c + 1) * 128, :],
               writes=[("X", c * S, (c + 1) * S)])
    sc.dma("sp", "consts", CT[:, :], cT_d, writes=[("CT", 0, 8)])
    sc.dma("sp", "consts2", PP[:, :], pp_d, writes=[rPP])
    sc.dma("pool", "ident", IDENT[:, :], ident_d, writes=[("IDENT", 0, 128)])
    sc.op("dve", _I("memset", ONES[:, :], 1.0 / 1024.0), writes=[("ONES", 0, 128)])
    sc.op("dve", _I("memset", EPSC[:, :], EPS_P), writes=[("EPS", 0, 1)])
    sc.op("dve", _I("memset", ONE1[:, :], 1.0), writes=[("ONE1", 0, 1)])
    for t in range(16):
        sc.op("dve", _I("memset", INVC[:, t:t + 1], 1.0 / (t + 1)), writes=[("INVC", t, t + 1)])
    sc.op("act", _I("activation", out=CACT[:, :], in_=CT[:, :], func=AF.Silu),
          reads=[("CT", 0, 8)], writes=[("CACT", 0, 8)])

    est = {"n": 0}

    egen = {}

    def e_load(src_ap):
        s = est["n"] % 4
        egen[s] = est["n"]
        est["n"] += 1
        dst = _ap3(E[:, s * 1024:(s + 1) * 1024], [[128, 8], [1, 128]])
        sc.dma("pool", "e%d" % s, dst, src_ap, writes=[("E", s * 1024, (s + 1) * 1024)])
        return s + 4 * egen[s]

    def e_slot(tok):
        s = tok % 4
        assert egen[s] == tok // 4, "E-ring slot %d reused before its consumers were emitted" % s
        return s

    def e_w(s, k):
        return E[:, s * 1024 + k * 128: s * 1024 + (k + 1) * 128]

    def rE(s):
        return ("E", s * 1024, (s + 1) * 1024)

    psrot = {"n": 0}

    def next_ps(lo=0, n=6):
        i = lo + psrot["n"] % n
        psrot["n"] += 1
        return i

    def ada_steps(l, big_ring=False):
        wv = w_ada_d[l].rearrange("(k p) e -> p k e", p=128)
        rowb, colb = 6, 7
        steps = []

        def mk(nb, kp):
            def f():
                if big_ring:
                    s = (nb * 4 + kp) % 16
                    RG, rgname, chan = B, "B", "a%d" % s
                else:
                    s = est["n"] % 4
                    egen[s] = est["n"]
                    est["n"] += 1
                    RG, rgname, chan = E, "E", "e%d" % s
                dst = _ap3(RG[:, s * 1024:(s + 1) * 1024], [[512, 2], [1, 512]])
                sc.dma("pool", chan, dst, wv[:, 2 * kp:2 * kp + 2, nb * 512:(nb + 1) * 512],
                       writes=[(rgname, s * 1024, (s + 1) * 1024)])
                for kk in range(2):
                    k = 2 * kp + kk
                    sc.op("pe", _I("matmul", ps[rowb][0:1, :], lhsT=CACT[:, k:k + 1],
                                   rhs=RG[:, s * 1024 + kk * 512: s * 1024 + (kk + 1) * 512], start=(k == 0), stop=(k == 7)),
                          reads=[(rgname, s * 1024, (s + 1) * 1024), ("CACT", 0, 8)], writes=[("PS%d" % rowb, 0, 512)],
                          signal=(k == 7))
                if kp == 3:
                    sc.op("act", _I("activation", out=ROW[0:1, :], in_=ps[rowb][0:1, :], func=AF.Copy),
                          reads=[("PS%d" % rowb, 0, 512)], writes=[("ROW", 0, 512)])
                    for q in range(4):
                        j = nb * 4 + q
                        sc.op("pe", _I("matmul", ps[colb][:, j:j + 1], lhsT=ROW[0:1, q * 128:(q + 1) * 128], rhs=ONE1[0:1, 0:1],
                                       start=True, stop=True),
                              reads=[("ROW", 0, 512), ("ONE1", 0, 1)], writes=[("PS%d" % colb, j, j + 1)], signal=(q == 3))
                if nb == 11 and kp == 3:
                    sc.op("dve", _I("tensor_tensor", out=MOD[:, :], in0=ps[colb][:, 0:48], in1=ppv(l, 0, 48), op=ALU.add),
                          reads=[("PS%d" % colb, 0, 48), rPP], writes=[("MOD", 0, 48)])
            return f

        for nb in range(12):
            for kp in range(4):
                steps.append(mk(nb, kp))
        return steps

    def emit_ada(l, bank):
        for f in ada_steps(l, big_ring=(l == 0)):
            f()

    def emit_coef_layer(l):
        rM = ("MOD", 0, 48)
        sc.op("dve", _I("tensor_scalar", out=coef(2), in0=MOD[:, 16:24], scalar1=1.0, scalar2=1.0 / ALPHA,
                                               op0=ALU.add, op1=ALU.mult), reads=[rM], writes=[rC(2)])
        sc.op("dve", _I("tensor_scalar", out=coef(5), in0=MOD[:, 40:48], scalar1=1.0, scalar2=1.0 / ALPHA,
                                               op0=ALU.add, op1=ALU.mult), reads=[rM], writes=[rC(5)])
        sc.op("dve", _I("scalar_tensor_tensor", out=coef(3), in0=MOD[:, 32:40], scalar=1.0, in1=ppv(l, 52, 8),
                                                      op0=ALU.add, op1=ALU.mult), reads=[rM, rPP], writes=[rC(3)])
        sc.op("dve", _I("scalar_tensor_tensor", out=coef(4), in0=MOD[:, 32:40], scalar=1.0, in1=ppv(l, 60, 8),
                                                      op0=ALU.add, op1=ALU.mult), reads=[rM, rPP], writes=[rC(4)])
        sc.op("dve", _I("tensor_tensor", out=coef(4), in0=coef(4), in1=MOD[:, 24:32], op=ALU.add),
              reads=[rM, rC(4)], writes=[rC(4)])

    def emit_coef_next(lprev):
        rM = ("MOD", 0, 48)
        sc.op("dve", _I("scalar_tensor_tensor", out=coef(6), in0=MOD[:, 8:16], scalar=1.0, in1=ppv(lprev, 68, 8),
                                                      op0=ALU.add, op1=ALU.mult), reads=[rM, rPP], writes=[rC(6)])
        sc.op("dve", _I("scalar_tensor_tensor", out=coef(7), in0=MOD[:, 8:16], scalar=1.0, in1=ppv(lprev, 76, 8),
                                                      op0=ALU.add, op1=ALU.mult), reads=[rM, rPP], writes=[rC(7)])
        sc.op("dve", _I("tensor_tensor", out=coef(7), in0=coef(7), in1=MOD[:, 0:8], op=ALU.add),
              reads=[rM, rC(7)], writes=[rC(7)])

    zb = Tb[:, 0:4096]

    def emit_ln_p1(b):
        c0 = b * 512
        zv = _ap3(X[:, c0:c0 + 512], [[S, 8], [1, 512]])
        rX = [("X", k * S + c0, k * S + c0 + 512) for k in range(8)]
        zb3 = _ap3(zb, [[512, 8], [1, 512]])
        bm, bq = (4, 5) if b % 2 == 0 else (6, 7)
        sc.op("dve", _I("tensor_copy", out=zb3, in_=zv), reads=rX, writes=[("T", 0, 2048)])
        for k in range(8):
            sc.op("act", _I("activation", out=ZQ[:, k * 512:(k + 1) * 512], in_=X[:, k * S + c0:k * S + c0 + 512], func=AF.Square),
                  reads=[rX[k]], writes=[("ZQ", k * 512, (k + 1) * 512)])
        for k in range(8):
            sc.op("pe", _I("matmul", ps[bm][:, :], lhsT=ONES[:, :], rhs=zb[:, k * 512:(k + 1) * 512],
                           start=(k == 0), stop=(k == 7)),
                  reads=[("T", k * 256, (k + 1) * 256), ("ONES", 0, 128)], writes=[("PS%d" % bm, 0, 512)], signal=(k == 7))
        for k in range(8):
            sc.op("pe", _I("matmul", ps[bq][:, :], lhsT=ONES[:, :], rhs=ZQ[:, k * 512:(k + 1) * 512],
                           start=(k == 0), stop=(k == 7)),
                  reads=[("ZQ", k * 512, (k + 1) * 512), ("ONES", 0, 128)], writes=[("PS%d" % bq, 0, 512)], signal=(k == 7))

    def emit_ln_p2(b, l, which, last):
        c0 = b * 512
        zv = _ap3(X[:, c0:c0 + 512], [[S, 8], [1, 512]])
        rX = [("X", k * S + c0, k * S + c0 + 512) for k in range(8)]
        bm, bq = (4, 5) if b % 2 == 0 else (6, 7)
        mean_sb, tmp, rstd = T[:, 2048:2560], T[:, 2560:3072], T[:, 3072:3584]
        rMean, rTmp, rRstd = ("T", 2048, 2560), ("T", 2560, 3072), ("T", 3072, 3584)
        sc.op("act", _I("activation", out=mean_sb, in_=ps[bm][:, :], func=AF.Copy), reads=[("PS%d" % bm, 0, 512)], writes=[rMean])
        sc.op("act", _I("activation", out=tmp, in_=ps[bm][:, :], func=AF.Square), reads=[("PS%d" % bm, 0, 512)], writes=[rTmp])
        sc.op("dve", _I("tensor_tensor", out=tmp, in0=ps[bq][:, :], in1=tmp, op=ALU.subtract),
              reads=[("PS%d" % bq, 0, 512), rTmp], writes=[rTmp])
        sc.op("act", _I("activation", out=tmp, in_=tmp, func=AF.Ln, bias=coef_eps), reads=[rTmp, ("EPS", 0, 1)], writes=[rTmp])
        sc.op("act", _I("activation", out=rstd, in_=tmp, func=AF.Exp, scale=-0.5), reads=[rTmp], writes=[rRstd])
        sc.op("dve", _I("scalar_tensor_tensor", out=mean_sb, in0=mean_sb, scalar=-1.0, in1=rstd, op0=ALU.mult, op1=ALU.mult),
              reads=[rMean, rRstd], writes=[rMean])
        nmr_b = _ap3(mean_sb, [[0, 8], [1, 512]])
        rstd_b = _ap3(rstd, [[0, 8], [1, 512]])
        sc.op("dve", _I("tensor_tensor", out=zv, in0=zv, in1=rstd_b, op=ALU.mult), reads=rX + [rRstd], writes=rX)
        sc.op("dve", _I("tensor_tensor", out=zv, in0=zv, in1=nmr_b, op=ALU.add), reads=rX + [rMean], writes=rX)
        if which == 1:
            ca, cb_, g_off, b_off = 3, 4, 52, 60
        else:
            ca, cb_, g_off, b_off = 6, 7, 68, 76
        for k in range(8):
            xk = X[:, k * S + c0: k * S + c0 + 512]
            rXk = ("X", k * S + c0, k * S + c0 + 512)
            if not (which == 2 and last):
                ak = A[:, k * S + c0: k * S + c0 + 512]
                sc.op("act", _I("activation", out=ak, in_=xk, func=AF.Identity,
                                scale=coef(ca)[:, k:k + 1], bias=coef(cb_)[:, k:k + 1]),
                      reads=[rXk, rC(ca), rC(cb_)], writes=[("A", k * S + c0, k * S + c0 + 512)])
            sc.op("dve", _I("tensor_scalar", out=xk, in0=xk, scalar1=ppv(l, g_off + k, 1),
                            scalar2=ppv(l, b_off + k, 1), op0=ALU.mult, op1=ALU.add),
                  reads=[rXk, rPP], writes=[rXk])

    def emit_ln(b, l, which, last):
        emit_ln_p1(b)
        emit_ln_p2(b, l, which, last)

    state = {}

    def out_block(b):
        for c in range(8):
            lo = c * S + b * 512
            sc.dma("sp", "out", oT_d[c * 128:(c + 1) * 128, b * 512:(b + 1) * 512], X[:, lo:lo + 512], reads=[("X", lo, lo + 512)])
        if b == 3:
            state["out_done"] = True

    def stop(tag):
        if DBG_STOP == tag:
            raise _Stop()

    def body():
        emit_ada(0, 6)
        rM = ("MOD", 0, 48)
        sc.op("dve", _I("tensor_scalar_add", out=coef(0), in0=MOD[:, 8:16], scalar1=1.0), reads=[rM], writes=[rC(0)])
        sc.op("dve", _I("tensor_copy", out=coef(1), in_=MOD[:, 0:8]), reads=[rM], writes=[rC(1)])
        for k in range(8):
            for b in range(4):
                lo = k * S + b * 512
                if (k + b) % 2 == 0:
                    sc.op("act", _I("activation", out=A[:, lo:lo + 512], in_=X[:, lo:lo + 512], func=AF.Identity,
                                    scale=coef(0)[:, k:k + 1], bias=coef(1)[:, k:k + 1]),
                          reads=[("X", lo, lo + 512), rC(0), rC(1)], writes=[("A", lo, lo + 512)])
                else:
                    sc.op("dve", _I("tensor_scalar", out=A[:, lo:lo + 512], in0=X[:, lo:lo + 512],
                                    scalar1=coef(0)[:, k:k + 1], scalar2=coef(1)[:, k:k + 1], op0=ALU.mult, op1=ALU.add),
                          reads=[("X", lo, lo + 512), rC(0), rC(1)], writes=[("A", lo, lo + 512)])

        stop('h0')
        pre0 = {}
        for l in range(L):
            last = (l == L - 1)
            emit_coef_layer(l)
            win = w_in_d[l].rearrange("(k p) e -> p k e", p=128)
            sc.dma("pool", "wp", _ap3(WP[:, 0:512], [[128, 4], [1, 128]]), w_pool_d[l].rearrange("g c d -> c g d"),
                   writes=[("WP", 0, 512)])
            sc.dma("pool", "bt", BT[:, :], bt_d[l], writes=[("BT", 0, 3072)])

            ZQf = ZQ[:, :].bitcast(F32)

            def pool_bufs(bb):
                if bb == 0:
                    return [(ZQf[:, o:o + 528], ("ZQ", 2 * o, 2 * (o + 528))) for o in (0, 528, 1056)]
                return [(T[:, o:o + 528], ("T", o, o + 528)) for o in (1664, 2192, 2720)]

            pstate = {}

            def pool_F1(g, b):
                def f():
                    if b == 0:
                        pstate["s"] = e_load(win[:, :, g * 128:(g + 1) * 128])
                    s = e_slot(pstate["s"])
                    (u, rU), _, _ = pool_bufs(b % 2)
                    bank = next_ps(6, 2)
                    for k in range(8):
                        sc.op("pe", _I("matmul", ps[bank][:, :], lhsT=e_w(s, k),
                                       rhs=A[:, k * S + b * 512: k * S + (b + 1) * 512], start=(k == 0), stop=(k == 7)),
                              reads=[rE(s), ("A", k * S + b * 512, k * S + (b + 1) * 512)],
                              writes=[("PS%d" % bank, 0, 512)], signal=(k == 7))
                    rg, lo_, hi_ = rU
                    sc_ = 2 if rg == "ZQ" else 1
                    if b == 0:
                        sc.op("dve", _I("memset", u[:, 0:16], 0.0), writes=[(rg, lo_, lo_ + 16 * sc_)])
                    else:
                        (pu, rPU), _, _ = pool_bufs((b - 1) % 2)
                        prg, plo, phi = rPU
                        sc.op("dve", _I("tensor_copy", out=u[:, 0:16], in_=pu[:, 512:528]),
                              reads=[(prg, phi - 16 * (2 if prg == "ZQ" else 1), phi)], writes=[(rg, lo_, lo_ + 16 * sc_)])
                    sc.op("act", _I("activation", out=u[:, 16:528], in_=ps[bank][:, :], func=AF.Copy),
                          reads=[("PS%d" % bank, 0, 512)], writes=[(rg, lo_ + 16 * sc_, hi_)])
                    w = POOL_W[g]
                    (u, rU), (sa, rSa), (sb_, rSb) = pool_bufs(b % 2)
                    src, rsrc = u, rU
                    bufs = [(sa, rSa), (sb_, rSb)]
                    step, lo2 = 1, 1
                    nst = {2: 1, 4: 2, 8: 3, 16: 4}[w]
                    for it in range(nst):
                        dst, rdst = bufs[it % 2]
                        sc.op("dve", _I("tensor_tensor", out=dst[:, lo2:528], in0=src[:, lo2:528],
                                         in1=src[:, lo2 - step:528 - step], op=ALU.add), reads=[rsrc], writes=[rdst])
                        src, rsrc = dst, rdst
                        step *= 2
                        lo2 = 2 * lo2 + 1
                    pdst, rpd = bufs[nst % 2]
                    if rpd[0] == "ZQ":
                        pbf = ZQ[:, rpd[1]:rpd[1] + 512]
                    else:
                        pbf = Tb[:, 2 * rpd[1]:2 * rpd[1] + 512]
                    if b == 0:
                        sc.op("dve", _I("tensor_tensor", out=PF[:, 0:w - 1], in0=src[:, 16:16 + w - 1], in1=INVC[:, 0:w - 1],
                                         op=ALU.mult), reads=[rsrc, ("INVC", 0, 16)], writes=[("PF", 0, 16)])
                    sc.op("dve", _I("tensor_scalar_mul", out=src[:, 16:528], in0=src[:, 16:528], scalar1=1.0 / w),
                          reads=[rsrc], writes=[rsrc])
                    sc.op("dve", _I("tensor_tensor", out=pbf, in0=src[:, 16:528], in1=u[:, 16:528], op=ALU.subtract),
                          reads=[rsrc, rU], writes=[rpd])
                    if b == 0:
                        sc.op("dve", _I("tensor_tensor", out=pbf[:, 0:w - 1], in0=PF[:, 0:w - 1], in1=u[:, 16:16 + w - 1],
                                         op=ALU.subtract), reads=[("PF", 0, 16), rU], writes=[rpd])
                    pstate[(g, b)] = (pbf, rpd)
                return f

            def pool_F3(g, b):
                def f():
                    pbf, rpd = pstate[(g, b)]
                    bank2 = next_ps(6, 2)
                    sc.op("pe", _I("matmul", ps[bank2][:, :], lhsT=WP[:, g * 128:(g + 1) * 128], rhs=pbf, start=True, stop=True),
                          reads=[rpd, ("WP", 0, 512)], writes=[("PS%d" % bank2, 0, 512)])
                    yl = g * S + b * 512
                    sc.op("act", _I("activation", out=Y[:, yl:yl + 512], in_=ps[bank2][:, :], func=AF.Copy, scale=ppv(l, 48 + g, 1)),
                          reads=[("PS%d" % bank2, 0, 512), rPP], writes=[("Y", yl, yl + 512)])
                return f

            gb = [(g, b) for g in range(4) for b in range(4)]
            pool_pieces = []
            for idx in range(18):
                fs = []
                if idx >= 2:
                    fs.append(pool_F3(*gb[idx - 2]))
                if idx < 16:
                    fs.append(pool_F1(*gb[idx]))
                pool_pieces.append(lambda fs=fs: [f() for f in fs])

            stop('pool')

            def qkv_pieces(c, wsrc=None, pslo=6, psn=2):
                wsrc = win if wsrc is None else wsrc
                j = c % 2
                bo = j * 8192
                qo, ko, vo = bo, bo + 2048, bo + 4096
                pieces = []
                st = {}

                def mk_load(key, col0):
                    def f():
                        st[key] = e_load(wsrc[:, :, col0 + c * 128: col0 + (c + 1) * 128])
                    return f

                def mk_qk(key, dsto, scl, b):
                    def f():
                        s = e_slot(st[key])
                        bank = next_ps(pslo, psn)
                        for k in range(8):
                            sc.op("pe", _I("matmul", ps[bank][:, :], lhsT=e_w(s, k),
                                           rhs=A[:, k * S + b * 512: k * S + (b + 1) * 512], start=(k == 0), stop=(k == 7)),
                                  reads=[rE(s), ("A", k * S + b * 512, k * S + (b + 1) * 512)],
                                  writes=[("PS%d" % bank, 0, 512)], signal=(k == 7))
                        dl = dsto + b * 512
                        sc.op("act", _I("activation", out=B[:, dl:dl + 512], in_=ps[bank][:, :], func=AF.Copy, scale=scl),
                              reads=[("PS%d" % bank, 0, 512)], writes=[("B", dl, dl + 512)])
                    return f

                def mk_v(t):
                    def f():
                        s = e_slot(st["v"])
                        tt = t % 4
                        if tt == 0:
                            st["vbank"] = next_ps(pslo, psn)
                        bank = st["vbank"]
                        for k in range(8):
                            sc.op("pe", _I("matmul", ps[bank][:, tt * 128:(tt + 1) * 128],
                                           lhsT=A[:, k * S + t * 128: k * S + (t + 1) * 128], rhs=e_w(s, k),
                                           start=(k == 0), stop=(k == 7)),
                                  reads=[rE(s), ("A", k * S + t * 128, k * S + (t + 1) * 128)],
                                  writes=[("PS%d" % bank, tt * 128, (tt + 1) * 128)], signal=(k == 7))
                        if tt == 3:
                            vt = vo + (t - 3) * 192
                            psv = ps[bank]
                            srcA = _ap3(psv[:, 0:64], [[128, 4], [1, 64]])
                            srcB = _ap3(psv[:, 64:128], [[128, 4], [1, 64]])
                            dstA = _ap3(B[:, vt:vt + 64], [[192, 4], [1, 64]])
                            dstB = _ap3(B[:, vt + 128:vt + 192], [[192, 4], [1, 64]])
                            rV = ("B", vt, vt + 4 * 192)
                            sc.op("dve", _I("tensor_copy", out=dstA, in_=srcA), reads=[("PS%d" % bank, 0, 512)], writes=[rV])
                            sc.op("dve", _I("tensor_copy", out=dstB, in_=srcB), reads=[("PS%d" % bank, 0, 512)], writes=[rV])
                    return f

                def mk_ones():
                    def f():
                        onesv = _ap3(B[:, vo + 64: vo + 128], [[192, 16], [1, 64]])
                        sc.op("dve", _I("memset", onesv, 1.0), writes=[("B", vo, vo + 3072)])
                        st["v"] = e_load(wsrc[:, :, 1536 + c * 128: 1536 + (c + 1) * 128])
                    return f

                pieces.append(mk_load("q", 512))
                for b in range(4):
                    pieces.append(mk_qk("q", qo, 0.125, b))
                pieces.append(mk_load("k", 1024))
                for b in range(4):
                    pieces.append(mk_qk("k", ko, 1.0, b))
                pieces.append(mk_ones())
                for t in range(16):
                    pieces.append(mk_v(t))
                return pieces

            def att_S(u, c, m, X_):
                j = c % 2
                bo = j * 8192
                qo, ko = bo, bo + 2048
                sset = u % 2
                U0, U1 = 2 * sset, 2 * sset + 1
                r0 = 64 * X_
                valid = list(range(max(0, 4 - m), 5))
                h = 2 * c + X_
                v034 = [i for i in valid if i in (0, 3, 4)]
                s_lo = 3 - len(v034)
                nn = len(v034) * 128
                bto = h * 384 + s_lo * 128
                sc.op("pe", _I("matmul", ps[U1][:, s_lo * 128: s_lo * 128 + nn], lhsT=IDENT[:, :], rhs=BT[:, bto:bto + nn],
                               start=True, stop=False),
                      reads=[("IDENT", 0, 128), ("BT", bto, bto + nn)], writes=[("PS%d" % U1, 0, 512)], signal=False)
                order = [i for i in valid if i in (0, 3, 4)] + [i for i in valid if i in (1, 2)]
                for i in order:
                    tk = m - 4 + i
                    if i in (1, 2):
                        bank_, col, first = U0, (i - 1) * 128, True
                    else:
                        bank_, col, first = U1, {0: 0, 3: 1, 4: 2}[i] * 128, False
                    sc.op("pe", _I("matmul", ps[bank_][:, col:col + 128],
                                   lhsT=B[r0:r0 + 64, ko + tk * 128: ko + (tk + 1) * 128],
                                   rhs=B[r0:r0 + 64, qo + m * 128: qo + (m + 1) * 128], start=first,
                                   stop=(first or i == 4)),
                          reads=[("B", ko + tk * 128, ko + (tk + 1) * 128), ("B", qo + m * 128, qo + (m + 1) * 128)],
                          writes=[("PS%d" % bank_, col, col + 128)], signal=(i == order[-1] or i == 4))

            def att_SM(u, c, m, X_):
                sset = u % 2
                U0, U1 = 2 * sset, 2 * sset + 1
                h = 2 * c + X_
                valid = list(range(max(0, 4 - m), 5))
                sbo = sset * 384
                pto = (768 + sset * 320) * 2
                v12 = [i for i in valid if i in (1, 2)]
                if v12:
                    c_lo = (v12[0] - 1) * 128
                    n12 = len(v12) * 128
                    dlo = pto + (v12[0] - 1) * 128
                    sc.op("act", _I("activation", out=Tb[:, dlo:dlo + n12], in_=ps[U0][:, c_lo:c_lo + n12],
                                    func=AF.Exp, bias=ppv(l, 84 + h, 1)),
                          reads=[("PS%d" % U0, c_lo, c_lo + n12), rPP], writes=[("T", dlo // 2, (dlo + n12) // 2)])
                v034 = [i for i in valid if i in (0, 3, 4)]
                s_lo = {3: 0, 2: 1, 1: 2}[len(v034)]
                nn = len(v034) * 128
                d0 = pto + (2 + s_lo) * 128
                sc.op("act", _I("activation", out=Tb[:, d0:d0 + nn], in_=ps[U1][:, s_lo * 128: s_lo * 128 + nn], func=AF.Exp),
                      reads=[("PS%d" % U1, 0, 512)], writes=[("T", d0 // 2, (d0 + nn) // 2)])

            def att_PV(u, c, m, X_):
                j = c % 2
                vo = j * 8192 + 4096
                sset = u % 2
                PO = 4 + sset
                pto = (768 + sset * 320) * 2
                valid = list(range(max(0, 4 - m), 5))
                vc = 0 if X_ == 0 else 64
                for n_, i in enumerate(valid):
                    tk = m - 4 + i
                    sl = {1: 0, 2: 1, 0: 2, 3: 3, 4: 4}[i]
                    pc = pto + sl * 128
                    vcol = vo + tk * 192 + vc
                    sc.op("pe", _I("matmul", ps[PO][:, 0:128], lhsT=B[:, vcol:vcol + 128], rhs=Tb[:, pc:pc + 128],
                                   start=(n_ == 0), stop=(n_ == len(valid) - 1)),
                          reads=[("B", vcol, vcol + 128), ("T", pc // 2, (pc + 128) // 2)],
                          writes=[("PS%d" % PO, 0, 128)], signal=(n_ == len(valid) - 1))

            def att_NORM(u, c, m, X_):
                sset = u % 2
                PO = 4 + sset
                dn = 1408 + sset * 128
                yc = (4 + c) * S + m * 128
                if X_ == 0:
                    nlo, dlo_ = 0, 64
                else:
                    nlo, dlo_ = 64, 0
                if X_ == 0:
                    sc.op("dve", _I("reciprocal", out=T[nlo:nlo + 64, dn:dn + 128], in_=ps[PO][dlo_:dlo_ + 64, 0:128]),
                          reads=[("PS%d" % PO, 0, 128)], writes=[("T", dn, dn + 128)])
                else:
                    sc.op("act", _I("activation", out=T[nlo:nlo + 64, dn:dn + 128], in_=ps[PO][dlo_:dlo_ + 64, 0:128], func=AF.Ln),
                          reads=[("PS%d" % PO, 0, 128)], writes=[("T", dn, dn + 128)])
                    sc.op("act", _I("activation", out=T[nlo:nlo + 64, dn:dn + 128], in_=T[nlo:nlo + 64, dn:dn + 128],
                                    func=AF.Exp, scale=-1.0),
                          reads=[("T", dn, dn + 128)], writes=[("T", dn, dn + 128)])
                sc.op("dve", _I("tensor_tensor", out=Y[nlo:nlo + 64, yc:yc + 128], in0=ps[PO][nlo:nlo + 64, 0:128],
                                in1=T[nlo:nlo + 64, dn:dn + 128], op=ALU.mult),
                      reads=[("PS%d" % PO, 0, 128), ("T", dn, dn + 128)], writes=[("Y", yc, yc + 128)])

            units = [(c, m, X_) for c in range(4) for m in range(16) for X_ in range(2)]
            if not pre0.get(l):
                for f in qkv_pieces(0):
                    f()
            stop('att_a')
            fill = []
            for u, (c, m, X_) in enumerate(units):
                if m == 0 and X_ == 0:
                    for f in fill:
                        f()
                    fill = (qkv_pieces(c + 1) if c + 1 < 4 else []) + pool_pieces[6 * c:6 * c + 6]
                    if c == 3:
                        wo = w_out_d[l].rearrange("(k p) e -> p k e", p=128)
                        for hh in range(2):
                            sc.dma("pool", "wo%d" % hh, _ap3(B[:, hh * 4096:(hh + 1) * 4096], [[1024, 4], [1, 1024]]),
                                   wo[:, hh * 4:(hh + 1) * 4, :], writes=[("B", hh * 4096, (hh + 1) * 4096)])
                if u == 0:
                    att_S(u, c, m, X_)
                    att_SM(u, c, m, X_)
                if u + 1 < len(units):
                    att_S(u + 1, *units[u + 1])
                    att_SM(u + 1, *units[u + 1])
                att_PV(u, c, m, X_)
                if fill:
                    fill.pop(0)()
                att_NORM(u, c, m, X_)

            stop('attn')
            w1v = w_ff1_d[l].rearrange("(k p) f -> p k f", p=128)
            w2v = w_ff2_d[l].rearrange("(fc p) d -> p fc d", p=128)

            def load_chunk(jp, ci):
                so_ = 8192 if jp % 2 == 0 else 0
                ch = "f%d_%d" % (ci, jp % 2)
                if ci < 4:
                    lo = so_ + ci * 1024
                    sc.dma("pool", ch, _ap3(B[:, lo:lo + 1024], [[512, 2], [1, 512]]),
                           w1v[:, 2 * ci:2 * ci + 2, jp * 512:(jp + 1) * 512], writes=[("B", lo, lo + 1024)])
                else:
                    fc = ci - 4
                    lo = so_ + 4096 + fc * 1024
                    sc.dma("pool", ch, B[:, lo:lo + 1024], w2v[:, jp * 4 + fc, :], writes=[("B", lo, lo + 1024)])

            def load_piece(jp):
                for ci in range(8):
                    load_chunk(jp, ci)

            load_piece(0)

            def s1d_mm(b):
                for dc in range(8):
                    bank = next_ps(0, 4)
                    for k in range(8):
                        sc.op("pe", _I("matmul",
                            ps[bank][:, :], lhsT=B[:, k * 1024 + dc * 128: k * 1024 + (dc + 1) * 128],
                            rhs=Y[:, k * S + b * 512: k * S + (b + 1) * 512], start=(k == 0), stop=(k == 7)),
                            reads=[("B", k * 1024 + dc * 128, k * 1024 + (dc + 1) * 128), ("Y", k * S + b * 512, k * S + (b + 1) * 512)],
                            writes=[("PS%d" % bank, 0, 512)], signal=(k == 7))
                    xl = dc * S + b * 512
                    sc.op("dve", _I("scalar_tensor_tensor",
                        out=X[:, xl:xl + 512], in0=ps[bank][:, :], scalar=coef(2)[:, dc:dc + 1], in1=X[:, xl:xl + 512],
                        op0=ALU.mult, op1=ALU.add),
                        reads=[("PS%d" % bank, 0, 512), rC(2), ("X", xl, xl + 512)], writes=[("X", xl, xl + 512)])

            asteps = ada_steps(l + 1) if not last else []

            def ffn_ff1(jp, b, narrow):
                so_ = 8192 if jp % 2 == 0 else 0
                if 1 <= jp <= 6:
                    for _ in range(2):
                        if asteps:
                            asteps.pop(0)()
                    if not asteps and not last and jp == 6 and b == 3:
                        emit_coef_next(l)
                fb = (jp * 4 + b) % 3
                fo = fb * S
                for fc in range(4):
                    bank = next_ps(0, 2) if narrow else next_ps(0, 3)
                    for k in range(8):
                        sc.op("pe", _I("matmul",
                            ps[bank][:, :], lhsT=B[:, so_ + k * 512 + fc * 128: so_ + k * 512 + (fc + 1) * 128],
                            rhs=A[:, k * S + b * 512: k * S + (b + 1) * 512], start=(k == 0), stop=(k == 7)),
                            reads=[("B", so_ + k * 512 + fc * 128, so_ + k * 512 + (fc + 1) * 128),
                                   ("A", k * S + b * 512, k * S + (b + 1) * 512)],
                            writes=[("PS%d" % bank, 0, 512)], signal=(k == 7))
                    rt = 3584 + (fc % 2) * 512
                    sc.op("act", _I("activation", out=T[:, rt:rt + 512], in_=ps[bank][:, :], func=AF.Relu),
                          reads=[("PS%d" % bank, 0, 512)], writes=[("T", rt, rt + 512)])
                    fl = fo + fc * 512
                    sc.op("act", _I("activation", out=Y[:, fl:fl + 512], in_=T[:, rt:rt + 512], func=AF.Square),
                          reads=[("T", rt, rt + 512)], writes=[("Y", fl, fl + 512)])

            def ffn_ff2(jp, b, narrow):
                so_ = 8192 if jp % 2 == 0 else 0
                if jp + 1 < 8:
                    load_chunk(jp + 1, 2 * b)
                    load_chunk(jp + 1, 2 * b + 1)
                fb = (jp * 4 + b) % 3
                fo = fb * S
                for dc in range(8):
                    bank = next_ps(2, 2) if narrow else next_ps(3, 3)
                    for fc in range(4):
                        sc.op("pe", _I("matmul",
                            ps[bank][:, :], lhsT=B[:, so_ + 4096 + fc * 1024 + dc * 128: so_ + 4096 + fc * 1024 + (dc + 1) * 128],
                            rhs=Y[:, fo + fc * 512: fo + (fc + 1) * 512], start=(fc == 0), stop=(fc == 3)),
                            reads=[("B", so_ + 4096 + fc * 1024 + dc * 128, so_ + 4096 + fc * 1024 + (dc + 1) * 128),
                                   ("Y", fo + fc * 512, fo + (fc + 1) * 512)],
                            writes=[("PS%d" % bank, 0, 512)], signal=(fc == 3))
                    xl = dc * S + b * 512
                    sc.op("dve", _I("scalar_tensor_tensor",
                        out=X[:, xl:xl + 512], in0=ps[bank][:, :], scalar=coef(5)[:, dc:dc + 1], in1=X[:, xl:xl + 512],
                        op0=ALU.mult, op1=ALU.add),
                        reads=[("PS%d" % bank, 0, 512), rC(5), ("X", xl, xl + 512)], writes=[("X", xl, xl + 512)])

            def ffn_slot(jp, b, narrow):
                ffn_ff1(jp, b, narrow)
                ffn_ff2(jp, b, narrow)

            R = sc.record

            def mmp1(b):
                return R(lambda: (s1d_mm(b), emit_ln_p1(b)))

            def p2(b, which):
                return R(lambda: emit_ln_p2(b, l, which, last))

            def zip_p2(sa, b, which):
                sb_ = p2(b, which)
                for st_ in sb_[:3]:
                    sc.replay(st_)
                sc.zip_emit(sa, sb_[3:])

            s1d_mm(0); emit_ln_p1(0)
            s1d_mm(1); emit_ln_p1(1)
            zip_p2(mmp1(2), 0, 1)
            zip_p2(mmp1(3), 1, 1)
            zip_p2(R(lambda: ffn_slot(0, 0, True)), 2, 1)
            zip_p2(R(lambda: ffn_slot(0, 1, True)), 3, 1)
            ffn_slot(0, 2, True)
            ffn_slot(0, 3, True)
            stop('ln1')
            slots = [(jp, b) for jp in range(1, 7) for b in range(4)]
            ffn_ff1(slots[0][0], slots[0][1], False)
            for i, (jp, b) in enumerate(slots):
                if i + 1 < len(slots):
                    ffn_ff1(slots[i + 1][0], slots[i + 1][1], False)
                ffn_ff2(jp, b, False)
            ffn_slot(7, 0, True); emit_ln_p1(0)
            ffn_slot(7, 1, True); emit_ln_p1(1)
            zip_p2(R(lambda: (ffn_slot(7, 2, True), emit_ln_p1(2))), 0, 2)
            if last:
                out_block(0)
            zip_p2(R(lambda: (ffn_slot(7, 3, True), emit_ln_p1(3))), 1, 2)
            if last:
                out_block(1)
            stop('ffn')
            if not last:
                wnext = w_in_d[l + 1].rearrange("(k p) e -> p k e", p=128)
                pcs = qkv_pieces(0, wsrc=wnext, pslo=0, psn=4)
                for i_ in (0, 5, 10):
                    pcs[i_]()

                def grp(b):
                    return [pcs[1 + b], pcs[6 + b]] + [pcs[11 + 4 * b + t] for t in range(4)]
                zip_p2(R(lambda: [f() for f in grp(0) + grp(1)]), 2, 2)
                zip_p2(R(lambda: [f() for f in grp(2)]), 3, 2)
                for f in grp(3):
                    f()
                pre0[l + 1] = True
            else:
                emit_ln_p2(2, l, 2, last)
                out_block(2)
                emit_ln_p2(3, l, 2, last)
                out_block(3)

    try:
        body()
    except _Stop:
        pass

    if DBG_STOP is not None or not state.get("out_done"):
        for c in range(8):
            sc.dma("sp", "out", oT_d[c * 128:(c + 1) * 128, :], X[:, c * S:(c + 1) * S], reads=[("X", c * S, (c + 1) * S)])
    sc.final_wait("sp", ["out"])

    with nc.Block() as block:
        sc.emit(block)
    return nc


def _bias_tables(rel_bias):
    p = np.arange(128)[:, None]
    q = np.arange(128)[None, :]
    out = np.empty((DEPTH, 128, 8, 3, 128), np.float32)
    for bi, i in enumerate((0, 3, 4)):
        diff = 128 * (4 - i) + q - p
        idx = np.clip(diff, -63, 128) + 63
        tab = rel_bias[:, :, idx]
        if i == 0:
            mask = (p < 64) & (q >= 64)
        elif i == 4:
            mask = (p >= 64) & (q < 64)
        else:
            mask = np.zeros((128, 128), bool)
        tab = np.where(mask[None, None], np.float32(MASKV), tab)
        out[:, :, :, bi, :] = np.transpose(tab, (0, 2, 1, 3))
    return out.reshape(DEPTH, 128, 8 * 3 * 128)


def _col(v):
    return np.ascontiguousarray(v.reshape(-1, 128).T)


_NC_CACHE = {}


def kernel(x, c, w_ada, b_ada, w_in, w_pool, pool_scale, rel_bias, w_out,
           ln1_g, ln1_b, w_ff1, w_ff2, ln2_g, ln2_b):
    f = lambda a: np.ascontiguousarray(np.asarray(a, dtype=np.float32))
    x, c, w_ada, b_ada, w_in, w_pool, pool_scale, rel_bias, w_out = map(
        f, (x, c, w_ada, b_ada, w_in, w_pool, pool_scale, rel_bias, w_out))
    ln1_g, ln1_b, w_ff1, w_ff2, ln2_g, ln2_b = map(f, (ln1_g, ln1_b, w_ff1, w_ff2, ln2_g, ln2_b))
    pp = np.zeros((128, DEPTH * NPP), np.float32)
    for l in range(DEPTH):
        o = l * NPP
        pp[:, o:o + 48] = _col(b_ada[l])
        pp[:, o + 48:o + 52] = _col(pool_scale[l])
        pp[:, o + 52:o + 60] = _col(ln1_g[l])
        pp[:, o + 60:o + 68] = _col(ln1_b[l])
        pp[:, o + 68:o + 76] = _col(ln2_g[l])
        pp[:, o + 76:o + 84] = _col(ln2_b[l])
        pp[:, o + 84:o + 92] = np.broadcast_to(rel_bias[l, :, 191][None, :], (128, 8))
    bt = _bias_tables(rel_bias)
    L = _NC_CACHE.get("L", DEPTH)
    if ("nc", L) not in _NC_CACHE:
        _NC_CACHE[("nc", L)] = build_nc(L)
    nc = _NC_CACHE[("nc", L)]
    in_maps = []
    for b in range(NCORES):
        in_maps.append({
            "xT": np.ascontiguousarray(x[b].T), "cT": _col(c[b]), "pp": pp, "bt": bt, "ident": np.eye(128, dtype=np.float32),
            "w_ada": w_ada, "w_in": w_in, "w_pool": w_pool, "w_out": w_out, "w_ff1": w_ff1, "w_ff2": w_ff2,
        })
    res = run_bass_kernel_spmd(nc, in_maps, core_ids=list(range(NCORES)))
    out = np.empty((NCORES, S, D), np.float32)
    for b in range(NCORES):
        out[b] = np.asarray(res.results[b]["oT"]).T
    return out
```

```python
import numpy as np
import concourse.bass as bass
import concourse.mybir as mybir
from concourse.bass_utils import run_bass_kernel_spmd

F32 = mybir.dt.float32
BF16 = mybir.dt.bfloat16
ALU = mybir.AluOpType
AF = mybir.ActivationFunctionType

D = 1024
S = 2048
DEPTH = 4
NCORES = 8
ALPHA = (2.0 * DEPTH) ** 0.25
LN_EPS = 1e-5
EPS_P = LN_EPS / (ALPHA * ALPHA)
MASKV = -100.0
NPP = 92
POOL_W = (2, 4, 8, 16)


class _Seg:
    __slots__ = ("w", "r")

    def __init__(self, w=None, r=None):
        self.w = w
        self.r = dict(r) if r else {}


class _Region:
    def __init__(self, size):
        self.b = [0, size]
        self.s = [_Seg()]

    def _split(self, x):
        import bisect
        i = bisect.bisect_left(self.b, x)
        if self.b[i] == x:
            return
        old = self.s[i - 1]
        self.b.insert(i, x)
        self.s.insert(i, _Seg(old.w, old.r))

    def segs(self, lo, hi):
        import bisect
        assert 0 <= lo < hi <= self.b[-1], (lo, hi, self.b[-1])
        self._split(lo)
        self._split(hi)
        i0 = bisect.bisect_left(self.b, lo)
        i1 = bisect.bisect_left(self.b, hi)
        return self.s[i0:i1]


class Sched:
    ENG = ("pe", "act", "dve", "pool", "sp")

    def __init__(self, nc):
        self.nc = nc
        self.sems = {}
        self.count = {}
        self.prog = {e: [] for e in self.ENG}
        self.seen = {e: {} for e in self.ENG}
        self.regions = {}
        for e in self.ENG:
            self._sem(e)

    def _sem(self, key):
        if key not in self.sems:
            self.sems[key] = self.nc.alloc_semaphore("s_" + str(key))
            self.count[key] = 0
        return self.sems[key]

    def region(self, name, size):
        self.regions[name] = _Region(size)

    def _deps(self, eng, reads, writes):
        deps = {}

        def add(d):
            if d is None:
                return
            k, v = d
            if deps.get(k, 0) < v:
                deps[k] = v

        for (rg, lo, hi) in reads:
            for sg in self.regions[rg].segs(lo, hi):
                add(sg.w)
        for (rg, lo, hi) in writes:
            for sg in self.regions[rg].segs(lo, hi):
                add(sg.w)
                for k, v in sg.r.items():
                    add((k, v))
        return deps

    def _emit_waits(self, eng, deps):
        seen = self.seen[eng]
        for k, v in deps.items():
            if k == "pe" and eng == "pe":
                continue
            if seen.get(k, 0) >= v:
                continue
            seen[k] = v
            self.prog[eng].append(("wait", k, v))

    def _mark(self, reads, writes, key, val):
        for (rg, lo, hi) in reads:
            for sg in self.regions[rg].segs(lo, hi):
                if sg.r.get(key, 0) < val:
                    sg.r[key] = val
        for (rg, lo, hi) in writes:
            for sg in self.regions[rg].segs(lo, hi):
                sg.w = (key, val)
                sg.r = {}

    @staticmethod
    def _bank(rs):
        return [((rg, 0, 512) if rg.startswith("PS") else (rg, lo, hi)) for (rg, lo, hi) in rs]

    def record(self, fn):
        self._rec = []
        try:
            fn()
        finally:
            rec, self._rec = self._rec, None
        steps, cur = [], []
        for call in rec:
            cur.append(call)
            if not (call[0] == "op" and call[1][0] == "pe"):
                steps.append(cur)
                cur = []
        if cur:
            steps.append(cur)
        return steps

    def replay(self, step):
        for kind, a, k in step:
            (self.op if kind == "op" else self.dma)(*a, **k)

    def zip_emit(self, sa, sb):
        na, nb = len(sa), len(sb)
        ia = ib = 0
        while ia < na or ib < nb:
            if ib >= nb or (ia < na and ia * nb <= ib * na):
                self.replay(sa[ia]); ia += 1
            else:
                self.replay(sb[ib]); ib += 1

    def op(self, eng, fn, reads=(), writes=(), signal=True, drain=False):
        if getattr(self, "_rec", None) is not None:
            self._rec.append(("op", (eng, fn), dict(reads=reads, writes=writes, signal=signal, drain=drain)))
            return
        reads, writes = self._bank(reads), self._bank(writes)
        deps = self._deps(eng, reads, writes)
        self._emit_waits(eng, deps)
        if drain and self.count[eng] > 0:
            self.prog[eng].append(("wait", eng, self.count[eng]))
        if signal:
            self.count[eng] += 1
            val = self.count[eng]
            self.prog[eng].append(("op", fn, eng, 1))
        else:
            val = self.count[eng] + 1
            self.prog[eng].append(("op", fn, None, 0))
        self._mark(reads, writes, eng, val)

    def dma(self, eng, chan, out, in_, reads=(), writes=()):
        if getattr(self, "_rec", None) is not None:
            self._rec.append(("dma", (eng, chan, out, in_), dict(reads=reads, writes=writes)))
            return
        self._sem(chan)
        deps = self._deps(eng, reads, writes)
        self._emit_waits(eng, deps)
        self.count[chan] += 16
        val = self.count[chan]
        self.prog[eng].append(("op", _I("dma_start", out=out, in_=in_), chan, 16))
        self._mark(reads, writes, chan, val)

    def final_wait(self, eng, chans):
        for c in chans:
            self.prog[eng].append(("wait", c, self.count[c]))

    def emit(self, block):
        nc = self.nc
        sems = self.sems

        def run(engobj, name):
            for it in self.prog[name]:
                if it[0] == "wait":
                    engobj.wait_ge(sems[it[1]], it[2])
                else:
                    ins = it[1](engobj)
                    if it[2] is not None:
                        ins.then_inc(sems[it[2]], it[3])

        @block.tensor
        def _(e):
            run(e, "pe")

        @block.scalar
        def _(e):
            run(e, "act")

        @block.vector
        def _(e):
            run(e, "dve")

        @block.gpsimd
        def _(e):
            run(e, "pool")

        @block.sync
        def _(e):
            run(e, "sp")


def _I(name, *a, **k):
    return lambda e: getattr(e, name)(*a, **k)


def _ap3(base2d, dims):
    return bass.AP(base2d.tensor, base2d.offset, [list(base2d.ap[0])] + [list(d) for d in dims])


class _Stop(Exception):
    pass


DBG_STOP = None


def build_nc(L=DEPTH):
    nc = bass.Bass("TRN2", target_bir_lowering=False)
    dram = {}

    def din(name, shape):
        dram[name] = nc.dram_tensor(name, list(shape), F32, kind="ExternalInput").ap()
        return dram[name]

    xT_d = din("xT", [D, S])
    cT_d = din("cT", [128, 8])
    pp_d = din("pp", [128, DEPTH * NPP])
    bt_d = din("bt", [DEPTH, 128, 8 * 3 * 128])
    ident_d = din("ident", [128, 128])
    w_ada_d = din("w_ada", [DEPTH, D, 6 * D])
    w_in_d = din("w_in", [DEPTH, D, 2048])
    w_pool_d = din("w_pool", [DEPTH, 4, 128, 128])
    w_out_d = din("w_out", [DEPTH, D, D])
    w_ff1_d = din("w_ff1", [DEPTH, D, 4 * D])
    w_ff2_d = din("w_ff2", [DEPTH, 4 * D, D])
    oT_d = nc.dram_tensor("oT", [D, S], F32, kind="ExternalOutput").ap()

    sb = {}
    SB_SPECS = [
        ("X", 16384, F32), ("A", 16384, BF16), ("Y", 16384, BF16), ("B", 16384, BF16),
        ("E", 4096, BF16), ("BT", 3072, BF16), ("PP", DEPTH * NPP, F32), ("T", 4608, F32),
        ("MOD", 48, F32), ("COEF", 64, F32), ("CT", 8, F32), ("CACT", 8, BF16),
        ("ONES", 128, BF16), ("INVC", 16, F32), ("WP", 512, BF16),
        ("ZQ", 4096, BF16), ("EPS", 1, F32), ("ROW", 512, F32), ("ONE1", 1, F32), ("PF", 16, F32), ("IDENT", 128, BF16), ("ONESF", 128, F32),
    ]
    for name, n, dt in SB_SPECS:
        sb[name] = nc.alloc_sbuf_tensor("sb_" + name, [128, n], dt)
    ps = [nc.alloc_psum_tensor("ps%d" % i, [128, 512], F32) for i in range(8)]

    sc = Sched(nc)
    for name, n, dt in SB_SPECS:
        sc.region(name, n)
    for i in range(8):
        sc.region("PS%d" % i, 512)

    X, A, Y, B, E, BT, PP, T = (sb[k] for k in ("X", "A", "Y", "B", "E", "BT", "PP", "T"))
    MOD, COEF, CT, CACT, ONES, INVC, WP = (sb[k] for k in ("MOD", "COEF", "CT", "CACT", "ONES", "INVC", "WP"))
    ZQ, EPSC, ROW, ONE1, PF, IDENT = sb["ZQ"], sb["EPS"], sb["ROW"], sb["ONE1"], sb["PF"], sb["IDENT"]
    ONESF = sb["ONESF"]
    coef_eps = EPSC[:, 0:1]
    Tb = T[:, :].bitcast(BF16)

    def coef(i):
        return COEF[:, i * 8:(i + 1) * 8]

    def rC(i):
        return ("COEF", i * 8, i * 8 + 8)

    def ppv(l, off, n):
        return PP[:, l * NPP + off: l * NPP + off + n]

    rPP = ("PP", 0, DEPTH * NPP)

    for c in range(8):
        sc.dma("sp", "x%d" % c, X[:, c * S:(c + 1) * S], xT_d[c * 128:(c + 1) * 128, :],
               writes=[("X", c * S, (c + 1) * S)])
    sc.dma("sp", "consts", CT[:, :], cT_d, writes=[("CT", 0, 8)])
    sc.dma("sp", "consts2", PP[:, :], pp_d, writes=[rPP])
    sc.dma("pool", "ident", IDENT[:, :], ident_d, writes=[("IDENT", 0, 128)])
    sc.op("dve", _I("memset", ONES[:, :], 1.0 / 1024.0), writes=[("ONES", 0, 128)])
    sc.op("dve", _I("memset", EPSC[:, :], EPS_P), writes=[("EPS", 0, 1)])
    sc.op("dve", _I("memset", ONESF[:, :], 1.0 / 1024.0), writes=[("ONESF", 0, 128)])
    sc.op("dve", _I("memset", ONE1[:, :], 1.0), writes=[("ONE1", 0, 1)])
    for t in range(16):
        sc.op("dve", _I("memset", INVC[:, t:t + 1], 1.0 / (t + 1)), writes=[("INVC", t, t + 1)])
    sc.op("act", _I("activation", out=CACT[:, :], in_=CT[:, :], func=AF.Silu),
          reads=[("CT", 0, 8)], writes=[("CACT", 0, 8)])

    est = {"n": 0}

    egen = {}

    def e_load(src_ap):
        s = est["n"] % 4
        egen[s] = est["n"]
        est["n"] += 1
        dst = _ap3(E[:, s * 1024:(s + 1) * 1024], [[128, 8], [1, 128]])
        sc.dma("pool", "e%d" % s, dst, src_ap, writes=[("E", s * 1024, (s + 1) * 1024)])
        return s + 4 * egen[s]

    def e_slot(tok):
        s = tok % 4
        assert egen[s] == tok // 4, "E-ring slot %d reused before its consumers were emitted" % s
        return s

    def e_w(s, k):
        return E[:, s * 1024 + k * 128: s * 1024 + (k + 1) * 128]

    def rE(s):
        return ("E", s * 1024, (s + 1) * 1024)

    psrot = {"n": 0}

    def next_ps(lo=0, n=6):
        i = lo + psrot["n"] % n
        psrot["n"] += 1
        return i

    def ada_steps(l, big_ring=False):
        wv = w_ada_d[l].rearrange("(k p) e -> p k e", p=128)
        rowb, colb = 6, 7
        steps = []

        def mk(nb, kp):
            def f():
                if big_ring:
                    s = (nb * 4 + kp) % 16
                    RG, rgname, chan = B, "B", "a%d" % s
                else:
                    s = est["n"] % 4
                    egen[s] = est["n"]
                    est["n"] += 1
                    RG, rgname, chan = E, "E", "e%d" % s
                dst = _ap3(RG[:, s * 1024:(s + 1) * 1024], [[512, 2], [1, 512]])
                sc.dma("pool", chan, dst, wv[:, 2 * kp:2 * kp + 2, nb * 512:(nb + 1) * 512],
                       writes=[(rgname, s * 1024, (s + 1) * 1024)])
                for kk in range(2):
                    k = 2 * kp + kk
                    sc.op("pe", _I("matmul", ps[rowb][0:1, :], lhsT=CACT[:, k:k + 1],
                                   rhs=RG[:, s * 1024 + kk * 512: s * 1024 + (kk + 1) * 512], start=(k == 0), stop=(k == 7)),
                          reads=[(rgname, s * 1024, (s + 1) * 1024), ("CACT", 0, 8)], writes=[("PS%d" % rowb, 0, 512)],
                          signal=(k == 7))
                if kp == 3:
                    sc.op("act", _I("activation", out=ROW[0:1, :], in_=ps[rowb][0:1, :], func=AF.Copy),
                          reads=[("PS%d" % rowb, 0, 512)], writes=[("ROW", 0, 512)])
                    for q in range(4):
                        j = nb * 4 + q
                        sc.op("pe", _I("matmul", ps[colb][:, j:j + 1], lhsT=ROW[0:1, q * 128:(q + 1) * 128], rhs=ONE1[0:1, 0:1],
                                       start=True, stop=True),
                              reads=[("ROW", 0, 512), ("ONE1", 0, 1)], writes=[("PS%d" % colb, j, j + 1)], signal=(q == 3))
                if nb == 11 and kp == 3:
                    sc.op("dve", _I("tensor_tensor", out=MOD[:, :], in0=ps[colb][:, 0:48], in1=ppv(l, 0, 48), op=ALU.add),
                          reads=[("PS%d" % colb, 0, 48), rPP], writes=[("MOD", 0, 48)])
            return f

        for nb in range(12):
            for kp in range(4):
                steps.append(mk(nb, kp))
        return steps

    def emit_ada(l, bank):
        for f in ada_steps(l, big_ring=(l == 0)):
            f()

    def emit_coef_layer(l):
        rM = ("MOD", 0, 48)
        sc.op("dve", _I("tensor_scalar", out=coef(2), in0=MOD[:, 16:24], scalar1=1.0, scalar2=1.0 / ALPHA,
                                               op0=ALU.add, op1=ALU.mult), reads=[rM], writes=[rC(2)])
        sc.op("dve", _I("tensor_scalar", out=coef(5), in0=MOD[:, 40:48], scalar1=1.0, scalar2=1.0 / ALPHA,
                                               op0=ALU.add, op1=ALU.mult), reads=[rM], writes=[rC(5)])
        sc.op("dve", _I("scalar_tensor_tensor", out=coef(3), in0=MOD[:, 32:40], scalar=1.0, in1=ppv(l, 52, 8),
                                                      op0=ALU.add, op1=ALU.mult), reads=[rM, rPP], writes=[rC(3)])
        sc.op("dve", _I("scalar_tensor_tensor", out=coef(4), in0=MOD[:, 32:40], scalar=1.0, in1=ppv(l, 60, 8),
                                                      op0=ALU.add, op1=ALU.mult), reads=[rM, rPP], writes=[rC(4)])
        sc.op("dve", _I("tensor_tensor", out=coef(4), in0=coef(4), in1=MOD[:, 24:32], op=ALU.add),
              reads=[rM, rC(4)], writes=[rC(4)])

    def emit_coef_next(lprev):
        rM = ("MOD", 0, 48)
        sc.op("dve", _I("scalar_tensor_tensor", out=coef(6), in0=MOD[:, 8:16], scalar=1.0, in1=ppv(lprev, 68, 8),
                                                      op0=ALU.add, op1=ALU.mult), reads=[rM, rPP], writes=[rC(6)])
        sc.op("dve", _I("scalar_tensor_tensor", out=coef(7), in0=MOD[:, 8:16], scalar=1.0, in1=ppv(lprev, 76, 8),
                                                      op0=ALU.add, op1=ALU.mult), reads=[rM, rPP], writes=[rC(7)])
        sc.op("dve", _I("tensor_tensor", out=coef(7), in0=coef(7), in1=MOD[:, 0:8], op=ALU.add),
              reads=[rM, rC(7)], writes=[rC(7)])

    zb = Tb[:, 0:4096]

    def emit_ln_p1(b):
        c0 = b * 512
        zv = _ap3(X[:, c0:c0 + 512], [[S, 8], [1, 512]])
        rX = [("X", k * S + c0, k * S + c0 + 512) for k in range(8)]
        zb3 = _ap3(zb, [[512, 8], [1, 512]])
        bm, bq = (4, 5) if b % 2 == 0 else (6, 7)
        for k in range(8):
            sc.op("pe", _I("matmul", ps[bm][:, :], lhsT=ONESF[:, :], rhs=X[:, k * S + c0:k * S + c0 + 512],
                           start=(k == 0), stop=(k == 7)),
                  reads=[rX[k], ("ONESF", 0, 128)], writes=[("PS%d" % bm, 0, 512)], signal=(k == 7))
        for k in range(8):
            sc.op("act", _I("activation", out=ZQ[:, k * 512:(k + 1) * 512], in_=X[:, k * S + c0:k * S + c0 + 512], func=AF.Square),
                  reads=[rX[k]], writes=[("ZQ", k * 512, (k + 1) * 512)])
        for k in range(8):
            sc.op("pe", _I("matmul", ps[bq][:, :], lhsT=ONES[:, :], rhs=ZQ[:, k * 512:(k + 1) * 512],
                           start=(k == 0), stop=(k == 7)),
                  reads=[("ZQ", k * 512, (k + 1) * 512), ("ONES", 0, 128)], writes=[("PS%d" % bq, 0, 512)], signal=(k == 7))

    def emit_ln_p2(b, l, which, last):
        c0 = b * 512
        zv = _ap3(X[:, c0:c0 + 512], [[S, 8], [1, 512]])
        rX = [("X", k * S + c0, k * S + c0 + 512) for k in range(8)]
        bm, bq = (4, 5) if b % 2 == 0 else (6, 7)
        mean_sb, tmp, rstd = T[:, 2048:2560], T[:, 2560:3072], T[:, 3072:3584]
        rMean, rTmp, rRstd = ("T", 2048, 2560), ("T", 2560, 3072), ("T", 3072, 3584)
        sc.op("act", _I("activation", out=mean_sb, in_=ps[bm][:, :], func=AF.Copy), reads=[("PS%d" % bm, 0, 512)], writes=[rMean])
        sc.op("act", _I("activation", out=tmp, in_=ps[bm][:, :], func=AF.Square), reads=[("PS%d" % bm, 0, 512)], writes=[rTmp])
        sc.op("dve", _I("tensor_tensor", out=tmp, in0=ps[bq][:, :], in1=tmp, op=ALU.subtract),
              reads=[("PS%d" % bq, 0, 512), rTmp], writes=[rTmp])
        sc.op("act", _I("activation", out=tmp, in_=tmp, func=AF.Ln, bias=coef_eps), reads=[rTmp, ("EPS", 0, 1)], writes=[rTmp])
        sc.op("act", _I("activation", out=rstd, in_=tmp, func=AF.Exp, scale=-0.5), reads=[rTmp], writes=[rRstd])
        sc.op("dve", _I("scalar_tensor_tensor", out=mean_sb, in0=mean_sb, scalar=-1.0, in1=rstd, op0=ALU.mult, op1=ALU.mult),
              reads=[rMean, rRstd], writes=[rMean])
        nmr_b = _ap3(mean_sb, [[0, 8], [1, 512]])
        rstd_b = _ap3(rstd, [[0, 8], [1, 512]])
        sc.op("dve", _I("tensor_tensor", out=zv, in0=zv, in1=rstd_b, op=ALU.mult), reads=rX + [rRstd], writes=rX)
        sc.op("dve", _I("tensor_tensor", out=zv, in0=zv, in1=nmr_b, op=ALU.add), reads=rX + [rMean], writes=rX)
        if which == 1:
            ca, cb_, g_off, b_off = 3, 4, 52, 60
        else:
            ca, cb_, g_off, b_off = 6, 7, 68, 76
        for k in range(8):
            xk = X[:, k * S + c0: k * S + c0 + 512]
            rXk = ("X", k * S + c0, k * S + c0 + 512)
            if not (which == 2 and last):
                ak = A[:, k * S + c0: k * S + c0 + 512]
                sc.op("act", _I("activation", out=ak, in_=xk, func=AF.Identity,
                                scale=coef(ca)[:, k:k + 1], bias=coef(cb_)[:, k:k + 1]),
                      reads=[rXk, rC(ca), rC(cb_)], writes=[("A", k * S + c0, k * S + c0 + 512)])
            if k < 4:
                sc.op("dve", _I("tensor_scalar", out=xk, in0=xk, scalar1=ppv(l, g_off + k, 1),
                                scalar2=ppv(l, b_off + k, 1), op0=ALU.mult, op1=ALU.add),
                      reads=[rXk, rPP], writes=[rXk])
            else:
                sc.op("act", _I("activation", out=xk, in_=xk, func=AF.Identity,
                                scale=ppv(l, g_off + k, 1), bias=ppv(l, b_off + k, 1)),
                      reads=[rXk, rPP], writes=[rXk])

    def emit_ln(b, l, which, last):
        emit_ln_p1(b)
        emit_ln_p2(b, l, which, last)

    state = {}

    def out_block(b):
        for c in range(8):
            lo = c * S + b * 512
            sc.dma("sp", "out", oT_d[c * 128:(c + 1) * 128, b * 512:(b + 1) * 512], X[:, lo:lo + 512], reads=[("X", lo, lo + 512)])
        if b == 3:
            state["out_done"] = True

    def stop(tag):
        if DBG_STOP == tag:
            raise _Stop()

    def body():
        emit_ada(0, 6)
        rM = ("MOD", 0, 48)
        sc.op("dve", _I("tensor_scalar_add", out=coef(0), in0=MOD[:, 8:16], scalar1=1.0), reads=[rM], writes=[rC(0)])
        sc.op("dve", _I("tensor_copy", out=coef(1), in_=MOD[:, 0:8]), reads=[rM], writes=[rC(1)])
        for k in range(8):
            for b in range(4):
                lo = k * S + b * 512
                if (k + b) % 2 == 0:
                    sc.op("act", _I("activation", out=A[:, lo:lo + 512], in_=X[:, lo:lo + 512], func=AF.Identity,
                                    scale=coef(0)[:, k:k + 1], bias=coef(1)[:, k:k + 1]),
                          reads=[("X", lo, lo + 512), rC(0), rC(1)], writes=[("A", lo, lo + 512)])
                else:
                    sc.op("dve", _I("tensor_scalar", out=A[:, lo:lo + 512], in0=X[:, lo:lo + 512],
                                    scalar1=coef(0)[:, k:k + 1], scalar2=coef(1)[:, k:k + 1], op0=ALU.mult, op1=ALU.add),
                          reads=[("X", lo, lo + 512), rC(0), rC(1)], writes=[("A", lo, lo + 512)])

        stop('h0')
        pre0 = {}
        for l in range(L):
            last = (l == L - 1)
            emit_coef_layer(l)
            win = w_in_d[l].rearrange("(k p) e -> p k e", p=128)
            sc.dma("pool", "wp", _ap3(WP[:, 0:512], [[128, 4], [1, 128]]), w_pool_d[l].rearrange("g c d -> c g d"),
                   writes=[("WP", 0, 512)])
            sc.dma("pool", "bt", BT[:, :], bt_d[l], writes=[("BT", 0, 3072)])

            ZQf = ZQ[:, :].bitcast(F32)

            def pool_bufs(bb):
                if bb == 0:
                    return [(ZQf[:, o:o + 528], ("ZQ", 2 * o, 2 * (o + 528))) for o in (0, 528, 1056)]
                return [(T[:, o:o + 528], ("T", o, o + 528)) for o in (1664, 2192, 2720)]

            pstate = {}

            def pool_F1(g, b):
                def f():
                    if b == 0:
                        pstate["s"] = e_load(win[:, :, g * 128:(g + 1) * 128])
                    s = e_slot(pstate["s"])
                    (u, rU), _, _ = pool_bufs(b % 2)
                    bank = next_ps(6, 2)
                    for k in range(8):
                        sc.op("pe", _I("matmul", ps[bank][:, :], lhsT=e_w(s, k),
                                       rhs=A[:, k * S + b * 512: k * S + (b + 1) * 512], start=(k == 0), stop=(k == 7)),
                              reads=[rE(s), ("A", k * S + b * 512, k * S + (b + 1) * 512)],
                              writes=[("PS%d" % bank, 0, 512)], signal=(k == 7))
                    rg, lo_, hi_ = rU
                    sc_ = 2 if rg == "ZQ" else 1
                    if b == 0:
                        sc.op("dve", _I("memset", u[:, 0:16], 0.0), writes=[(rg, lo_, lo_ + 16 * sc_)])
                    else:
                        (pu, rPU), _, _ = pool_bufs((b - 1) % 2)
                        prg, plo, phi = rPU
                        sc.op("dve", _I("tensor_copy", out=u[:, 0:16], in_=pu[:, 512:528]),
                              reads=[(prg, phi - 16 * (2 if prg == "ZQ" else 1), phi)], writes=[(rg, lo_, lo_ + 16 * sc_)])
                    sc.op("act", _I("activation", out=u[:, 16:528], in_=ps[bank][:, :], func=AF.Copy),
                          reads=[("PS%d" % bank, 0, 512)], writes=[(rg, lo_ + 16 * sc_, hi_)])
                    w = POOL_W[g]
                    (u, rU), (sa, rSa), (sb_, rSb) = pool_bufs(b % 2)
                    src, rsrc = u, rU
                    bufs = [(sa, rSa), (sb_, rSb)]
                    step, lo2 = 1, 1
                    nst = {2: 1, 4: 2, 8: 3, 16: 4}[w]
                    for it in range(nst):
                        dst, rdst = bufs[it % 2]
                        sc.op("dve", _I("tensor_tensor", out=dst[:, lo2:528], in0=src[:, lo2:528],
                                         in1=src[:, lo2 - step:528 - step], op=ALU.add), reads=[rsrc], writes=[rdst])
                        src, rsrc = dst, rdst
                        step *= 2
                        lo2 = 2 * lo2 + 1
                    pdst, rpd = bufs[nst % 2]
                    if rpd[0] == "ZQ":
                        pbf = ZQ[:, rpd[1]:rpd[1] + 512]
                    else:
                        pbf = Tb[:, 2 * rpd[1]:2 * rpd[1] + 512]
                    if b == 0:
                        sc.op("dve", _I("tensor_tensor", out=PF[:, 0:w - 1], in0=src[:, 16:16 + w - 1], in1=INVC[:, 0:w - 1],
                                         op=ALU.mult), reads=[rsrc, ("INVC", 0, 16)], writes=[("PF", 0, 16)])
                    sc.op("dve", _I("tensor_scalar_mul", out=src[:, 16:528], in0=src[:, 16:528], scalar1=1.0 / w),
                          reads=[rsrc], writes=[rsrc])
                    sc.op("dve", _I("tensor_tensor", out=pbf, in0=src[:, 16:528], in1=u[:, 16:528], op=ALU.subtract),
                          reads=[rsrc, rU], writes=[rpd])
                    if b == 0:
                        sc.op("dve", _I("tensor_tensor", out=pbf[:, 0:w - 1], in0=PF[:, 0:w - 1], in1=u[:, 16:16 + w - 1],
                                         op=ALU.subtract), reads=[("PF", 0, 16), rU], writes=[rpd])
                    pstate[(g, b)] = (pbf, rpd)
                return f

            def pool_F3(g, b):
                def f():
                    pbf, rpd = pstate[(g, b)]
                    bank2 = next_ps(6, 2)
                    sc.op("pe", _I("matmul", ps[bank2][:, :], lhsT=WP[:, g * 128:(g + 1) * 128], rhs=pbf, start=True, stop=True),
                          reads=[rpd, ("WP", 0, 512)], writes=[("PS%d" % bank2, 0, 512)])
                    yl = g * S + b * 512
                    sc.op("act", _I("activation", out=Y[:, yl:yl + 512], in_=ps[bank2][:, :], func=AF.Copy, scale=ppv(l, 48 + g, 1)),
                          reads=[("PS%d" % bank2, 0, 512), rPP], writes=[("Y", yl, yl + 512)])
                return f

            gb = [(g, b) for g in range(4) for b in range(4)]
            pool_pieces = []
            for idx in range(18):
                fs = []
                if idx >= 2:
                    fs.append(pool_F3(*gb[idx - 2]))
                if idx < 16:
                    fs.append(pool_F1(*gb[idx]))
                pool_pieces.append(lambda fs=fs: [f() for f in fs])

            stop('pool')

            def qkv_pieces(c, wsrc=None, pslo=6, psn=2):
                wsrc = win if wsrc is None else wsrc
                j = c % 2
                bo = j * 8192
                qo, ko, vo = bo, bo + 2048, bo + 4096
                pieces = []
                st = {}

                def mk_load(key, col0):
                    def f():
                        st[key] = e_load(wsrc[:, :, col0 + c * 128: col0 + (c + 1) * 128])
                    return f

                def mk_qk(key, dsto, scl, b):
                    def f():
                        s = e_slot(st[key])
                        bank = next_ps(pslo, psn)
                        for k in range(8):
                            sc.op("pe", _I("matmul", ps[bank][:, :], lhsT=e_w(s, k),
                                           rhs=A[:, k * S + b * 512: k * S + (b + 1) * 512], start=(k == 0), stop=(k == 7)),
                                  reads=[rE(s), ("A", k * S + b * 512, k * S + (b + 1) * 512)],
                                  writes=[("PS%d" % bank, 0, 512)], signal=(k == 7))
                        dl = dsto + b * 512
                        sc.op("act", _I("activation", out=B[:, dl:dl + 512], in_=ps[bank][:, :], func=AF.Copy, scale=scl),
                              reads=[("PS%d" % bank, 0, 512)], writes=[("B", dl, dl + 512)])
                    return f

                def mk_v(t):
                    def f():
                        s = e_slot(st["v"])
                        tt = t % 4
                        if tt == 0:
                            st["vbank"] = next_ps(pslo, psn)
                        bank = st["vbank"]
                        for k in range(8):
                            sc.op("pe", _I("matmul", ps[bank][:, tt * 128:(tt + 1) * 128],
                                           lhsT=A[:, k * S + t * 128: k * S + (t + 1) * 128], rhs=e_w(s, k),
                                           start=(k == 0), stop=(k == 7)),
                                  reads=[rE(s), ("A", k * S + t * 128, k * S + (t + 1) * 128)],
                                  writes=[("PS%d" % bank, tt * 128, (tt + 1) * 128)], signal=(k == 7))
                        if tt == 3:
                            vt = vo + (t - 3) * 192
                            psv = ps[bank]
                            srcA = _ap3(psv[:, 0:64], [[128, 4], [1, 64]])
                            srcB = _ap3(psv[:, 64:128], [[128, 4], [1, 64]])
                            dstA = _ap3(B[:, vt:vt + 64], [[192, 4], [1, 64]])
                            dstB = _ap3(B[:, vt + 128:vt + 192], [[192, 4], [1, 64]])
                            rV = ("B", vt, vt + 4 * 192)
                            sc.op("dve", _I("tensor_copy", out=dstA, in_=srcA), reads=[("PS%d" % bank, 0, 512)], writes=[rV])
                            sc.op("dve", _I("tensor_copy", out=dstB, in_=srcB), reads=[("PS%d" % bank, 0, 512)], writes=[rV])
                    return f

                def mk_ones():
                    def f():
                        onesv = _ap3(B[:, vo + 64: vo + 128], [[192, 16], [1, 64]])
                        sc.op("dve", _I("memset", onesv, 1.0), writes=[("B", vo, vo + 3072)])
                        st["v"] = e_load(wsrc[:, :, 1536 + c * 128: 1536 + (c + 1) * 128])
                    return f

                pieces.append(mk_load("q", 512))
                for b in range(4):
                    pieces.append(mk_qk("q", qo, 0.125, b))
                pieces.append(mk_load("k", 1024))
                for b in range(4):
                    pieces.append(mk_qk("k", ko, 1.0, b))
                pieces.append(mk_ones())
                for t in range(16):
                    pieces.append(mk_v(t))
                return pieces

            def att_S(u, c, m, X_):
                j = c % 2
                bo = j * 8192
                qo, ko = bo, bo + 2048
                sset = u % 2
                U0, U1 = 2 * sset, 2 * sset + 1
                r0 = 64 * X_
                valid = list(range(max(0, 4 - m), 5))
                h = 2 * c + X_
                v034 = [i for i in valid if i in (0, 3, 4)]
                s_lo = 3 - len(v034)
                nn = len(v034) * 128
                bto = h * 384 + s_lo * 128
                sc.op("pe", _I("matmul", ps[U1][:, s_lo * 128: s_lo * 128 + nn], lhsT=IDENT[:, :], rhs=BT[:, bto:bto + nn],
                               start=True, stop=False),
                      reads=[("IDENT", 0, 128), ("BT", bto, bto + nn)], writes=[("PS%d" % U1, 0, 512)], signal=False)
                order = [i for i in valid if i in (0, 3, 4)] + [i for i in valid if i in (1, 2)]
                for i in order:
                    tk = m - 4 + i
                    if i in (1, 2):
                        bank_, col, first = U0, (i - 1) * 128, True
                    else:
                        bank_, col, first = U1, {0: 0, 3: 1, 4: 2}[i] * 128, False
                    sc.op("pe", _I("matmul", ps[bank_][:, col:col + 128],
                                   lhsT=B[r0:r0 + 64, ko + tk * 128: ko + (tk + 1) * 128],
                                   rhs=B[r0:r0 + 64, qo + m * 128: qo + (m + 1) * 128], start=first,
                                   stop=(first or i == 4)),
                          reads=[("B", ko + tk * 128, ko + (tk + 1) * 128), ("B", qo + m * 128, qo + (m + 1) * 128)],
                          writes=[("PS%d" % bank_, col, col + 128)], signal=(i == order[-1] or i == 4))

            def att_SM(u, c, m, X_):
                sset = u % 2
                U0, U1 = 2 * sset, 2 * sset + 1
                h = 2 * c + X_
                valid = list(range(max(0, 4 - m), 5))
                sbo = sset * 384
                pto = (768 + sset * 320) * 2
                v12 = [i for i in valid if i in (1, 2)]
                if v12:
                    c_lo = (v12[0] - 1) * 128
                    n12 = len(v12) * 128
                    dlo = pto + (v12[0] - 1) * 128
                    sc.op("act", _I("activation", out=Tb[:, dlo:dlo + n12], in_=ps[U0][:, c_lo:c_lo + n12],
                                    func=AF.Exp, bias=ppv(l, 84 + h, 1)),
                          reads=[("PS%d" % U0, c_lo, c_lo + n12), rPP], writes=[("T", dlo // 2, (dlo + n12) // 2)])
                v034 = [i for i in valid if i in (0, 3, 4)]
                s_lo = {3: 0, 2: 1, 1: 2}[len(v034)]
                nn = len(v034) * 128
                d0 = pto + (2 + s_lo) * 128
                sc.op("act", _I("activation", out=Tb[:, d0:d0 + nn], in_=ps[U1][:, s_lo * 128: s_lo * 128 + nn], func=AF.Exp),
                      reads=[("PS%d" % U1, 0, 512)], writes=[("T", d0 // 2, (d0 + nn) // 2)])

            def att_PV(u, c, m, X_):
                j = c % 2
                vo = j * 8192 + 4096
                sset = u % 2
                PO = 4 + sset
                pto = (768 + sset * 320) * 2
                valid = list(range(max(0, 4 - m), 5))
                vc = 0 if X_ == 0 else 64
                for n_, i in enumerate(valid):
                    tk = m - 4 + i
                    sl = {1: 0, 2: 1, 0: 2, 3: 3, 4: 4}[i]
                    pc = pto + sl * 128
                    vcol = vo + tk * 192 + vc
                    sc.op("pe", _I("matmul", ps[PO][:, 0:128], lhsT=B[:, vcol:vcol + 128], rhs=Tb[:, pc:pc + 128],
                                   start=(n_ == 0), stop=(n_ == len(valid) - 1)),
                          reads=[("B", vcol, vcol + 128), ("T", pc // 2, (pc + 128) // 2)],
                          writes=[("PS%d" % PO, 0, 128)], signal=(n_ == len(valid) - 1))

            def att_NORM(u, c, m, X_):
                sset = u % 2
                PO = 4 + sset
                dn = 1408 + sset * 128
                yc = (4 + c) * S + m * 128
                if X_ == 0:
                    nlo, dlo_ = 0, 64
                else:
                    nlo, dlo_ = 64, 0
                if X_ == 0:
                    sc.op("dve", _I("reciprocal", out=T[nlo:nlo + 64, dn:dn + 128], in_=ps[PO][dlo_:dlo_ + 64, 0:128]),
                          reads=[("PS%d" % PO, 0, 128)], writes=[("T", dn, dn + 128)])
                else:
                    sc.op("act", _I("activation", out=T[nlo:nlo + 64, dn:dn + 128], in_=ps[PO][dlo_:dlo_ + 64, 0:128], func=AF.Ln),
                          reads=[("PS%d" % PO, 0, 128)], writes=[("T", dn, dn + 128)])
                    sc.op("act", _I("activation", out=T[nlo:nlo + 64, dn:dn + 128], in_=T[nlo:nlo + 64, dn:dn + 128],
                                    func=AF.Exp, scale=-1.0),
                          reads=[("T", dn, dn + 128)], writes=[("T", dn, dn + 128)])
                sc.op("dve", _I("tensor_tensor", out=Y[nlo:nlo + 64, yc:yc + 128], in0=ps[PO][nlo:nlo + 64, 0:128],
                                in1=T[nlo:nlo + 64, dn:dn + 128], op=ALU.mult),
                      reads=[("PS%d" % PO, 0, 128), ("T", dn, dn + 128)], writes=[("Y", yc, yc + 128)])

            units = [(c, m, X_) for c in range(4) for m in range(16) for X_ in range(2)]
            if not pre0.get(l):
                for f in qkv_pieces(0):
                    f()
            stop('att_a')
            fill = []
            for u, (c, m, X_) in enumerate(units):
                if m == 0 and X_ == 0:
                    for f in fill:
                        f()
                    fill = (qkv_pieces(c + 1) if c + 1 < 4 else []) + pool_pieces[6 * c:6 * c + 6]
                    if c == 3:
                        wo = w_out_d[l].rearrange("(k p) e -> p k e", p=128)
                        for hh in range(2):
                            sc.dma("pool", "wo%d" % hh, _ap3(B[:, hh * 4096:(hh + 1) * 4096], [[1024, 4], [1, 1024]]),
                                   wo[:, hh * 4:(hh + 1) * 4, :], writes=[("B", hh * 4096, (hh + 1) * 4096)])
                if u == 0:
                    att_S(u, c, m, X_)
                    att_SM(u, c, m, X_)
                if u + 1 < len(units):
                    att_S(u + 1, *units[u + 1])
                    att_SM(u + 1, *units[u + 1])
                att_PV(u, c, m, X_)
                if fill:
                    fill.pop(0)()
                att_NORM(u, c, m, X_)

            stop('attn')
            w1v = w_ff1_d[l].rearrange("(k p) f -> p k f", p=128)
            w2v = w_ff2_d[l].rearrange("(fc p) d -> p fc d", p=128)

            def load_chunk(jp, ci):
                so_ = 8192 if jp % 2 == 0 else 0
                ch = "f%d_%d" % (ci, jp % 2)
                if ci < 4:
                    lo = so_ + ci * 1024
                    sc.dma("pool", ch, _ap3(B[:, lo:lo + 1024], [[512, 2], [1, 512]]),
                           w1v[:, 2 * ci:2 * ci + 2, jp * 512:(jp + 1) * 512], writes=[("B", lo, lo + 1024)])
                else:
                    fc = ci - 4
                    lo = so_ + 4096 + fc * 1024
                    sc.dma("pool", ch, B[:, lo:lo + 1024], w2v[:, jp * 4 + fc, :], writes=[("B", lo, lo + 1024)])

            def load_piece(jp):
                for ci in range(8):
                    load_chunk(jp, ci)

            load_piece(0)

            def s1d_mm(b):
                for dc in range(8):
                    bank = next_ps(0, 4)
                    for k in range(8):
                        sc.op("pe", _I("matmul",
                            ps[bank][:, :], lhsT=B[:, k * 1024 + dc * 128: k * 1024 + (dc + 1) * 128],
                            rhs=Y[:, k * S + b * 512: k * S + (b + 1) * 512], start=(k == 0), stop=(k == 7)),
                            reads=[("B", k * 1024 + dc * 128, k * 1024 + (dc + 1) * 128), ("Y", k * S + b * 512, k * S + (b + 1) * 512)],
                            writes=[("PS%d" % bank, 0, 512)], signal=(k == 7))
                    xl = dc * S + b * 512
                    sc.op("dve", _I("scalar_tensor_tensor",
                        out=X[:, xl:xl + 512], in0=ps[bank][:, :], scalar=coef(2)[:, dc:dc + 1], in1=X[:, xl:xl + 512],
                        op0=ALU.mult, op1=ALU.add),
                        reads=[("PS%d" % bank, 0, 512), rC(2), ("X", xl, xl + 512)], writes=[("X", xl, xl + 512)])

            asteps = ada_steps(l + 1) if not last else []

            def ffn_ff1(jp, b, narrow):
                so_ = 8192 if jp % 2 == 0 else 0
                if 1 <= jp <= 6:
                    for _ in range(2):
                        if asteps:
                            asteps.pop(0)()
                    if not asteps and not last and jp == 6 and b == 3:
                        emit_coef_next(l)
                fb = (jp * 4 + b) % 3
                fo = fb * S
                for fc in range(4):
                    bank = next_ps(0, 2) if narrow else next_ps(0, 3)
                    for k in range(8):
                        sc.op("pe", _I("matmul",
                            ps[bank][:, :], lhsT=B[:, so_ + k * 512 + fc * 128: so_ + k * 512 + (fc + 1) * 128],
                            rhs=A[:, k * S + b * 512: k * S + (b + 1) * 512], start=(k == 0), stop=(k == 7)),
                            reads=[("B", so_ + k * 512 + fc * 128, so_ + k * 512 + (fc + 1) * 128),
                                   ("A", k * S + b * 512, k * S + (b + 1) * 512)],
                            writes=[("PS%d" % bank, 0, 512)], signal=(k == 7))
                    rt = 3584 + (fc % 2) * 512
                    sc.op("act", _I("activation", out=T[:, rt:rt + 512], in_=ps[bank][:, :], func=AF.Relu),
                          reads=[("PS%d" % bank, 0, 512)], writes=[("T", rt, rt + 512)])
                    fl = fo + fc * 512
                    sc.op("act", _I("activation", out=Y[:, fl:fl + 512], in_=T[:, rt:rt + 512], func=AF.Square),
                          reads=[("T", rt, rt + 512)], writes=[("Y", fl, fl + 512)])

            def ffn_ff2(jp, b, narrow):
                so_ = 8192 if jp % 2 == 0 else 0
                if jp + 1 < 8:
                    load_chunk(jp + 1, 2 * b)
                    load_chunk(jp + 1, 2 * b + 1)
                fb = (jp * 4 + b) % 3
                fo = fb * S
                for dc in range(8):
                    bank = next_ps(2, 2) if narrow else next_ps(3, 3)
                    for fc in range(4):
                        sc.op("pe", _I("matmul",
                            ps[bank][:, :], lhsT=B[:, so_ + 4096 + fc * 1024 + dc * 128: so_ + 4096 + fc * 1024 + (dc + 1) * 128],
                            rhs=Y[:, fo + fc * 512: fo + (fc + 1) * 512], start=(fc == 0), stop=(fc == 3)),
                            reads=[("B", so_ + 4096 + fc * 1024 + dc * 128, so_ + 4096 + fc * 1024 + (dc + 1) * 128),
                                   ("Y", fo + fc * 512, fo + (fc + 1) * 512)],
                            writes=[("PS%d" % bank, 0, 512)], signal=(fc == 3))
                    xl = dc * S + b * 512
                    sc.op("dve", _I("scalar_tensor_tensor",
                        out=X[:, xl:xl + 512], in0=ps[bank][:, :], scalar=coef(5)[:, dc:dc + 1], in1=X[:, xl:xl + 512],
                        op0=ALU.mult, op1=ALU.add),
                        reads=[("PS%d" % bank, 0, 512), rC(5), ("X", xl, xl + 512)], writes=[("X", xl, xl + 512)])

            def ffn_slot(jp, b, narrow):
                ffn_ff1(jp, b, narrow)
                ffn_ff2(jp, b, narrow)

            R = sc.record

            def mmp1(b):
                return R(lambda: (s1d_mm(b), emit_ln_p1(b)))

            def p2(b, which):
                return R(lambda: emit_ln_p2(b, l, which, last))

            def zip_p2(sa, b, which):
                sb_ = p2(b, which)
                for st_ in sb_[:3]:
                    sc.replay(st_)
                sc.zip_emit(sa, sb_[3:])

            s1d_mm(0); emit_ln_p1(0)
            s1d_mm(1); emit_ln_p1(1)
            zip_p2(mmp1(2), 0, 1)
            zip_p2(mmp1(3), 1, 1)
            zip_p2(R(lambda: ffn_slot(0, 0, True)), 2, 1)
            zip_p2(R(lambda: ffn_slot(0, 1, True)), 3, 1)
            ffn_slot(0, 2, True)
            ffn_slot(0, 3, True)
            stop('ln1')
            slots = [(jp, b) for jp in range(1, 7) for b in range(4)]
            ffn_ff1(slots[0][0], slots[0][1], False)
            for i, (jp, b) in enumerate(slots):
                if i + 1 < len(slots):
                    ffn_ff1(slots[i + 1][0], slots[i + 1][1], False)
                ffn_ff2(jp, b, False)
            ffn_slot(7, 0, True); emit_ln_p1(0)
            ffn_slot(7, 1, True); emit_ln_p1(1)
            zip_p2(R(lambda: (ffn_slot(7, 2, True), emit_ln_p1(2))), 0, 2)
            if last:
                out_block(0)
            zip_p2(R(lambda: (ffn_slot(7, 3, True), emit_ln_p1(3))), 1, 2)
            if last:
                out_block(1)
            stop('ffn')
            if not last:
                wnext = w_in_d[l + 1].rearrange("(k p) e -> p k e", p=128)
                pcs = qkv_pieces(0, wsrc=wnext, pslo=0, psn=4)
                for i_ in (0, 5, 10):
                    pcs[i_]()

                def grp(b):
                    return [pcs[1 + b], pcs[6 + b]] + [pcs[11 + 4 * b + t] for t in range(4)]
                zip_p2(R(lambda: [f() for f in grp(0) + grp(1)]), 2, 2)
                zip_p2(R(lambda: [f() for f in grp(2)]), 3, 2)
                for f in grp(3):
                    f()
                pre0[l + 1] = True
            else:
                emit_ln_p2(2, l, 2, last)
                out_block(2)
                emit_ln_p2(3, l, 2, last)
                out_block(3)

    try:
        body()
    except _Stop:
        pass

    if DBG_STOP is not None or not state.get("out_done"):
        for c in range(8):
            sc.dma("sp", "out", oT_d[c * 128:(c + 1) * 128, :], X[:, c * S:(c + 1) * S], reads=[("X", c * S, (c + 1) * S)])
    sc.final_wait("sp", ["out"])

    with nc.Block() as block:
        sc.emit(block)
    return nc


def _bias_tables(rel_bias):
    p = np.arange(128)[:, None]
    q = np.arange(128)[None, :]
    out = np.empty((DEPTH, 128, 8, 3, 128), np.float32)
    for bi, i in enumerate((0, 3, 4)):
        diff = 128 * (4 - i) + q - p
        idx = np.clip(diff, -63, 128) + 63
        tab = rel_bias[:, :, idx]
        if i == 0:
            mask = (p < 64) & (q >= 64)
        elif i == 4:
            mask = (p >= 64) & (q < 64)
        else:
            mask = np.zeros((128, 128), bool)
        tab = np.where(mask[None, None], np.float32(MASKV), tab)
        out[:, :, :, bi, :] = np.transpose(tab, (0, 2, 1, 3))
    return out.reshape(DEPTH, 128, 8 * 3 * 128)


def _col(v):
    return np.ascontiguousarray(v.reshape(-1, 128).T)


_NC_CACHE = {}


def kernel(x, c, w_ada, b_ada, w_in, w_pool, pool_scale, rel_bias, w_out,
           ln1_g, ln1_b, w_ff1, w_ff2, ln2_g, ln2_b):
    f = lambda a: np.ascontiguousarray(np.asarray(a, dtype=np.float32))
    x, c, w_ada, b_ada, w_in, w_pool, pool_scale, rel_bias, w_out = map(
        f, (x, c, w_ada, b_ada, w_in, w_pool, pool_scale, rel_bias, w_out))
    ln1_g, ln1_b, w_ff1, w_ff2, ln2_g, ln2_b = map(f, (ln1_g, ln1_b, w_ff1, w_ff2, ln2_g, ln2_b))
    pp = np.zeros((128, DEPTH * NPP), np.float32)
    for l in range(DEPTH):
        o = l * NPP
        pp[:, o:o + 48] = _col(b_ada[l])
        pp[:, o + 48:o + 52] = _col(pool_scale[l])
        pp[:, o + 52:o + 60] = _col(ln1_g[l])
        pp[:, o + 60:o + 68] = _col(ln1_b[l])
        pp[:, o + 68:o + 76] = _col(ln2_g[l])
        pp[:, o + 76:o + 84] = _col(ln2_b[l])
        pp[:, o + 84:o + 92] = np.broadcast_to(rel_bias[l, :, 191][None, :], (128, 8))
    bt = _bias_tables(rel_bias)
    L = _NC_CACHE.get("L", DEPTH)
    if ("nc", L) not in _NC_CACHE:
        _NC_CACHE[("nc", L)] = build_nc(L)
    nc = _NC_CACHE[("nc", L)]
    in_maps = []
    for b in range(NCORES):
        in_maps.append({
            "xT": np.ascontiguousarray(x[b].T), "cT": _col(c[b]), "pp": pp, "bt": bt, "ident": np.eye(128, dtype=np.float32),
            "w_ada": w_ada, "w_in": w_in, "w_pool": w_pool, "w_out": w_out, "w_ff1": w_ff1, "w_ff2": w_ff2,
        })
    res = run_bass_kernel_spmd(nc, in_maps, core_ids=list(range(NCORES)))
    out = np.empty((NCORES, S, D), np.float32)
    for b in range(NCORES):
        out[b] = np.asarray(res.results[b]["oT"]).T
    return out
```

```python
import numpy as np
import concourse.bass as bass
import concourse.mybir as mybir
from concourse.bass_utils import run_bass_kernel_spmd

F32 = mybir.dt.float32
BF16 = mybir.dt.bfloat16
ALU = mybir.AluOpType
AF = mybir.ActivationFunctionType

D = 1024
S = 2048
DEPTH = 4
NCORES = 8
ALPHA = (2.0 * DEPTH) ** 0.25
LN_EPS = 1e-5
EPS_P = LN_EPS / (ALPHA * ALPHA)
MASKV = -100.0
NPP = 92
POOL_W = (2, 4, 8, 16)


class _Seg:
    __slots__ = ("w", "r")

    def __init__(self, w=None, r=None):
        self.w = w
        self.r = dict(r) if r else {}


class _Region:
    def __init__(self, size):
        self.b = [0, size]
        self.s = [_Seg()]

    def _split(self, x):
        import bisect
        i = bisect.bisect_left(self.b, x)
        if self.b[i] == x:
            return
        old = self.s[i - 1]
        self.b.insert(i, x)
        self.s.insert(i, _Seg(old.w, old.r))

    def segs(self, lo, hi):
        import bisect
        assert 0 <= lo < hi <= self.b[-1], (lo, hi, self.b[-1])
        self._split(lo)
        self._split(hi)
        i0 = bisect.bisect_left(self.b, lo)
        i1 = bisect.bisect_left(self.b, hi)
        return self.s[i0:i1]


class Sched:
    ENG = ("pe", "act", "dve", "pool", "sp")

    def __init__(self, nc):
        self.nc = nc
        self.sems = {}
        self.count = {}
        self.prog = {e: [] for e in self.ENG}
        self.seen = {e: {} for e in self.ENG}
        self.regions = {}
        for e in self.ENG:
            self._sem(e)

    def _sem(self, key):
        if key not in self.sems:
            self.sems[key] = self.nc.alloc_semaphore("s_" + str(key))
            self.count[key] = 0
        return self.sems[key]

    def region(self, name, size):
        self.regions[name] = _Region(size)

    def _deps(self, eng, reads, writes):
        deps = {}

        def add(d):
            if d is None:
                return
            k, v = d
            if deps.get(k, 0) < v:
                deps[k] = v

        for (rg, lo, hi) in reads:
            for sg in self.regions[rg].segs(lo, hi):
                add(sg.w)
        for (rg, lo, hi) in writes:
            for sg in self.regions[rg].segs(lo, hi):
                add(sg.w)
                for k, v in sg.r.items():
                    add((k, v))
        return deps

    def _emit_waits(self, eng, deps):
        seen = self.seen[eng]
        for k, v in deps.items():
            if k == "pe" and eng == "pe":
                continue
            if seen.get(k, 0) >= v:
                continue
            seen[k] = v
            self.prog[eng].append(("wait", k, v))

    def _mark(self, reads, writes, key, val):
        for (rg, lo, hi) in reads:
            for sg in self.regions[rg].segs(lo, hi):
                if sg.r.get(key, 0) < val:
                    sg.r[key] = val
        for (rg, lo, hi) in writes:
            for sg in self.regions[rg].segs(lo, hi):
                sg.w = (key, val)
                sg.r = {}

    @staticmethod
    def _bank(rs):
        return [((rg, 0, 512) if rg.startswith("PS") else (rg, lo, hi)) for (rg, lo, hi) in rs]

    def record(self, fn):
        self._rec = []
        try:
            fn()
        finally:
            rec, self._rec = self._rec, None
        steps, cur = [], []
        for call in rec:
            cur.append(call)
            if not (call[0] == "op" and call[1][0] == "pe"):
                steps.append(cur)
                cur = []
        if cur:
            steps.append(cur)
        return steps

    def replay(self, step):
        for kind, a, k in step:
            (self.op if kind == "op" else self.dma)(*a, **k)

    def zip_emit(self, sa, sb):
        na, nb = len(sa), len(sb)
        ia = ib = 0
        while ia < na or ib < nb:
            if ib >= nb or (ia < na and ia * nb <= ib * na):
                self.replay(sa[ia]); ia += 1
            else:
                self.replay(sb[ib]); ib += 1

    def op(self, eng, fn, reads=(), writes=(), signal=True, drain=False):
        if getattr(self, "_rec", None) is not None:
            self._rec.append(("op", (eng, fn), dict(reads=reads, writes=writes, signal=signal, drain=drain)))
            return
        reads, writes = self._bank(reads), self._bank(writes)
        deps = self._deps(eng, reads, writes)
        self._emit_waits(eng, deps)
        if drain and self.count[eng] > 0:
            self.prog[eng].append(("wait", eng, self.count[eng]))
        if signal:
            self.count[eng] += 1
            val = self.count[eng]
            self.prog[eng].append(("op", fn, eng, 1))
        else:
            val = self.count[eng] + 1
            self.prog[eng].append(("op", fn, None, 0))
        self._mark(reads, writes, eng, val)

    def dma(self, eng, chan, out, in_, reads=(), writes=()):
        if getattr(self, "_rec", None) is not None:
            self._rec.append(("dma", (eng, chan, out, in_), dict(reads=reads, writes=writes)))
            return
        self._sem(chan)
        deps = self._deps(eng, reads, writes)
        self._emit_waits(eng, deps)
        self.count[chan] += 16
        val = self.count[chan]
        self.prog[eng].append(("op", _I("dma_start", out=out, in_=in_), chan, 16))
        self._mark(reads, writes, chan, val)

    def final_wait(self, eng, chans):
        for c in chans:
            self.prog[eng].append(("wait", c, self.count[c]))

    def emit(self, block):
        nc = self.nc
        sems = self.sems

        def run(engobj, name):
            for it in self.prog[name]:
                if it[0] == "wait":
                    engobj.wait_ge(sems[it[1]], it[2])
                else:
                    ins = it[1](engobj)
                    if it[2] is not None:
                        ins.then_inc(sems[it[2]], it[3])

        @block.tensor
        def _(e):
            run(e, "pe")

        @block.scalar
        def _(e):
            run(e, "act")

        @block.vector
        def _(e):
            run(e, "dve")

        @block.gpsimd
        def _(e):
            run(e, "pool")

        @block.sync
        def _(e):
            run(e, "sp")


def _I(name, *a, **k):
    return lambda e: getattr(e, name)(*a, **k)


def _ap3(base2d, dims):
    return bass.AP(base2d.tensor, base2d.offset, [list(base2d.ap[0])] + [list(d) for d in dims])


class _Stop(Exception):
    pass


DBG_STOP = None


def build_nc(L=DEPTH):
    nc = bass.Bass("TRN2", target_bir_lowering=False)
    dram = {}

    def din(name, shape):
        dram[name] = nc.dram_tensor(name, list(shape), F32, kind="ExternalInput").ap()
        return dram[name]

    xT_d = din("xT", [D, S])
    cT_d = din("cT", [128, 8])
    pp_d = din("pp", [128, DEPTH * NPP])
    bt_d = din("bt", [DEPTH, 128, 8 * 3 * 128])
    ident_d = din("ident", [128, 128])
    w_ada_d = din("w_ada", [DEPTH, D, 6 * D])
    w_in_d = din("w_in", [DEPTH, D, 2048])
    w_pool_d = din("w_pool", [DEPTH, 4, 128, 128])
    w_out_d = din("w_out", [DEPTH, D, D])
    w_ff1_d = din("w_ff1", [DEPTH, D, 4 * D])
    w_ff2_d = din("w_ff2", [DEPTH, 4 * D, D])
    oT_d = nc.dram_tensor("oT", [D, S], F32, kind="ExternalOutput").ap()

    sb = {}
    SB_SPECS = [
        ("X", 16384, F32), ("A", 16384, BF16), ("Y", 16384, BF16), ("B", 16384, BF16),
        ("E", 4096, BF16), ("BT", 3072, BF16), ("PP", DEPTH * NPP, F32), ("T", 4608, F32),
        ("MOD", 48, F32), ("COEF", 64, F32), ("CT", 8, F32), ("CACT", 8, BF16),
        ("ONES", 128, BF16), ("INVC", 16, F32), ("WP", 512, BF16),
        ("ZQ", 4096, BF16), ("EPS", 1, F32), ("ROW", 512, F32), ("ONE1", 1, F32), ("PF", 16, F32), ("IDENT", 128, BF16),
    ]
    for name, n, dt in SB_SPECS:
        sb[name] = nc.alloc_sbuf_tensor("sb_" + name, [128, n], dt)
    ps = [nc.alloc_psum_tensor("ps%d" % i, [128, 512], F32) for i in range(8)]

    sc = Sched(nc)
    for name, n, dt in SB_SPECS:
        sc.region(name, n)
    for i in range(8):
        sc.region("PS%d" % i, 512)

    X, A, Y, B, E, BT, PP, T = (sb[k] for k in ("X", "A", "Y", "B", "E", "BT", "PP", "T"))
    MOD, COEF, CT, CACT, ONES, INVC, WP = (sb[k] for k in ("MOD", "COEF", "CT", "CACT", "ONES", "INVC", "WP"))
    ZQ, EPSC, ROW, ONE1, PF, IDENT = sb["ZQ"], sb["EPS"], sb["ROW"], sb["ONE1"], sb["PF"], sb["IDENT"]
    coef_eps = EPSC[:, 0:1]
    Tb = T[:, :].bitcast(BF16)

    def coef(i):
        return COEF[:, i * 8:(i + 1) * 8]

    def rC(i):
        return ("COEF", i * 8, i * 8 + 8)

    def ppv(l, off, n):
        return PP[:, l * NPP + off: l * NPP + off + n]

    rPP = ("PP", 0, DEPTH * NPP)

    for c in range(8):
        sc.dma("sp", "x%d" % c, X[:, c * S:(c + 1) * S], xT_d[c * 128:(c + 1) * 128, :],
               writes=[("X", c * S, (c + 1) * S)])
    sc.dma("sp", "consts", CT[:, :], cT_d, writes=[("CT", 0, 8)])
    sc.dma("sp", "consts2", PP[:, :], pp_d, writes=[rPP])
    sc.dma("pool", "ident", IDENT[:, :], ident_d, writes=[("IDENT", 0, 128)])
    sc.op("dve", _I("memset", ONES[:, :], 1.0 / 1024.0), writes=[("ONES", 0, 128)])
    sc.op("dve", _I("memset", EPSC[:, :], EPS_P), writes=[("EPS", 0, 1)])
    sc.op("dve", _I("memset", ONE1[:, :], 1.0), writes=[("ONE1", 0, 1)])
    for t in range(16):
        sc.op("dve", _I("memset", INVC[:, t:t + 1], 1.0 / (t + 1)), writes=[("INVC", t, t + 1)])
    sc.op("act", _I("activation", out=CACT[:, :], in_=CT[:, :], func=AF.Silu),
          reads=[("CT", 0, 8)], writes=[("CACT", 0, 8)])

    est = {"n": 0}

    egen = {}

    def e_load(src_ap):
        s = est["n"] % 4
        egen[s] = est["n"]
        est["n"] += 1
        dst = _ap3(E[:, s * 1024:(s + 1) * 1024], [[128, 8], [1, 128]])
        sc.dma("pool", "e%d" % s, dst, src_ap, writes=[("E", s * 1024, (s + 1) * 1024)])
        return s + 4 * egen[s]

    def e_slot(tok):
        s = tok % 4
        assert egen[s] == tok // 4, "E-ring slot %d reused before its consumers were emitted" % s
        return s

    def e_w(s, k):
        return E[:, s * 1024 + k * 128: s * 1024 + (k + 1) * 128]

    def rE(s):
        return ("E", s * 1024, (s + 1) * 1024)

    psrot = {"n": 0}

    def next_ps(lo=0, n=6):
        i = lo + psrot["n"] % n
        psrot["n"] += 1
        return i

    def ada_steps(l, big_ring=False):
        wv = w_ada_d[l].rearrange("(k p) e -> p k e", p=128)
        rowb, colb = 6, 7
        steps = []

        def mk(nb, kp):
            def f():
                if big_ring:
                    s = (nb * 4 + kp) % 16
                    RG, rgname, chan = B, "B", "a%d" % s
                else:
                    s = est["n"] % 4
                    egen[s] = est["n"]
                    est["n"] += 1
                    RG, rgname, chan = E, "E", "e%d" % s
                dst = _ap3(RG[:, s * 1024:(s + 1) * 1024], [[512, 2], [1, 512]])
                sc.dma("pool", chan, dst, wv[:, 2 * kp:2 * kp + 2, nb * 512:(nb + 1) * 512],
                       writes=[(rgname, s * 1024, (s + 1) * 1024)])
                for kk in range(2):
                    k = 2 * kp + kk
                    sc.op("pe", _I("matmul", ps[rowb][0:1, :], lhsT=CACT[:, k:k + 1],
                                   rhs=RG[:, s * 1024 + kk * 512: s * 1024 + (kk + 1) * 512], start=(k == 0), stop=(k == 7)),
                          reads=[(rgname, s * 1024, (s + 1) * 1024), ("CACT", 0, 8)], writes=[("PS%d" % rowb, 0, 512)],
                          signal=(k == 7))
                if kp == 3:
                    sc.op("act", _I("activation", out=ROW[0:1, :], in_=ps[rowb][0:1, :], func=AF.Copy),
                          reads=[("PS%d" % rowb, 0, 512)], writes=[("ROW", 0, 512)])
                    for q in range(4):
                        j = nb * 4 + q
                        sc.op("pe", _I("matmul", ps[colb][:, j:j + 1], lhsT=ROW[0:1, q * 128:(q + 1) * 128], rhs=ONE1[0:1, 0:1],
                                       start=True, stop=True),
                              reads=[("ROW", 0, 512), ("ONE1", 0, 1)], writes=[("PS%d" % colb, j, j + 1)], signal=(q == 3))
                if nb == 11 and kp == 3:
                    sc.op("dve", _I("tensor_tensor", out=MOD[:, :], in0=ps[colb][:, 0:48], in1=ppv(l, 0, 48), op=ALU.add),
                          reads=[("PS%d" % colb, 0, 48), rPP], writes=[("MOD", 0, 48)])
            return f

        for nb in range(12):
            for kp in range(4):
                steps.append(mk(nb, kp))
        return steps

    def emit_ada(l, bank):
        for f in ada_steps(l, big_ring=(l == 0)):
            f()

    def emit_coef_layer(l):
        rM = ("MOD", 0, 48)
        sc.op("dve", _I("tensor_scalar", out=coef(2), in0=MOD[:, 16:24], scalar1=1.0, scalar2=1.0 / ALPHA,
                                               op0=ALU.add, op1=ALU.mult), reads=[rM], writes=[rC(2)])
        sc.op("dve", _I("tensor_scalar", out=coef(5), in0=MOD[:, 40:48], scalar1=1.0, scalar2=1.0 / ALPHA,
                                               op0=ALU.add, op1=ALU.mult), reads=[rM], writes=[rC(5)])
        sc.op("dve", _I("scalar_tensor_tensor", out=coef(3), in0=MOD[:, 32:40], scalar=1.0, in1=ppv(l, 52, 8),
                                                      op0=ALU.add, op1=ALU.mult), reads=[rM, rPP], writes=[rC(3)])
        sc.op("dve", _I("scalar_tensor_tensor", out=coef(4), in0=MOD[:, 32:40], scalar=1.0, in1=ppv(l, 60, 8),
                                                      op0=ALU.add, op1=ALU.mult), reads=[rM, rPP], writes=[rC(4)])
        sc.op("dve", _I("tensor_tensor", out=coef(4), in0=coef(4), in1=MOD[:, 24:32], op=ALU.add),
              reads=[rM, rC(4)], writes=[rC(4)])

    def emit_coef_next(lprev):
        rM = ("MOD", 0, 48)
        sc.op("dve", _I("scalar_tensor_tensor", out=coef(6), in0=MOD[:, 8:16], scalar=1.0, in1=ppv(lprev, 68, 8),
                                                      op0=ALU.add, op1=ALU.mult), reads=[rM, rPP], writes=[rC(6)])
        sc.op("dve", _I("scalar_tensor_tensor", out=coef(7), in0=MOD[:, 8:16], scalar=1.0, in1=ppv(lprev, 76, 8),
                                                      op0=ALU.add, op1=ALU.mult), reads=[rM, rPP], writes=[rC(7)])
        sc.op("dve", _I("tensor_tensor", out=coef(7), in0=coef(7), in1=MOD[:, 0:8], op=ALU.add),
              reads=[rM, rC(7)], writes=[rC(7)])

    zb = Tb[:, 0:4096]

    def emit_ln_p1(b):
        c0 = b * 512
        zv = _ap3(X[:, c0:c0 + 512], [[S, 8], [1, 512]])
        rX = [("X", k * S + c0, k * S + c0 + 512) for k in range(8)]
        zb3 = _ap3(zb, [[512, 8], [1, 512]])
        bm, bq = (4, 5) if b % 2 == 0 else (6, 7)
        sc.op("dve", _I("tensor_copy", out=zb3, in_=zv), reads=rX, writes=[("T", 0, 2048)])
        for k in range(8):
            sc.op("act", _I("activation", out=ZQ[:, k * 512:(k + 1) * 512], in_=X[:, k * S + c0:k * S + c0 + 512], func=AF.Square),
                  reads=[rX[k]], writes=[("ZQ", k * 512, (k + 1) * 512)])
        for k in range(8):
            sc.op("pe", _I("matmul", ps[bm][:, :], lhsT=ONES[:, :], rhs=zb[:, k * 512:(k + 1) * 512],
                           start=(k == 0), stop=(k == 7)),
                  reads=[("T", k * 256, (k + 1) * 256), ("ONES", 0, 128)], writes=[("PS%d" % bm, 0, 512)], signal=(k == 7))
        for k in range(8):
            sc.op("pe", _I("matmul", ps[bq][:, :], lhsT=ONES[:, :], rhs=ZQ[:, k * 512:(k + 1) * 512],
                           start=(k == 0), stop=(k == 7)),
                  reads=[("ZQ", k * 512, (k + 1) * 512), ("ONES", 0, 128)], writes=[("PS%d" % bq, 0, 512)], signal=(k == 7))

    def emit_ln_p2(b, l, which, last):
        c0 = b * 512
        zv = _ap3(X[:, c0:c0 + 512], [[S, 8], [1, 512]])
        rX = [("X", k * S + c0, k * S + c0 + 512) for k in range(8)]
        bm, bq = (4, 5) if b % 2 == 0 else (6, 7)
        mean_sb, tmp, rstd = T[:, 2048:2560], T[:, 2560:3072], T[:, 3072:3584]
        rMean, rTmp, rRstd = ("T", 2048, 2560), ("T", 2560, 3072), ("T", 3072, 3584)
        sc.op("act", _I("activation", out=mean_sb, in_=ps[bm][:, :], func=AF.Copy), reads=[("PS%d" % bm, 0, 512)], writes=[rMean])
        sc.op("act", _I("activation", out=tmp, in_=ps[bm][:, :], func=AF.Square), reads=[("PS%d" % bm, 0, 512)], writes=[rTmp])
        sc.op("dve", _I("tensor_tensor", out=tmp, in0=ps[bq][:, :], in1=tmp, op=ALU.subtract),
              reads=[("PS%d" % bq, 0, 512), rTmp], writes=[rTmp])
        sc.op("act", _I("activation", out=tmp, in_=tmp, func=AF.Ln, bias=coef_eps), reads=[rTmp, ("EPS", 0, 1)], writes=[rTmp])
        sc.op("act", _I("activation", out=rstd, in_=tmp, func=AF.Exp, scale=-0.5), reads=[rTmp], writes=[rRstd])
        sc.op("dve", _I("scalar_tensor_tensor", out=mean_sb, in0=mean_sb, scalar=-1.0, in1=rstd, op0=ALU.mult, op1=ALU.mult),
              reads=[rMean, rRstd], writes=[rMean])
        nmr_b = _ap3(mean_sb, [[0, 8], [1, 512]])
        rstd_b = _ap3(rstd, [[0, 8], [1, 512]])
        sc.op("dve", _I("tensor_tensor", out=zv, in0=zv, in1=rstd_b, op=ALU.mult), reads=rX + [rRstd], writes=rX)
        sc.op("dve", _I("tensor_tensor", out=zv, in0=zv, in1=nmr_b, op=ALU.add), reads=rX + [rMean], writes=rX)
        if which == 1:
            ca, cb_, g_off, b_off = 3, 4, 52, 60
        else:
            ca, cb_, g_off, b_off = 6, 7, 68, 76
        for k in range(8):
            xk = X[:, k * S + c0: k * S + c0 + 512]
            rXk = ("X", k * S + c0, k * S + c0 + 512)
            if not (which == 2 and last):
                ak = A[:, k * S + c0: k * S + c0 + 512]
                sc.op("act", _I("activation", out=ak, in_=xk, func=AF.Identity,
                                scale=coef(ca)[:, k:k + 1], bias=coef(cb_)[:, k:k + 1]),
                      reads=[rXk, rC(ca), rC(cb_)], writes=[("A", k * S + c0, k * S + c0 + 512)])
            if k < 5:
                sc.op("dve", _I("tensor_scalar", out=xk, in0=xk, scalar1=ppv(l, g_off + k, 1),
                                scalar2=ppv(l, b_off + k, 1), op0=ALU.mult, op1=ALU.add),
                      reads=[rXk, rPP], writes=[rXk])
            else:
                sc.op("act", _I("activation", out=xk, in_=xk, func=AF.Identity,
                                scale=ppv(l, g_off + k, 1), bias=ppv(l, b_off + k, 1)),
                      reads=[rXk, rPP], writes=[rXk])

    def emit_ln(b, l, which, last):
        emit_ln_p1(b)
        emit_ln_p2(b, l, which, last)

    state = {}

    def out_block(b):
        for c in range(8):
            lo = c * S + b * 512
            sc.dma("sp", "out", oT_d[c * 128:(c + 1) * 128, b * 512:(b + 1) * 512], X[:, lo:lo + 512], reads=[("X", lo, lo + 512)])
        if b == 3:
            state["out_done"] = True

    def stop(tag):
        if DBG_STOP == tag:
            raise _Stop()

    def body():
        emit_ada(0, 6)
        rM = ("MOD", 0, 48)
        sc.op("dve", _I("tensor_scalar_add", out=coef(0), in0=MOD[:, 8:16], scalar1=1.0), reads=[rM], writes=[rC(0)])
        sc.op("dve", _I("tensor_copy", out=coef(1), in_=MOD[:, 0:8]), reads=[rM], writes=[rC(1)])
        for k in range(8):
            for b in range(4):
                lo = k * S + b * 512
                if (k + b) % 2 == 0:
                    sc.op("act", _I("activation", out=A[:, lo:lo + 512], in_=X[:, lo:lo + 512], func=AF.Identity,
                                    scale=coef(0)[:, k:k + 1], bias=coef(1)[:, k:k + 1]),
                          reads=[("X", lo, lo + 512), rC(0), rC(1)], writes=[("A", lo, lo + 512)])
                else:
                    sc.op("dve", _I("tensor_scalar", out=A[:, lo:lo + 512], in0=X[:, lo:lo + 512],
                                    scalar1=coef(0)[:, k:k + 1], scalar2=coef(1)[:, k:k + 1], op0=ALU.mult, op1=ALU.add),
                          reads=[("X", lo, lo + 512), rC(0), rC(1)], writes=[("A", lo, lo + 512)])

        stop('h0')
        pre0 = {}
        for l in range(L):
            last = (l == L - 1)
            emit_coef_layer(l)
            win = w_in_d[l].rearrange("(k p) e -> p k e", p=128)
            sc.dma("pool", "wp", _ap3(WP[:, 0:512], [[128, 4], [1, 128]]), w_pool_d[l].rearrange("g c d -> c g d"),
                   writes=[("WP", 0, 512)])
            sc.dma("pool", "bt", BT[:, :], bt_d[l], writes=[("BT", 0, 3072)])

            ZQf = ZQ[:, :].bitcast(F32)

            def pool_bufs(bb):
                if bb == 0:
                    return [(ZQf[:, o:o + 528], ("ZQ", 2 * o, 2 * (o + 528))) for o in (0, 528, 1056)]
                return [(T[:, o:o + 528], ("T", o, o + 528)) for o in (1664, 2192, 2720)]

            pstate = {}

            def pool_F1(g, b):
                def f():
                    if b == 0:
                        pstate["s"] = e_load(win[:, :, g * 128:(g + 1) * 128])
                    s = e_slot(pstate["s"])
                    (u, rU), _, _ = pool_bufs(b % 2)
                    bank = next_ps(6, 2)
                    for k in range(8):
                        sc.op("pe", _I("matmul", ps[bank][:, :], lhsT=e_w(s, k),
                                       rhs=A[:, k * S + b * 512: k * S + (b + 1) * 512], start=(k == 0), stop=(k == 7)),
                              reads=[rE(s), ("A", k * S + b * 512, k * S + (b + 1) * 512)],
                              writes=[("PS%d" % bank, 0, 512)], signal=(k == 7))
                    rg, lo_, hi_ = rU
                    sc_ = 2 if rg == "ZQ" else 1
                    if b == 0:
                        sc.op("dve", _I("memset", u[:, 0:16], 0.0), writes=[(rg, lo_, lo_ + 16 * sc_)])
                    else:
                        (pu, rPU), _, _ = pool_bufs((b - 1) % 2)
                        prg, plo, phi = rPU
                        sc.op("dve", _I("tensor_copy", out=u[:, 0:16], in_=pu[:, 512:528]),
                              reads=[(prg, phi - 16 * (2 if prg == "ZQ" else 1), phi)], writes=[(rg, lo_, lo_ + 16 * sc_)])
                    sc.op("act", _I("activation", out=u[:, 16:528], in_=ps[bank][:, :], func=AF.Copy),
                          reads=[("PS%d" % bank, 0, 512)], writes=[(rg, lo_ + 16 * sc_, hi_)])
                    w = POOL_W[g]
                    (u, rU), (sa, rSa), (sb_, rSb) = pool_bufs(b % 2)
                    src, rsrc = u, rU
                    bufs = [(sa, rSa), (sb_, rSb)]
                    step, lo2 = 1, 1
                    nst = {2: 1, 4: 2, 8: 3, 16: 4}[w]
                    for it in range(nst):
                        dst, rdst = bufs[it % 2]
                        sc.op("dve", _I("tensor_tensor", out=dst[:, lo2:528], in0=src[:, lo2:528],
                                         in1=src[:, lo2 - step:528 - step], op=ALU.add), reads=[rsrc], writes=[rdst])
                        src, rsrc = dst, rdst
                        step *= 2
                        lo2 = 2 * lo2 + 1
                    pdst, rpd = bufs[nst % 2]
                    if rpd[0] == "ZQ":
                        pbf = ZQ[:, rpd[1]:rpd[1] + 512]
                    else:
                        pbf = Tb[:, 2 * rpd[1]:2 * rpd[1] + 512]
                    if b == 0:
                        sc.op("dve", _I("tensor_tensor", out=PF[:, 0:w - 1], in0=src[:, 16:16 + w - 1], in1=INVC[:, 0:w - 1],
                                         op=ALU.mult), reads=[rsrc, ("INVC", 0, 16)], writes=[("PF", 0, 16)])
                    sc.op("dve", _I("tensor_scalar_mul", out=src[:, 16:528], in0=src[:, 16:528], scalar1=1.0 / w),
                          reads=[rsrc], writes=[rsrc])
                    sc.op("dve", _I("tensor_tensor", out=pbf, in0=src[:, 16:528], in1=u[:, 16:528], op=ALU.subtract),
                          reads=[rsrc, rU], writes=[rpd])
                    if b == 0:
                        sc.op("dve", _I("tensor_tensor", out=pbf[:, 0:w - 1], in0=PF[:, 0:w - 1], in1=u[:, 16:16 + w - 1],
                                         op=ALU.subtract), reads=[("PF", 0, 16), rU], writes=[rpd])
                    pstate[(g, b)] = (pbf, rpd)
                return f

            def pool_F3(g, b):
                def f():
                    pbf, rpd = pstate[(g, b)]
                    bank2 = next_ps(6, 2)
                    sc.op("pe", _I("matmul", ps[bank2][:, :], lhsT=WP[:, g * 128:(g + 1) * 128], rhs=pbf, start=True, stop=True),
                          reads=[rpd, ("WP", 0, 512)], writes=[("PS%d" % bank2, 0, 512)])
                    yl = g * S + b * 512
                    sc.op("act", _I("activation", out=Y[:, yl:yl + 512], in_=ps[bank2][:, :], func=AF.Copy, scale=ppv(l, 48 + g, 1)),
                          reads=[("PS%d" % bank2, 0, 512), rPP], writes=[("Y", yl, yl + 512)])
                return f

            gb = [(g, b) for g in range(4) for b in range(4)]
            pool_pieces = []
            for idx in range(18):
                fs = []
                if idx >= 2:
                    fs.append(pool_F3(*gb[idx - 2]))
                if idx < 16:
                    fs.append(pool_F1(*gb[idx]))
                pool_pieces.append(lambda fs=fs: [f() for f in fs])

            stop('pool')

            def qkv_pieces(c, wsrc=None, pslo=6, psn=2):
                wsrc = win if wsrc is None else wsrc
                j = c % 2
                bo = j * 8192
                qo, ko, vo = bo, bo + 2048, bo + 4096
                pieces = []
                st = {}

                def mk_load(key, col0):
                    def f():
                        st[key] = e_load(wsrc[:, :, col0 + c * 128: col0 + (c + 1) * 128])
                    return f

                def mk_qk(key, dsto, scl, b):
                    def f():
                        s = e_slot(st[key])
                        bank = next_ps(pslo, psn)
                        for k in range(8):
                            sc.op("pe", _I("matmul", ps[bank][:, :], lhsT=e_w(s, k),
                                           rhs=A[:, k * S + b * 512: k * S + (b + 1) * 512], start=(k == 0), stop=(k == 7)),
                                  reads=[rE(s), ("A", k * S + b * 512, k * S + (b + 1) * 512)],
                                  writes=[("PS%d" % bank, 0, 512)], signal=(k == 7))
                        dl = dsto + b * 512
                        sc.op("act", _I("activation", out=B[:, dl:dl + 512], in_=ps[bank][:, :], func=AF.Copy, scale=scl),
                              reads=[("PS%d" % bank, 0, 512)], writes=[("B", dl, dl + 512)])
                    return f

                def mk_v(t):
                    def f():
                        s = e_slot(st["v"])
                        tt = t % 4
                        if tt == 0:
                            st["vbank"] = next_ps(pslo, psn)
                        bank = st["vbank"]
                        for k in range(8):
                            sc.op("pe", _I("matmul", ps[bank][:, tt * 128:(tt + 1) * 128],
                                           lhsT=A[:, k * S + t * 128: k * S + (t + 1) * 128], rhs=e_w(s, k),
                                           start=(k == 0), stop=(k == 7)),
                                  reads=[rE(s), ("A", k * S + t * 128, k * S + (t + 1) * 128)],
                                  writes=[("PS%d" % bank, tt * 128, (tt + 1) * 128)], signal=(k == 7))
                        if tt == 3:
                            vt = vo + (t - 3) * 192
                            psv = ps[bank]
                            srcA = _ap3(psv[:, 0:64], [[128, 4], [1, 64]])
                            srcB = _ap3(psv[:, 64:128], [[128, 4], [1, 64]])
                            dstA = _ap3(B[:, vt:vt + 64], [[192, 4], [1, 64]])
                            dstB = _ap3(B[:, vt + 128:vt + 192], [[192, 4], [1, 64]])
                            rV = ("B", vt, vt + 4 * 192)
                            sc.op("dve", _I("tensor_copy", out=dstA, in_=srcA), reads=[("PS%d" % bank, 0, 512)], writes=[rV])
                            sc.op("dve", _I("tensor_copy", out=dstB, in_=srcB), reads=[("PS%d" % bank, 0, 512)], writes=[rV])
                    return f

                def mk_ones():
                    def f():
                        onesv = _ap3(B[:, vo + 64: vo + 128], [[192, 16], [1, 64]])
                        sc.op("dve", _I("memset", onesv, 1.0), writes=[("B", vo, vo + 3072)])
                        st["v"] = e_load(wsrc[:, :, 1536 + c * 128: 1536 + (c + 1) * 128])
                    return f

                pieces.append(mk_load("q", 512))
                for b in range(4):
                    pieces.append(mk_qk("q", qo, 0.125, b))
                pieces.append(mk_load("k", 1024))
                for b in range(4):
                    pieces.append(mk_qk("k", ko, 1.0, b))
                pieces.append(mk_ones())
                for t in range(16):
                    pieces.append(mk_v(t))
                return pieces

            def att_S(u, c, m, X_):
                j = c % 2
                bo = j * 8192
                qo, ko = bo, bo + 2048
                sset = u % 2
                U0, U1 = 2 * sset, 2 * sset + 1
                r0 = 64 * X_
                valid = list(range(max(0, 4 - m), 5))
                h = 2 * c + X_
                v034 = [i for i in valid if i in (0, 3, 4)]
                s_lo = 3 - len(v034)
                nn = len(v034) * 128
                bto = h * 384 + s_lo * 128
                sc.op("pe", _I("matmul", ps[U1][:, s_lo * 128: s_lo * 128 + nn], lhsT=IDENT[:, :], rhs=BT[:, bto:bto + nn],
                               start=True, stop=False),
                      reads=[("IDENT", 0, 128), ("BT", bto, bto + nn)], writes=[("PS%d" % U1, 0, 512)], signal=False)
                order = [i for i in valid if i in (0, 3, 4)] + [i for i in valid if i in (1, 2)]
                for i in order:
                    tk = m - 4 + i
                    if i in (1, 2):
                        bank_, col, first = U0, (i - 1) * 128, True
                    else:
                        bank_, col, first = U1, {0: 0, 3: 1, 4: 2}[i] * 128, False
                    sc.op("pe", _I("matmul", ps[bank_][:, col:col + 128],
                                   lhsT=B[r0:r0 + 64, ko + tk * 128: ko + (tk + 1) * 128],
                                   rhs=B[r0:r0 + 64, qo + m * 128: qo + (m + 1) * 128], start=first,
                                   stop=(first or i == 4)),
                          reads=[("B", ko + tk * 128, ko + (tk + 1) * 128), ("B", qo + m * 128, qo + (m + 1) * 128)],
                          writes=[("PS%d" % bank_, col, col + 128)], signal=(i == order[-1] or i == 4))

            def att_SM(u, c, m, X_):
                sset = u % 2
                U0, U1 = 2 * sset, 2 * sset + 1
                h = 2 * c + X_
                valid = list(range(max(0, 4 - m), 5))
                sbo = sset * 384
                pto = (768 + sset * 320) * 2
                v12 = [i for i in valid if i in (1, 2)]
                if v12:
                    c_lo = (v12[0] - 1) * 128
                    n12 = len(v12) * 128
                    dlo = pto + (v12[0] - 1) * 128
                    sc.op("act", _I("activation", out=Tb[:, dlo:dlo + n12], in_=ps[U0][:, c_lo:c_lo + n12],
                                    func=AF.Exp, bias=ppv(l, 84 + h, 1)),
                          reads=[("PS%d" % U0, c_lo, c_lo + n12), rPP], writes=[("T", dlo // 2, (dlo + n12) // 2)])
                v034 = [i for i in valid if i in (0, 3, 4)]
                s_lo = {3: 0, 2: 1, 1: 2}[len(v034)]
                nn = len(v034) * 128
                d0 = pto + (2 + s_lo) * 128
                sc.op("act", _I("activation", out=Tb[:, d0:d0 + nn], in_=ps[U1][:, s_lo * 128: s_lo * 128 + nn], func=AF.Exp),
                      reads=[("PS%d" % U1, 0, 512)], writes=[("T", d0 // 2, (d0 + nn) // 2)])

            def att_PV(u, c, m, X_):
                j = c % 2
                vo = j * 8192 + 4096
                sset = u % 2
                PO = 4 + sset
                pto = (768 + sset * 320) * 2
                valid = list(range(max(0, 4 - m), 5))
                vc = 0 if X_ == 0 else 64
                for n_, i in enumerate(valid):
                    tk = m - 4 + i
                    sl = {1: 0, 2: 1, 0: 2, 3: 3, 4: 4}[i]
                    pc = pto + sl * 128
                    vcol = vo + tk * 192 + vc
                    sc.op("pe", _I("matmul", ps[PO][:, 0:128], lhsT=B[:, vcol:vcol + 128], rhs=Tb[:, pc:pc + 128],
                                   start=(n_ == 0), stop=(n_ == len(valid) - 1)),
                          reads=[("B", vcol, vcol + 128), ("T", pc // 2, (pc + 128) // 2)],
                          writes=[("PS%d" % PO, 0, 128)], signal=(n_ == len(valid) - 1))

            def att_NORM(u, c, m, X_):
                sset = u % 2
                PO = 4 + sset
                dn = 1408 + sset * 128
                yc = (4 + c) * S + m * 128
                if X_ == 0:
                    nlo, dlo_ = 0, 64
                else:
                    nlo, dlo_ = 64, 0
                if X_ == 0:
                    sc.op("dve", _I("reciprocal", out=T[nlo:nlo + 64, dn:dn + 128], in_=ps[PO][dlo_:dlo_ + 64, 0:128]),
                          reads=[("PS%d" % PO, 0, 128)], writes=[("T", dn, dn + 128)])
                else:
                    sc.op("act", _I("activation", out=T[nlo:nlo + 64, dn:dn + 128], in_=ps[PO][dlo_:dlo_ + 64, 0:128], func=AF.Ln),
                          reads=[("PS%d" % PO, 0, 128)], writes=[("T", dn, dn + 128)])
                    sc.op("act", _I("activation", out=T[nlo:nlo + 64, dn:dn + 128], in_=T[nlo:nlo + 64, dn:dn + 128],
                                    func=AF.Exp, scale=-1.0),
                          reads=[("T", dn, dn + 128)], writes=[("T", dn, dn + 128)])
                sc.op("dve", _I("tensor_tensor", out=Y[nlo:nlo + 64, yc:yc + 128], in0=ps[PO][nlo:nlo + 64, 0:128],
                                in1=T[nlo:nlo + 64, dn:dn + 128], op=ALU.mult),
                      reads=[("PS%d" % PO, 0, 128), ("T", dn, dn + 128)], writes=[("Y", yc, yc + 128)])

            units = [(c, m, X_) for c in range(4) for m in range(16) for X_ in range(2)]
            if not pre0.get(l):
                for f in qkv_pieces(0):
                    f()
            stop('att_a')
            fill = []
            for u, (c, m, X_) in enumerate(units):
                if m == 0 and X_ == 0:
                    for f in fill:
                        f()
                    fill = (qkv_pieces(c + 1) if c + 1 < 4 else []) + pool_pieces[6 * c:6 * c + 6]
                    if c == 3:
                        wo = w_out_d[l].rearrange("(k p) e -> p k e", p=128)
                        for hh in range(2):
                            sc.dma("pool", "wo%d" % hh, _ap3(B[:, hh * 4096:(hh + 1) * 4096], [[1024, 4], [1, 1024]]),
                                   wo[:, hh * 4:(hh + 1) * 4, :], writes=[("B", hh * 4096, (hh + 1) * 4096)])
                if u == 0:
                    att_S(u, c, m, X_)
                    att_SM(u, c, m, X_)
                if u + 1 < len(units):
                    att_S(u + 1, *units[u + 1])
                    att_SM(u + 1, *units[u + 1])
                att_PV(u, c, m, X_)
                if fill:
                    fill.pop(0)()
                att_NORM(u, c, m, X_)

            stop('attn')
            w1v = w_ff1_d[l].rearrange("(k p) f -> p k f", p=128)
            w2v = w_ff2_d[l].rearrange("(fc p) d -> p fc d", p=128)

            def load_chunk(jp, ci):
                so_ = 8192 if jp % 2 == 0 else 0
                ch = "f%d_%d" % (ci, jp % 2)
                if ci < 4:
                    lo = so_ + ci * 1024
                    sc.dma("pool", ch, _ap3(B[:, lo:lo + 1024], [[512, 2], [1, 512]]),
                           w1v[:, 2 * ci:2 * ci + 2, jp * 512:(jp + 1) * 512], writes=[("B", lo, lo + 1024)])
                else:
                    fc = ci - 4
                    lo = so_ + 4096 + fc * 1024
                    sc.dma("pool", ch, B[:, lo:lo + 1024], w2v[:, jp * 4 + fc, :], writes=[("B", lo, lo + 1024)])

            def load_piece(jp):
                for ci in range(8):
                    load_chunk(jp, ci)

            load_piece(0)

            def s1d_mm(b):
                for dc in range(8):
                    bank = next_ps(0, 4)
                    for k in range(8):
                        sc.op("pe", _I("matmul",
                            ps[bank][:, :], lhsT=B[:, k * 1024 + dc * 128: k * 1024 + (dc + 1) * 128],
                            rhs=Y[:, k * S + b * 512: k * S + (b + 1) * 512], start=(k == 0), stop=(k == 7)),
                            reads=[("B", k * 1024 + dc * 128, k * 1024 + (dc + 1) * 128), ("Y", k * S + b * 512, k * S + (b + 1) * 512)],
                            writes=[("PS%d" % bank, 0, 512)], signal=(k == 7))
                    xl = dc * S + b * 512
                    sc.op("dve", _I("scalar_tensor_tensor",
                        out=X[:, xl:xl + 512], in0=ps[bank][:, :], scalar=coef(2)[:, dc:dc + 1], in1=X[:, xl:xl + 512],
                        op0=ALU.mult, op1=ALU.add),
                        reads=[("PS%d" % bank, 0, 512), rC(2), ("X", xl, xl + 512)], writes=[("X", xl, xl + 512)])

            asteps = ada_steps(l + 1) if not last else []

            def ffn_ff1(jp, b, narrow):
                so_ = 8192 if jp % 2 == 0 else 0
                if 1 <= jp <= 6:
                    for _ in range(2):
                        if asteps:
                            asteps.pop(0)()
                    if not asteps and not last and jp == 6 and b == 3:
                        emit_coef_next(l)
                fb = (jp * 4 + b) % 3
                fo = fb * S
                for fc in range(4):
                    bank = next_ps(0, 2) if narrow else next_ps(0, 3)
                    for k in range(8):
                        sc.op("pe", _I("matmul",
                            ps[bank][:, :], lhsT=B[:, so_ + k * 512 + fc * 128: so_ + k * 512 + (fc + 1) * 128],
                            rhs=A[:, k * S + b * 512: k * S + (b + 1) * 512], start=(k == 0), stop=(k == 7)),
                            reads=[("B", so_ + k * 512 + fc * 128, so_ + k * 512 + (fc + 1) * 128),
                                   ("A", k * S + b * 512, k * S + (b + 1) * 512)],
                            writes=[("PS%d" % bank, 0, 512)], signal=(k == 7))
                    rt = 3584 + (fc % 2) * 512
                    sc.op("act", _I("activation", out=T[:, rt:rt + 512], in_=ps[bank][:, :], func=AF.Relu),
                          reads=[("PS%d" % bank, 0, 512)], writes=[("T", rt, rt + 512)])
                    fl = fo + fc * 512
                    sc.op("act", _I("activation", out=Y[:, fl:fl + 512], in_=T[:, rt:rt + 512], func=AF.Square),
                          reads=[("T", rt, rt + 512)], writes=[("Y", fl, fl + 512)])

            def ffn_ff2(jp, b, narrow):
                so_ = 8192 if jp % 2 == 0 else 0
                if jp + 1 < 8:
                    load_chunk(jp + 1, 2 * b)
                    load_chunk(jp + 1, 2 * b + 1)
                fb = (jp * 4 + b) % 3
                fo = fb * S
                for dc in range(8):
                    bank = next_ps(2, 2) if narrow else next_ps(3, 3)
                    for fc in range(4):
                        sc.op("pe", _I("matmul",
                            ps[bank][:, :], lhsT=B[:, so_ + 4096 + fc * 1024 + dc * 128: so_ + 4096 + fc * 1024 + (dc + 1) * 128],
                            rhs=Y[:, fo + fc * 512: fo + (fc + 1) * 512], start=(fc == 0), stop=(fc == 3)),
                            reads=[("B", so_ + 4096 + fc * 1024 + dc * 128, so_ + 4096 + fc * 1024 + (dc + 1) * 128),
                                   ("Y", fo + fc * 512, fo + (fc + 1) * 512)],
                            writes=[("PS%d" % bank, 0, 512)], signal=(fc == 3))
                    xl = dc * S + b * 512
                    sc.op("dve", _I("scalar_tensor_tensor",
                        out=X[:, xl:xl + 512], in0=ps[bank][:, :], scalar=coef(5)[:, dc:dc + 1], in1=X[:, xl:xl + 512],
                        op0=ALU.mult, op1=ALU.add),
                        reads=[("PS%d" % bank, 0, 512), rC(5), ("X", xl, xl + 512)], writes=[("X", xl, xl + 512)])

            def ffn_slot(jp, b, narrow):
                ffn_ff1(jp, b, narrow)
                ffn_ff2(jp, b, narrow)

            R = sc.record

            def mmp1(b):
                return R(lambda: (s1d_mm(b), emit_ln_p1(b)))

            def p2(b, which):
                return R(lambda: emit_ln_p2(b, l, which, last))

            def zip_p2(sa, b, which):
                sb_ = p2(b, which)
                for st_ in sb_[:3]:
                    sc.replay(st_)
                sc.zip_emit(sa, sb_[3:])

            s1d_mm(0); emit_ln_p1(0)
            s1d_mm(1); emit_ln_p1(1)
            zip_p2(mmp1(2), 0, 1)
            zip_p2(mmp1(3), 1, 1)
            zip_p2(R(lambda: ffn_slot(0, 0, True)), 2, 1)
            zip_p2(R(lambda: ffn_slot(0, 1, True)), 3, 1)
            ffn_slot(0, 2, True)
            ffn_slot(0, 3, True)
            stop('ln1')
            slots = [(jp, b) for jp in range(1, 7) for b in range(4)]
            ffn_ff1(slots[0][0], slots[0][1], False)
            for i, (jp, b) in enumerate(slots):
                if i + 1 < len(slots):
                    ffn_ff1(slots[i + 1][0], slots[i + 1][1], False)
                ffn_ff2(jp, b, False)
            ffn_slot(7, 0, True); emit_ln_p1(0)
            ffn_slot(7, 1, True); emit_ln_p1(1)
            zip_p2(R(lambda: (ffn_slot(7, 2, True), emit_ln_p1(2))), 0, 2)
            if last:
                out_block(0)
            zip_p2(R(lambda: (ffn_slot(7, 3, True), emit_ln_p1(3))), 1, 2)
            if last:
                out_block(1)
            stop('ffn')
            if not last:
                wnext = w_in_d[l + 1].rearrange("(k p) e -> p k e", p=128)
                pcs = qkv_pieces(0, wsrc=wnext, pslo=0, psn=4)
                for i_ in (0, 5, 10):
                    pcs[i_]()

                def grp(b):
                    return [pcs[1 + b], pcs[6 + b]] + [pcs[11 + 4 * b + t] for t in range(4)]
                zip_p2(R(lambda: [f() for f in grp(0) + grp(1)]), 2, 2)
                zip_p2(R(lambda: [f() for f in grp(2)]), 3, 2)
                for f in grp(3):
                    f()
                pre0[l + 1] = True
            else:
                emit_ln_p2(2, l, 2, last)
                out_block(2)
                emit_ln_p2(3, l, 2, last)
                out_block(3)

    try:
        body()
    except _Stop:
        pass

    if DBG_STOP is not None or not state.get("out_done"):
        for c in range(8):
            sc.dma("sp", "out", oT_d[c * 128:(c + 1) * 128, :], X[:, c * S:(c + 1) * S], reads=[("X", c * S, (c + 1) * S)])
    sc.final_wait("sp", ["out"])

    with nc.Block() as block:
        sc.emit(block)
    return nc


def _bias_tables(rel_bias):
    p = np.arange(128)[:, None]
    q = np.arange(128)[None, :]
    out = np.empty((DEPTH, 128, 8, 3, 128), np.float32)
    for bi, i in enumerate((0, 3, 4)):
        diff = 128 * (4 - i) + q - p
        idx = np.clip(diff, -63, 128) + 63
        tab = rel_bias[:, :, idx]
        if i == 0:
            mask = (p < 64) & (q >= 64)
        elif i == 4:
            mask = (p >= 64) & (q < 64)
        else:
            mask = np.zeros((128, 128), bool)
        tab = np.where(mask[None, None], np.float32(MASKV), tab)
        out[:, :, :, bi, :] = np.transpose(tab, (0, 2, 1, 3))
    return out.reshape(DEPTH, 128, 8 * 3 * 128)


def _col(v):
    return np.ascontiguousarray(v.reshape(-1, 128).T)


_NC_CACHE = {}


def kernel(x, c, w_ada, b_ada, w_in, w_pool, pool_scale, rel_bias, w_out,
           ln1_g, ln1_b, w_ff1, w_ff2, ln2_g, ln2_b):
    f = lambda a: np.ascontiguousarray(np.asarray(a, dtype=np.float32))
    x, c, w_ada, b_ada, w_in, w_pool, pool_scale, rel_bias, w_out = map(
        f, (x, c, w_ada, b_ada, w_in, w_pool, pool_scale, rel_bias, w_out))
    ln1_g, ln1_b, w_ff1, w_ff2, ln2_g, ln2_b = map(f, (ln1_g, ln1_b, w_ff1, w_ff2, ln2_g, ln2_b))
    pp = np.zeros((128, DEPTH * NPP), np.float32)
    for l in range(DEPTH):
        o = l * NPP
        pp[:, o:o + 48] = _col(b_ada[l])
        pp[:, o + 48:o + 52] = _col(pool_scale[l])
        pp[:, o + 52:o + 60] = _col(ln1_g[l])
        pp[:, o + 60:o + 68] = _col(ln1_b[l])
        pp[:, o + 68:o + 76] = _col(ln2_g[l])
        pp[:, o + 76:o + 84] = _col(ln2_b[l])
        pp[:, o + 84:o + 92] = np.broadcast_to(rel_bias[l, :, 191][None, :], (128, 8))
    bt = _bias_tables(rel_bias)
    L = _NC_CACHE.get("L", DEPTH)
    if ("nc", L) not in _NC_CACHE:
        _NC_CACHE[("nc", L)] = build_nc(L)
    nc = _NC_CACHE[("nc", L)]
    in_maps = []
    for b in range(NCORES):
        in_maps.append({
            "xT": np.ascontiguousarray(x[b].T), "cT": _col(c[b]), "pp": pp, "bt": bt, "ident": np.eye(128, dtype=np.float32),
            "w_ada": w_ada, "w_in": w_in, "w_pool": w_pool, "w_out": w_out, "w_ff1": w_ff1, "w_ff2": w_ff2,
        })
    res = run_bass_kernel_spmd(nc, in_maps, core_ids=list(range(NCORES)))
    out = np.empty((NCORES, S, D), np.float32)
    for b in range(NCORES):
        out[b] = np.asarray(res.results[b]["oT"]).T
    return out
```

```python
import numpy as np
import concourse.bass as bass
import concourse.mybir as mybir
from concourse.bass_utils import run_bass_kernel_spmd

F32 = mybir.dt.float32
BF16 = mybir.dt.bfloat16
ALU = mybir.AluOpType
AF = mybir.ActivationFunctionType

D = 1024
S = 2048
DEPTH = 4
NCORES = 8
ALPHA = (2.0 * DEPTH) ** 0.25
LN_EPS = 1e-5
EPS_P = LN_EPS / (ALPHA * ALPHA)
MASKV = -100.0
NPP = 92
POOL_W = (2, 4, 8, 16)


class _Seg:
    __slots__ = ("w", "r")

    def __init__(self, w=None, r=None):
        self.w = w
        self.r = dict(r) if r else {}


class _Region:
    def __init__(self, size):
        self.b = [0, size]
        self.s = [_Seg()]

    def _split(self, x):
        import bisect
        i = bisect.bisect_left(self.b, x)
        if self.b[i] == x:
            return
        old = self.s[i - 1]
        self.b.insert(i, x)
        self.s.insert(i, _Seg(old.w, old.r))

    def segs(self, lo, hi):
        import bisect
        assert 0 <= lo < hi <= self.b[-1], (lo, hi, self.b[-1])
        self._split(lo)
        self._split(hi)
        i0 = bisect.bisect_left(self.b, lo)
        i1 = bisect.bisect_left(self.b, hi)
        return self.s[i0:i1]


class Sched:
    ENG = ("pe", "act", "dve", "pool", "sp")

    def __init__(self, nc):
        self.nc = nc
        self.sems = {}
        self.count = {}
        self.prog = {e: [] for e in self.ENG}
        self.seen = {e: {} for e in self.ENG}
        self.regions = {}
        for e in self.ENG:
            self._sem(e)

    def _sem(self, key):
        if key not in self.sems:
            self.sems[key] = self.nc.alloc_semaphore("s_" + str(key))
            self.count[key] = 0
        return self.sems[key]

    def region(self, name, size):
        self.regions[name] = _Region(size)

    def _deps(self, eng, reads, writes):
        deps = {}

        def add(d):
            if d is None:
                return
            k, v = d
            if deps.get(k, 0) < v:
                deps[k] = v

        for (rg, lo, hi) in reads:
            for sg in self.regions[rg].segs(lo, hi):
                add(sg.w)
        for (rg, lo, hi) in writes:
            for sg in self.regions[rg].segs(lo, hi):
                add(sg.w)
                for k, v in sg.r.items():
                    add((k, v))
        return deps

    def _emit_waits(self, eng, deps):
        seen = self.seen[eng]
        for k, v in deps.items():
            if k == "pe" and eng == "pe":
                continue
            if seen.get(k, 0) >= v:
                continue
            seen[k] = v
            self.prog[eng].append(("wait", k, v))

    def _mark(self, reads, writes, key, val):
        for (rg, lo, hi) in reads:
            for sg in self.regions[rg].segs(lo, hi):
                if sg.r.get(key, 0) < val:
                    sg.r[key] = val
        for (rg, lo, hi) in writes:
            for sg in self.regions[rg].segs(lo, hi):
                sg.w = (key, val)
                sg.r = {}

    @staticmethod
    def _bank(rs):
        return [((rg, 0, 512) if rg.startswith("PS") else (rg, lo, hi)) for (rg, lo, hi) in rs]

    def record(self, fn):
        self._rec = []
        try:
            fn()
        finally:
            rec, self._rec = self._rec, None
        steps, cur = [], []
        for call in rec:
            cur.append(call)
            if not (call[0] == "op" and call[1][0] == "pe"):
                steps.append(cur)
                cur = []
        if cur:
            steps.append(cur)
        return steps

    def replay(self, step):
        for kind, a, k in step:
            (self.op if kind == "op" else self.dma)(*a, **k)

    def zip_emit(self, sa, sb):
        na, nb = len(sa), len(sb)
        ia = ib = 0
        while ia < na or ib < nb:
            if ib >= nb or (ia < na and ia * nb <= ib * na):
                self.replay(sa[ia]); ia += 1
            else:
                self.replay(sb[ib]); ib += 1

    def op(self, eng, fn, reads=(), writes=(), signal=True, drain=False):
        if getattr(self, "_rec", None) is not None:
            self._rec.append(("op", (eng, fn), dict(reads=reads, writes=writes, signal=signal, drain=drain)))
            return
        reads, writes = self._bank(reads), self._bank(writes)
        deps = self._deps(eng, reads, writes)
        self._emit_waits(eng, deps)
        if drain and self.count[eng] > 0:
            self.prog[eng].append(("wait", eng, self.count[eng]))
        if signal:
            self.count[eng] += 1
            val = self.count[eng]
            self.prog[eng].append(("op", fn, eng, 1))
        else:
            val = self.count[eng] + 1
            self.prog[eng].append(("op", fn, None, 0))
        self._mark(reads, writes, eng, val)

    def dma(self, eng, chan, out, in_, reads=(), writes=()):
        if getattr(self, "_rec", None) is not None:
            self._rec.append(("dma", (eng, chan, out, in_), dict(reads=reads, writes=writes)))
            return
        self._sem(chan)
        deps = self._deps(eng, reads, writes)
        self._emit_waits(eng, deps)
        self.count[chan] += 16
        val = self.count[chan]
        self.prog[eng].append(("op", _I("dma_start", out=out, in_=in_), chan, 16))
        self._mark(reads, writes, chan, val)

    def final_wait(self, eng, chans):
        for c in chans:
            self.prog[eng].append(("wait", c, self.count[c]))

    def emit(self, block):
        nc = self.nc
        sems = self.sems

        def run(engobj, name):
            for it in self.prog[name]:
                if it[0] == "wait":
                    engobj.wait_ge(sems[it[1]], it[2])
                else:
                    ins = it[1](engobj)
                    if it[2] is not None:
                        ins.then_inc(sems[it[2]], it[3])

        @block.tensor
        def _(e):
            run(e, "pe")

        @block.scalar
        def _(e):
            run(e, "act")

        @block.vector
        def _(e):
            run(e, "dve")

        @block.gpsimd
        def _(e):
            run(e, "pool")

        @block.sync
        def _(e):
            run(e, "sp")


def _I(name, *a, **k):
    return lambda e: getattr(e, name)(*a, **k)


def _ap3(base2d, dims):
    return bass.AP(base2d.tensor, base2d.offset, [list(base2d.ap[0])] + [list(d) for d in dims])


class _Stop(Exception):
    pass


DBG_STOP = None


def build_nc(L=DEPTH):
    nc = bass.Bass("TRN2", target_bir_lowering=False)
    dram = {}

    def din(name, shape):
        dram[name] = nc.dram_tensor(name, list(shape), F32, kind="ExternalInput").ap()
        return dram[name]

    xT_d = din("xT", [D, S])
    cT_d = din("cT", [128, 8])
    pp_d = din("pp", [128, DEPTH * NPP])
    bt_d = din("bt", [DEPTH, 128, 8 * 3 * 128])
    ident_d = din("ident", [128, 128])
    w_ada_d = din("w_ada", [DEPTH, D, 6 * D])
    w_in_d = din("w_in", [DEPTH, D, 2048])
    w_pool_d = din("w_pool", [DEPTH, 4, 128, 128])
    w_out_d = din("w_out", [DEPTH, D, D])
    w_ff1_d = din("w_ff1", [DEPTH, D, 4 * D])
    w_ff2_d = din("w_ff2", [DEPTH, 4 * D, D])
    oT_d = nc.dram_tensor("oT", [D, S], F32, kind="ExternalOutput").ap()

    sb = {}
    SB_SPECS = [
        ("X", 16384, F32), ("A", 16384, BF16), ("Y", 16384, BF16), ("B", 16384, BF16),
        ("E", 4096, BF16), ("BT", 3072, BF16), ("PP", DEPTH * NPP, F32), ("T", 4608, F32),
        ("MOD", 48, F32), ("COEF", 64, F32), ("CT", 8, F32), ("CACT", 8, BF16),
        ("ONES", 128, BF16), ("INVC", 16, F32), ("WP", 512, BF16),
        ("ZQ", 4096, BF16), ("EPS", 1, F32), ("ROW", 512, F32), ("ONE1", 1, F32), ("PF", 16, F32), ("IDENT", 128, BF16),
    ]
    for name, n, dt in SB_SPECS:
        sb[name] = nc.alloc_sbuf_tensor("sb_" + name, [128, n], dt)
    ps = [nc.alloc_psum_tensor("ps%d" % i, [128, 512], F32) for i in range(8)]

    sc = Sched(nc)
    for name, n, dt in SB_SPECS:
        sc.region(name, n)
    for i in range(8):
        sc.region("PS%d" % i, 512)

    X, A, Y, B, E, BT, PP, T = (sb[k] for k in ("X", "A", "Y", "B", "E", "BT", "PP", "T"))
    MOD, COEF, CT, CACT, ONES, INVC, WP = (sb[k] for k in ("MOD", "COEF", "CT", "CACT", "ONES", "INVC", "WP"))
    ZQ, EPSC, ROW, ONE1, PF, IDENT = sb["ZQ"], sb["EPS"], sb["ROW"], sb["ONE1"], sb["PF"], sb["IDENT"]
    coef_eps = EPSC[:, 0:1]
    Tb = T[:, :].bitcast(BF16)

    def coef(i):
        return COEF[:, i * 8:(i + 1) * 8]

    def rC(i):
        return ("COEF", i * 8, i * 8 + 8)

    def ppv(l, off, n):
        return PP[:, l * NPP + off: l * NPP + off + n]

    rPP = ("PP", 0, DEPTH * NPP)

    for c in range(8):
        sc.dma("sp", "x%d" % c, X[:, c * S:(c + 1) * S], xT_d[c * 128:(c + 1) * 128, :],
               writes=[("X", c * S, (c + 1) * S)])
    sc.dma("sp", "consts", CT[:, :], cT_d, writes=[("CT", 0, 8)])
    sc.dma("sp", "consts2", PP[:, :], pp_d, writes=[rPP])
    sc.dma("pool", "ident", IDENT[:, :], ident_d, writes=[("IDENT", 0, 128)])
    sc.op("dve", _I("memset", ONES[:, :], 1.0 / 1024.0), writes=[("ONES", 0, 128)])
    sc.op("dve", _I("memset", EPSC[:, :], EPS_P), writes=[("EPS", 0, 1)])
    sc.op("dve", _I("memset", ONE1[:, :], 1.0), writes=[("ONE1", 0, 1)])
    for t in range(16):
        sc.op("dve", _I("memset", INVC[:, t:t + 1], 1.0 / (t + 1)), writes=[("INVC", t, t + 1)])
    sc.op("act", _I("activation", out=CACT[:, :], in_=CT[:, :], func=AF.Silu),
          reads=[("CT", 0, 8)], writes=[("CACT", 0, 8)])

    est = {"n": 0}

    egen = {}

    def e_load(src_ap):
        s = est["n"] % 4
        egen[s] = est["n"]
        est["n"] += 1
        dst = _ap3(E[:, s * 1024:(s + 1) * 1024], [[128, 8], [1, 128]])
        sc.dma("pool", "e%d" % s, dst, src_ap, writes=[("E", s * 1024, (s + 1) * 1024)])
        return s + 4 * egen[s]

    def e_slot(tok):
        s = tok % 4
        assert egen[s] == tok // 4, "E-ring slot %d reused before its consumers were emitted" % s
        return s

    def e_w(s, k):
        return E[:, s * 1024 + k * 128: s * 1024 + (k + 1) * 128]

    def rE(s):
        return ("E", s * 1024, (s + 1) * 1024)

    psrot = {"n": 0}

    def next_ps(lo=0, n=6):
        i = lo + psrot["n"] % n
        psrot["n"] += 1
        return i

    def ada_steps(l, big_ring=False):
        wv = w_ada_d[l].rearrange("(k p) e -> p k e", p=128)
        rowb, colb = 6, 7
        steps = []

        def mk(nb, kp):
            def f():
                if big_ring:
                    s = 8 + (nb * 4 + kp) % 8
                    RG, rgname, chan = B, "B", "a%d" % s
                else:
                    s = est["n"] % 4
                    egen[s] = est["n"]
                    est["n"] += 1
                    RG, rgname, chan = E, "E", "e%d" % s
                dst = _ap3(RG[:, s * 1024:(s + 1) * 1024], [[512, 2], [1, 512]])
                sc.dma("pool", chan, dst, wv[:, 2 * kp:2 * kp + 2, nb * 512:(nb + 1) * 512],
                       writes=[(rgname, s * 1024, (s + 1) * 1024)])
                for kk in range(2):
                    k = 2 * kp + kk
                    sc.op("pe", _I("matmul", ps[rowb][0:1, :], lhsT=CACT[:, k:k + 1],
                                   rhs=RG[:, s * 1024 + kk * 512: s * 1024 + (kk + 1) * 512], start=(k == 0), stop=(k == 7)),
                          reads=[(rgname, s * 1024, (s + 1) * 1024), ("CACT", 0, 8)], writes=[("PS%d" % rowb, 0, 512)],
                          signal=(k == 7))
                if kp == 3:
                    sc.op("act", _I("activation", out=ROW[0:1, :], in_=ps[rowb][0:1, :], func=AF.Copy),
                          reads=[("PS%d" % rowb, 0, 512)], writes=[("ROW", 0, 512)])
                    for q in range(4):
                        j = nb * 4 + q
                        sc.op("pe", _I("matmul", ps[colb][:, j:j + 1], lhsT=ROW[0:1, q * 128:(q + 1) * 128], rhs=ONE1[0:1, 0:1],
                                       start=True, stop=True),
                              reads=[("ROW", 0, 512), ("ONE1", 0, 1)], writes=[("PS%d" % colb, j, j + 1)], signal=(q == 3))
                if big_ring and nb == 3 and kp == 3:
                    sc.op("dve", _I("tensor_tensor", out=MOD[:, 0:16], in0=ps[colb][:, 0:16], in1=ppv(l, 0, 16), op=ALU.add),
                          reads=[("PS%d" % colb, 0, 16), rPP], writes=[("MOD", 0, 16)])
                if nb == 11 and kp == 3:
                    c0_ = 16 if big_ring else 0
                    sc.op("dve", _I("tensor_tensor", out=MOD[:, c0_:48], in0=ps[colb][:, c0_:48], in1=ppv(l, c0_, 48 - c0_), op=ALU.add),
                          reads=[("PS%d" % colb, c0_, 48), rPP], writes=[("MOD", c0_, 48)])
            return f

        for nb in range(12):
            for kp in range(4):
                steps.append(mk(nb, kp))
        return steps

    def emit_ada(l, bank):
        for f in ada_steps(l, big_ring=(l == 0)):
            f()

    def emit_coef_layer(l):
        rM = ("MOD", 0, 48)
        sc.op("dve", _I("tensor_scalar", out=coef(2), in0=MOD[:, 16:24], scalar1=1.0, scalar2=1.0 / ALPHA,
                                               op0=ALU.add, op1=ALU.mult), reads=[rM], writes=[rC(2)])
        sc.op("dve", _I("tensor_scalar", out=coef(5), in0=MOD[:, 40:48], scalar1=1.0, scalar2=1.0 / ALPHA,
                                               op0=ALU.add, op1=ALU.mult), reads=[rM], writes=[rC(5)])
        sc.op("dve", _I("scalar_tensor_tensor", out=coef(3), in0=MOD[:, 32:40], scalar=1.0, in1=ppv(l, 52, 8),
                                                      op0=ALU.add, op1=ALU.mult), reads=[rM, rPP], writes=[rC(3)])
        sc.op("dve", _I("scalar_tensor_tensor", out=coef(4), in0=MOD[:, 32:40], scalar=1.0, in1=ppv(l, 60, 8),
                                                      op0=ALU.add, op1=ALU.mult), reads=[rM, rPP], writes=[rC(4)])
        sc.op("dve", _I("tensor_tensor", out=coef(4), in0=coef(4), in1=MOD[:, 24:32], op=ALU.add),
              reads=[rM, rC(4)], writes=[rC(4)])

    def emit_coef_next(lprev):
        rM = ("MOD", 0, 48)
        sc.op("dve", _I("scalar_tensor_tensor", out=coef(6), in0=MOD[:, 8:16], scalar=1.0, in1=ppv(lprev, 68, 8),
                                                      op0=ALU.add, op1=ALU.mult), reads=[rM, rPP], writes=[rC(6)])
        sc.op("dve", _I("scalar_tensor_tensor", out=coef(7), in0=MOD[:, 8:16], scalar=1.0, in1=ppv(lprev, 76, 8),
                                                      op0=ALU.add, op1=ALU.mult), reads=[rM, rPP], writes=[rC(7)])
        sc.op("dve", _I("tensor_tensor", out=coef(7), in0=coef(7), in1=MOD[:, 0:8], op=ALU.add),
              reads=[rM, rC(7)], writes=[rC(7)])

    zb = Tb[:, 0:4096]

    def emit_ln_p1(b):
        c0 = b * 512
        zv = _ap3(X[:, c0:c0 + 512], [[S, 8], [1, 512]])
        rX = [("X", k * S + c0, k * S + c0 + 512) for k in range(8)]
        zb3 = _ap3(zb, [[512, 8], [1, 512]])
        bm, bq = (4, 5) if b % 2 == 0 else (6, 7)
        sc.op("dve", _I("tensor_copy", out=zb3, in_=zv), reads=rX, writes=[("T", 0, 2048)])
        for k in range(8):
            sc.op("act", _I("activation", out=ZQ[:, k * 512:(k + 1) * 512], in_=X[:, k * S + c0:k * S + c0 + 512], func=AF.Square),
                  reads=[rX[k]], writes=[("ZQ", k * 512, (k + 1) * 512)])
        for k in range(8):
            sc.op("pe", _I("matmul", ps[bm][:, :], lhsT=ONES[:, :], rhs=zb[:, k * 512:(k + 1) * 512],
                           start=(k == 0), stop=(k == 7)),
                  reads=[("T", k * 256, (k + 1) * 256), ("ONES", 0, 128)], writes=[("PS%d" % bm, 0, 512)], signal=(k == 7))
        for k in range(8):
            sc.op("pe", _I("matmul", ps[bq][:, :], lhsT=ONES[:, :], rhs=ZQ[:, k * 512:(k + 1) * 512],
                           start=(k == 0), stop=(k == 7)),
                  reads=[("ZQ", k * 512, (k + 1) * 512), ("ONES", 0, 128)], writes=[("PS%d" % bq, 0, 512)], signal=(k == 7))

    def emit_ln_p2(b, l, which, last):
        c0 = b * 512
        zv = _ap3(X[:, c0:c0 + 512], [[S, 8], [1, 512]])
        rX = [("X", k * S + c0, k * S + c0 + 512) for k in range(8)]
        bm, bq = (4, 5) if b % 2 == 0 else (6, 7)
        mean_sb, tmp, rstd = T[:, 2048:2560], T[:, 2560:3072], T[:, 3072:3584]
        rMean, rTmp, rRstd = ("T", 2048, 2560), ("T", 2560, 3072), ("T", 3072, 3584)
        sc.op("act", _I("activation", out=mean_sb, in_=ps[bm][:, :], func=AF.Copy), reads=[("PS%d" % bm, 0, 512)], writes=[rMean])
        sc.op("act", _I("activation", out=tmp, in_=ps[bm][:, :], func=AF.Square), reads=[("PS%d" % bm, 0, 512)], writes=[rTmp])
        sc.op("dve", _I("tensor_tensor", out=tmp, in0=ps[bq][:, :], in1=tmp, op=ALU.subtract),
              reads=[("PS%d" % bq, 0, 512), rTmp], writes=[rTmp])
        sc.op("act", _I("activation", out=tmp, in_=tmp, func=AF.Ln, bias=coef_eps), reads=[rTmp, ("EPS", 0, 1)], writes=[rTmp])
        sc.op("act", _I("activation", out=rstd, in_=tmp, func=AF.Exp, scale=-0.5), reads=[rTmp], writes=[rRstd])
        sc.op("dve", _I("scalar_tensor_tensor", out=mean_sb, in0=mean_sb, scalar=-1.0, in1=rstd, op0=ALU.mult, op1=ALU.mult),
              reads=[rMean, rRstd], writes=[rMean])
        nmr_b = _ap3(mean_sb, [[0, 8], [1, 512]])
        rstd_b = _ap3(rstd, [[0, 8], [1, 512]])
        sc.op("dve", _I("tensor_tensor", out=zv, in0=zv, in1=rstd_b, op=ALU.mult), reads=rX + [rRstd], writes=rX)
        sc.op("dve", _I("tensor_tensor", out=zv, in0=zv, in1=nmr_b, op=ALU.add), reads=rX + [rMean], writes=rX)
        if which == 1:
            ca, cb_, g_off, b_off = 3, 4, 52, 60
        else:
            ca, cb_, g_off, b_off = 6, 7, 68, 76
        for k in range(8):
            xk = X[:, k * S + c0: k * S + c0 + 512]
            rXk = ("X", k * S + c0, k * S + c0 + 512)
            if not (which == 2 and last):
                ak = A[:, k * S + c0: k * S + c0 + 512]
                sc.op("act", _I("activation", out=ak, in_=xk, func=AF.Identity,
                                scale=coef(ca)[:, k:k + 1], bias=coef(cb_)[:, k:k + 1]),
                      reads=[rXk, rC(ca), rC(cb_)], writes=[("A", k * S + c0, k * S + c0 + 512)])
            sc.op("dve", _I("tensor_scalar", out=xk, in0=xk, scalar1=ppv(l, g_off + k, 1),
                            scalar2=ppv(l, b_off + k, 1), op0=ALU.mult, op1=ALU.add),
                  reads=[rXk, rPP], writes=[rXk])

    def emit_ln(b, l, which, last):
        emit_ln_p1(b)
        emit_ln_p2(b, l, which, last)

    state = {}

    def out_block(b):
        for c in range(8):
            lo = c * S + b * 512
            sc.dma("sp", "out", oT_d[c * 128:(c + 1) * 128, b * 512:(b + 1) * 512], X[:, lo:lo + 512], reads=[("X", lo, lo + 512)])
        if b == 3:
            state["out_done"] = True

    def stop(tag):
        if DBG_STOP == tag:
            raise _Stop()

    def body():
        ada0 = ada_steps(0, big_ring=True)
        for f_ in ada0[:16]:
            f_()
        ada0 = ada0[16:]
        rM = ("MOD", 0, 16)
        sc.op("dve", _I("tensor_scalar_add", out=coef(0), in0=MOD[:, 8:16], scalar1=1.0), reads=[rM], writes=[rC(0)])
        sc.op("dve", _I("tensor_copy", out=coef(1), in_=MOD[:, 0:8]), reads=[rM], writes=[rC(1)])
        for k in range(8):
            for b in range(4):
                lo = k * S + b * 512
                if (k + b) % 2 == 0:
                    sc.op("act", _I("activation", out=A[:, lo:lo + 512], in_=X[:, lo:lo + 512], func=AF.Identity,
                                    scale=coef(0)[:, k:k + 1], bias=coef(1)[:, k:k + 1]),
                          reads=[("X", lo, lo + 512), rC(0), rC(1)], writes=[("A", lo, lo + 512)])
                else:
                    sc.op("dve", _I("tensor_scalar", out=A[:, lo:lo + 512], in0=X[:, lo:lo + 512],
                                    scalar1=coef(0)[:, k:k + 1], scalar2=coef(1)[:, k:k + 1], op0=ALU.mult, op1=ALU.add),
                          reads=[("X", lo, lo + 512), rC(0), rC(1)], writes=[("A", lo, lo + 512)])

        stop('h0')
        pre0 = {}
        for l in range(L):
            last = (l == L - 1)
            if l > 0:
                emit_coef_layer(l)
            win = w_in_d[l].rearrange("(k p) e -> p k e", p=128)
            sc.dma("pool", "wp", _ap3(WP[:, 0:512], [[128, 4], [1, 128]]), w_pool_d[l].rearrange("g c d -> c g d"),
                   writes=[("WP", 0, 512)])
            sc.dma("pool", "bt", BT[:, :], bt_d[l], writes=[("BT", 0, 3072)])

            ZQf = ZQ[:, :].bitcast(F32)

            def pool_bufs(bb):
                if bb == 0:
                    return [(ZQf[:, o:o + 528], ("ZQ", 2 * o, 2 * (o + 528))) for o in (0, 528, 1056)]
                return [(T[:, o:o + 528], ("T", o, o + 528)) for o in (1664, 2192, 2720)]

            pstate = {}

            def pool_F1(g, b):
                def f():
                    if b == 0:
                        pstate["s"] = e_load(win[:, :, g * 128:(g + 1) * 128])
                    s = e_slot(pstate["s"])
                    (u, rU), _, _ = pool_bufs(b % 2)
                    bank = next_ps(6, 2)
                    for k in range(8):
                        sc.op("pe", _I("matmul", ps[bank][:, :], lhsT=e_w(s, k),
                                       rhs=A[:, k * S + b * 512: k * S + (b + 1) * 512], start=(k == 0), stop=(k == 7)),
                              reads=[rE(s), ("A", k * S + b * 512, k * S + (b + 1) * 512)],
                              writes=[("PS%d" % bank, 0, 512)], signal=(k == 7))
                    rg, lo_, hi_ = rU
                    sc_ = 2 if rg == "ZQ" else 1
                    if b == 0:
                        sc.op("dve", _I("memset", u[:, 0:16], 0.0), writes=[(rg, lo_, lo_ + 16 * sc_)])
                    else:
                        (pu, rPU), _, _ = pool_bufs((b - 1) % 2)
                        prg, plo, phi = rPU
                        sc.op("dve", _I("tensor_copy", out=u[:, 0:16], in_=pu[:, 512:528]),
                              reads=[(prg, phi - 16 * (2 if prg == "ZQ" else 1), phi)], writes=[(rg, lo_, lo_ + 16 * sc_)])
                    sc.op("act", _I("activation", out=u[:, 16:528], in_=ps[bank][:, :], func=AF.Copy),
                          reads=[("PS%d" % bank, 0, 512)], writes=[(rg, lo_ + 16 * sc_, hi_)])
                    w = POOL_W[g]
                    (u, rU), (sa, rSa), (sb_, rSb) = pool_bufs(b % 2)
                    src, rsrc = u, rU
                    bufs = [(sa, rSa), (sb_, rSb)]
                    step, lo2 = 1, 1
                    nst = {2: 1, 4: 2, 8: 3, 16: 4}[w]
                    for it in range(nst):
                        dst, rdst = bufs[it % 2]
                        sc.op("dve", _I("tensor_tensor", out=dst[:, lo2:528], in0=src[:, lo2:528],
                                         in1=src[:, lo2 - step:528 - step], op=ALU.add), reads=[rsrc], writes=[rdst])
                        src, rsrc = dst, rdst
                        step *= 2
                        lo2 = 2 * lo2 + 1
                    pdst, rpd = bufs[nst % 2]
                    if rpd[0] == "ZQ":
                        pbf = ZQ[:, rpd[1]:rpd[1] + 512]
                    else:
                        pbf = Tb[:, 2 * rpd[1]:2 * rpd[1] + 512]
                    if b == 0:
                        sc.op("dve", _I("tensor_tensor", out=PF[:, 0:w - 1], in0=src[:, 16:16 + w - 1], in1=INVC[:, 0:w - 1],
                                         op=ALU.mult), reads=[rsrc, ("INVC", 0, 16)], writes=[("PF", 0, 16)])
                    sc.op("dve", _I("tensor_scalar_mul", out=src[:, 16:528], in0=src[:, 16:528], scalar1=1.0 / w),
                          reads=[rsrc], writes=[rsrc])
                    sc.op("dve", _I("tensor_tensor", out=pbf, in0=src[:, 16:528], in1=u[:, 16:528], op=ALU.subtract),
                          reads=[rsrc, rU], writes=[rpd])
                    if b == 0:
                        sc.op("dve", _I("tensor_tensor", out=pbf[:, 0:w - 1], in0=PF[:, 0:w - 1], in1=u[:, 16:16 + w - 1],
                                         op=ALU.subtract), reads=[("PF", 0, 16), rU], writes=[rpd])
                    pstate[(g, b)] = (pbf, rpd)
                return f

            def pool_F3(g, b):
                def f():
                    pbf, rpd = pstate[(g, b)]
                    bank2 = next_ps(6, 2)
                    sc.op("pe", _I("matmul", ps[bank2][:, :], lhsT=WP[:, g * 128:(g + 1) * 128], rhs=pbf, start=True, stop=True),
                          reads=[rpd, ("WP", 0, 512)], writes=[("PS%d" % bank2, 0, 512)])
                    yl = g * S + b * 512
                    sc.op("act", _I("activation", out=Y[:, yl:yl + 512], in_=ps[bank2][:, :], func=AF.Copy, scale=ppv(l, 48 + g, 1)),
                          reads=[("PS%d" % bank2, 0, 512), rPP], writes=[("Y", yl, yl + 512)])
                return f

            gb = [(g, b) for g in range(4) for b in range(4)]
            pool_pieces = []
            for idx in range(18):
                fs = []
                if idx >= 2:
                    fs.append(pool_F3(*gb[idx - 2]))
                if idx < 16:
                    fs.append(pool_F1(*gb[idx]))
                pool_pieces.append(lambda fs=fs: [f() for f in fs])

            stop('pool')

            def qkv_pieces(c, wsrc=None, pslo=6, psn=2):
                wsrc = win if wsrc is None else wsrc
                j = c % 2
                bo = j * 8192
                qo, ko, vo = bo, bo + 2048, bo + 4096
                pieces = []
                st = {}

                def mk_load(key, col0):
                    def f():
                        st[key] = e_load(wsrc[:, :, col0 + c * 128: col0 + (c + 1) * 128])
                    return f

                def mk_qk(key, dsto, scl, b):
                    def f():
                        s = e_slot(st[key])
                        bank = next_ps(pslo, psn)
                        for k in range(8):
                            sc.op("pe", _I("matmul", ps[bank][:, :], lhsT=e_w(s, k),
                                           rhs=A[:, k * S + b * 512: k * S + (b + 1) * 512], start=(k == 0), stop=(k == 7)),
                                  reads=[rE(s), ("A", k * S + b * 512, k * S + (b + 1) * 512)],
                                  writes=[("PS%d" % bank, 0, 512)], signal=(k == 7))
                        dl = dsto + b * 512
                        sc.op("act", _I("activation", out=B[:, dl:dl + 512], in_=ps[bank][:, :], func=AF.Copy, scale=scl),
                              reads=[("PS%d" % bank, 0, 512)], writes=[("B", dl, dl + 512)])
                    return f

                def mk_v(t):
                    def f():
                        s = e_slot(st["v"])
                        tt = t % 4
                        if tt == 0:
                            st["vbank"] = next_ps(pslo, psn)
                        bank = st["vbank"]
                        for k in range(8):
                            sc.op("pe", _I("matmul", ps[bank][:, tt * 128:(tt + 1) * 128],
                                           lhsT=A[:, k * S + t * 128: k * S + (t + 1) * 128], rhs=e_w(s, k),
                                           start=(k == 0), stop=(k == 7)),
                                  reads=[rE(s), ("A", k * S + t * 128, k * S + (t + 1) * 128)],
                                  writes=[("PS%d" % bank, tt * 128, (tt + 1) * 128)], signal=(k == 7))
                        if tt == 3:
                            vt = vo + (t - 3) * 192
                            psv = ps[bank]
                            srcA = _ap3(psv[:, 0:64], [[128, 4], [1, 64]])
                            srcB = _ap3(psv[:, 64:128], [[128, 4], [1, 64]])
                            dstA = _ap3(B[:, vt:vt + 64], [[192, 4], [1, 64]])
                            dstB = _ap3(B[:, vt + 128:vt + 192], [[192, 4], [1, 64]])
                            rV = ("B", vt, vt + 4 * 192)
                            sc.op("dve", _I("tensor_copy", out=dstA, in_=srcA), reads=[("PS%d" % bank, 0, 512)], writes=[rV])
                            sc.op("dve", _I("tensor_copy", out=dstB, in_=srcB), reads=[("PS%d" % bank, 0, 512)], writes=[rV])
                    return f

                def mk_ones():
                    def f():
                        onesv = _ap3(B[:, vo + 64: vo + 128], [[192, 16], [1, 64]])
                        sc.op("dve", _I("memset", onesv, 1.0), writes=[("B", vo, vo + 3072)])
                        st["v"] = e_load(wsrc[:, :, 1536 + c * 128: 1536 + (c + 1) * 128])
                    return f

                pieces.append(mk_load("q", 512))
                for b in range(4):
                    pieces.append(mk_qk("q", qo, 0.125, b))
                pieces.append(mk_load("k", 1024))
                for b in range(4):
                    pieces.append(mk_qk("k", ko, 1.0, b))
                pieces.append(mk_ones())
                for t in range(16):
                    pieces.append(mk_v(t))
                return pieces

            def att_S(u, c, m, X_):
                j = c % 2
                bo = j * 8192
                qo, ko = bo, bo + 2048
                sset = u % 2
                U0, U1 = 2 * sset, 2 * sset + 1
                r0 = 64 * X_
                valid = list(range(max(0, 4 - m), 5))
                h = 2 * c + X_
                v034 = [i for i in valid if i in (0, 3, 4)]
                s_lo = 3 - len(v034)
                nn = len(v034) * 128
                bto = h * 384 + s_lo * 128
                sc.op("pe", _I("matmul", ps[U1][:, s_lo * 128: s_lo * 128 + nn], lhsT=IDENT[:, :], rhs=BT[:, bto:bto + nn],
                               start=True, stop=False),
                      reads=[("IDENT", 0, 128), ("BT", bto, bto + nn)], writes=[("PS%d" % U1, 0, 512)], signal=False)
                order = [i for i in valid if i in (0, 3, 4)] + [i for i in valid if i in (1, 2)]
                for i in order:
                    tk = m - 4 + i
                    if i in (1, 2):
                        bank_, col, first = U0, (i - 1) * 128, True
                    else:
                        bank_, col, first = U1, {0: 0, 3: 1, 4: 2}[i] * 128, False
                    sc.op("pe", _I("matmul", ps[bank_][:, col:col + 128],
                                   lhsT=B[r0:r0 + 64, ko + tk * 128: ko + (tk + 1) * 128],
                                   rhs=B[r0:r0 + 64, qo + m * 128: qo + (m + 1) * 128], start=first,
                                   stop=(first or i == 4)),
                          reads=[("B", ko + tk * 128, ko + (tk + 1) * 128), ("B", qo + m * 128, qo + (m + 1) * 128)],
                          writes=[("PS%d" % bank_, col, col + 128)], signal=(i == order[-1] or i == 4))

            def att_SM(u, c, m, X_):
                sset = u % 2
                U0, U1 = 2 * sset, 2 * sset + 1
                h = 2 * c + X_
                valid = list(range(max(0, 4 - m), 5))
                sbo = sset * 384
                pto = (768 + sset * 320) * 2
                v12 = [i for i in valid if i in (1, 2)]
                if v12:
                    c_lo = (v12[0] - 1) * 128
                    n12 = len(v12) * 128
                    dlo = pto + (v12[0] - 1) * 128
                    sc.op("act", _I("activation", out=Tb[:, dlo:dlo + n12], in_=ps[U0][:, c_lo:c_lo + n12],
                                    func=AF.Exp, bias=ppv(l, 84 + h, 1)),
                          reads=[("PS%d" % U0, c_lo, c_lo + n12), rPP], writes=[("T", dlo // 2, (dlo + n12) // 2)])
                v034 = [i for i in valid if i in (0, 3, 4)]
                s_lo = {3: 0, 2: 1, 1: 2}[len(v034)]
                nn = len(v034) * 128
                d0 = pto + (2 + s_lo) * 128
                sc.op("act", _I("activation", out=Tb[:, d0:d0 + nn], in_=ps[U1][:, s_lo * 128: s_lo * 128 + nn], func=AF.Exp),
                      reads=[("PS%d" % U1, 0, 512)], writes=[("T", d0 // 2, (d0 + nn) // 2)])

            def att_PV(u, c, m, X_):
                j = c % 2
                vo = j * 8192 + 4096
                sset = u % 2
                PO = 4 + sset
                pto = (768 + sset * 320) * 2
                valid = list(range(max(0, 4 - m), 5))
                vc = 0 if X_ == 0 else 64
                for n_, i in enumerate(valid):
                    tk = m - 4 + i
                    sl = {1: 0, 2: 1, 0: 2, 3: 3, 4: 4}[i]
                    pc = pto + sl * 128
                    vcol = vo + tk * 192 + vc
                    sc.op("pe", _I("matmul", ps[PO][:, 0:128], lhsT=B[:, vcol:vcol + 128], rhs=Tb[:, pc:pc + 128],
                                   start=(n_ == 0), stop=(n_ == len(valid) - 1)),
                          reads=[("B", vcol, vcol + 128), ("T", pc // 2, (pc + 128) // 2)],
                          writes=[("PS%d" % PO, 0, 128)], signal=(n_ == len(valid) - 1))

            def att_NORM(u, c, m, X_):
                sset = u % 2
                PO = 4 + sset
                dn = 1408 + sset * 128
                yc = (4 + c) * S + m * 128
                if X_ == 0:
                    nlo, dlo_ = 0, 64
                else:
                    nlo, dlo_ = 64, 0
                if X_ == 0:
                    sc.op("dve", _I("reciprocal", out=T[nlo:nlo + 64, dn:dn + 128], in_=ps[PO][dlo_:dlo_ + 64, 0:128]),
                          reads=[("PS%d" % PO, 0, 128)], writes=[("T", dn, dn + 128)])
                else:
                    sc.op("act", _I("activation", out=T[nlo:nlo + 64, dn:dn + 128], in_=ps[PO][dlo_:dlo_ + 64, 0:128], func=AF.Ln),
                          reads=[("PS%d" % PO, 0, 128)], writes=[("T", dn, dn + 128)])
                    sc.op("act", _I("activation", out=T[nlo:nlo + 64, dn:dn + 128], in_=T[nlo:nlo + 64, dn:dn + 128],
                                    func=AF.Exp, scale=-1.0),
                          reads=[("T", dn, dn + 128)], writes=[("T", dn, dn + 128)])
                sc.op("dve", _I("tensor_tensor", out=Y[nlo:nlo + 64, yc:yc + 128], in0=ps[PO][nlo:nlo + 64, 0:128],
                                in1=T[nlo:nlo + 64, dn:dn + 128], op=ALU.mult),
                      reads=[("PS%d" % PO, 0, 128), ("T", dn, dn + 128)], writes=[("Y", yc, yc + 128)])

            units = [(c, m, X_) for c in range(4) for m in range(16) for X_ in range(2)]
            if not pre0.get(l):
                if l == 0:
                    pcs0 = qkv_pieces(0, pslo=0, psn=4)
                    for i_, f in enumerate(pcs0):
                        f()
                        for _ in range(2 if i_ < 5 else 1):
                            if ada0:
                                ada0.pop(0)()
                    while ada0:
                        ada0.pop(0)()
                    emit_coef_layer(0)
                else:
                    for f in qkv_pieces(0):
                        f()
            stop('att_a')
            fill = []
            for u, (c, m, X_) in enumerate(units):
                if m == 0 and X_ == 0:
                    for f in fill:
                        f()
                    fill = (qkv_pieces(c + 1) if c + 1 < 4 else []) + pool_pieces[6 * c:6 * c + 6]
                    if c == 3:
                        wo = w_out_d[l].rearrange("(k p) e -> p k e", p=128)
                        for hh in range(2):
                            sc.dma("pool", "wo%d" % hh, _ap3(B[:, hh * 4096:(hh + 1) * 4096], [[1024, 4], [1, 1024]]),
                                   wo[:, hh * 4:(hh + 1) * 4, :], writes=[("B", hh * 4096, (hh + 1) * 4096)])
                if u == 0:
                    att_S(u, c, m, X_)
                    att_SM(u, c, m, X_)
                if u + 1 < len(units):
                    att_S(u + 1, *units[u + 1])
                    att_SM(u + 1, *units[u + 1])
                att_PV(u, c, m, X_)
                if fill:
                    fill.pop(0)()
                att_NORM(u, c, m, X_)

            stop('attn')
            w1v = w_ff1_d[l].rearrange("(k p) f -> p k f", p=128)
            w2v = w_ff2_d[l].rearrange("(fc p) d -> p fc d", p=128)

            def load_chunk(jp, ci):
                so_ = 8192 if jp % 2 == 0 else 0
                ch = "f%d_%d" % (ci, jp % 2)
                if ci < 4:
                    lo = so_ + ci * 1024
                    sc.dma("pool", ch, _ap3(B[:, lo:lo + 1024], [[512, 2], [1, 512]]),
                           w1v[:, 2 * ci:2 * ci + 2, jp * 512:(jp + 1) * 512], writes=[("B", lo, lo + 1024)])
                else:
                    fc = ci - 4
                    lo = so_ + 4096 + fc * 1024
                    sc.dma("pool", ch, B[:, lo:lo + 1024], w2v[:, jp * 4 + fc, :], writes=[("B", lo, lo + 1024)])

            def load_piece(jp):
                for ci in range(8):
                    load_chunk(jp, ci)

            load_piece(0)

            def s1d_mm(b):
                for dc in range(8):
                    bank = next_ps(0, 4)
                    for k in range(8):
                        sc.op("pe", _I("matmul",
                            ps[bank][:, :], lhsT=B[:, k * 1024 + dc * 128: k * 1024 + (dc + 1) * 128],
                            rhs=Y[:, k * S + b * 512: k * S + (b + 1) * 512], start=(k == 0), stop=(k == 7)),
                            reads=[("B", k * 1024 + dc * 128, k * 1024 + (dc + 1) * 128), ("Y", k * S + b * 512, k * S + (b + 1) * 512)],
                            writes=[("PS%d" % bank, 0, 512)], signal=(k == 7))
                    xl = dc * S + b * 512
                    sc.op("dve", _I("scalar_tensor_tensor",
                        out=X[:, xl:xl + 512], in0=ps[bank][:, :], scalar=coef(2)[:, dc:dc + 1], in1=X[:, xl:xl + 512],
                        op0=ALU.mult, op1=ALU.add),
                        reads=[("PS%d" % bank, 0, 512), rC(2), ("X", xl, xl + 512)], writes=[("X", xl, xl + 512)])

            asteps = ada_steps(l + 1) if not last else []

            def ffn_ff1(jp, b, narrow):
                so_ = 8192 if jp % 2 == 0 else 0
                if 1 <= jp <= 6:
                    for _ in range(2):
                        if asteps:
                            asteps.pop(0)()
                    if not asteps and not last and jp == 6 and b == 3:
                        emit_coef_next(l)
                fb = (jp * 4 + b) % 3
                fo = fb * S
                for fc in range(4):
                    bank = next_ps(0, 2) if narrow else next_ps(0, 3)
                    for k in range(8):
                        sc.op("pe", _I("matmul",
                            ps[bank][:, :], lhsT=B[:, so_ + k * 512 + fc * 128: so_ + k * 512 + (fc + 1) * 128],
                            rhs=A[:, k * S + b * 512: k * S + (b + 1) * 512], start=(k == 0), stop=(k == 7)),
                            reads=[("B", so_ + k * 512 + fc * 128, so_ + k * 512 + (fc + 1) * 128),
                                   ("A", k * S + b * 512, k * S + (b + 1) * 512)],
                            writes=[("PS%d" % bank, 0, 512)], signal=(k == 7))
                    rt = 3584 + (fc % 2) * 512
                    sc.op("act", _I("activation", out=T[:, rt:rt + 512], in_=ps[bank][:, :], func=AF.Relu),
                          reads=[("PS%d" % bank, 0, 512)], writes=[("T", rt, rt + 512)])
                    fl = fo + fc * 512
                    sc.op("act", _I("activation", out=Y[:, fl:fl + 512], in_=T[:, rt:rt + 512], func=AF.Square),
                          reads=[("T", rt, rt + 512)], writes=[("Y", fl, fl + 512)])

            def ffn_ff2(jp, b, narrow):
                so_ = 8192 if jp % 2 == 0 else 0
                if jp + 1 < 8:
                    load_chunk(jp + 1, 2 * b)
                    load_chunk(jp + 1, 2 * b + 1)
                fb = (jp * 4 + b) % 3
                fo = fb * S
                for dc in range(8):
                    bank = next_ps(2, 2) if narrow else next_ps(3, 3)
                    for fc in range(4):
                        sc.op("pe", _I("matmul",
                            ps[bank][:, :], lhsT=B[:, so_ + 4096 + fc * 1024 + dc * 128: so_ + 4096 + fc * 1024 + (dc + 1) * 128],
                            rhs=Y[:, fo + fc * 512: fo + (fc + 1) * 512], start=(fc == 0), stop=(fc == 3)),
                            reads=[("B", so_ + 4096 + fc * 1024 + dc * 128, so_ + 4096 + fc * 1024 + (dc + 1) * 128),
                                   ("Y", fo + fc * 512, fo + (fc + 1) * 512)],
                            writes=[("PS%d" % bank, 0, 512)], signal=(fc == 3))
                    xl = dc * S + b * 512
                    sc.op("dve", _I("scalar_tensor_tensor",
                        out=X[:, xl:xl + 512], in0=ps[bank][:, :], scalar=coef(5)[:, dc:dc + 1], in1=X[:, xl:xl + 512],
                        op0=ALU.mult, op1=ALU.add),
                        reads=[("PS%d" % bank, 0, 512), rC(5), ("X", xl, xl + 512)], writes=[("X", xl, xl + 512)])

            def ffn_slot(jp, b, narrow):
                ffn_ff1(jp, b, narrow)
                ffn_ff2(jp, b, narrow)

            R = sc.record

            def mmp1(b):
                return R(lambda: (s1d_mm(b), emit_ln_p1(b)))

            def p2(b, which):
                return R(lambda: emit_ln_p2(b, l, which, last))

            def zip_p2(sa, b, which):
                sb_ = p2(b, which)
                for st_ in sb_[:3]:
                    sc.replay(st_)
                sc.zip_emit(sa, sb_[3:])

            s1d_mm(0); emit_ln_p1(0)
            s1d_mm(1); emit_ln_p1(1)
            zip_p2(mmp1(2), 0, 1)
            zip_p2(mmp1(3), 1, 1)
            zip_p2(R(lambda: ffn_slot(0, 0, True)), 2, 1)
            zip_p2(R(lambda: ffn_slot(0, 1, True)), 3, 1)
            ffn_slot(0, 2, True)
            ffn_slot(0, 3, True)
            stop('ln1')
            slots = [(jp, b) for jp in range(1, 7) for b in range(4)]
            ffn_ff1(slots[0][0], slots[0][1], False)
            for i, (jp, b) in enumerate(slots):
                if i + 1 < len(slots):
                    ffn_ff1(slots[i + 1][0], slots[i + 1][1], False)
                ffn_ff2(jp, b, False)
            ffn_slot(7, 0, True); emit_ln_p1(0)
            ffn_slot(7, 1, True); emit_ln_p1(1)
            zip_p2(R(lambda: (ffn_slot(7, 2, True), emit_ln_p1(2))), 0, 2)
            if last:
                out_block(0)
            zip_p2(R(lambda: (ffn_slot(7, 3, True), emit_ln_p1(3))), 1, 2)
            if last:
                out_block(1)
            stop('ffn')
            if not last:
                wnext = w_in_d[l + 1].rearrange("(k p) e -> p k e", p=128)
                pcs = qkv_pieces(0, wsrc=wnext, pslo=0, psn=4)
                for i_ in (0, 5, 10):
                    pcs[i_]()

                def grp(b):
                    return [pcs[1 + b], pcs[6 + b]] + [pcs[11 + 4 * b + t] for t in range(4)]
                zip_p2(R(lambda: [f() for f in grp(0) + grp(1)]), 2, 2)
                zip_p2(R(lambda: [f() for f in grp(2)]), 3, 2)
                for f in grp(3):
                    f()
                pre0[l + 1] = True
            else:
                emit_ln_p2(2, l, 2, last)
                out_block(2)
                emit_ln_p2(3, l, 2, last)
                out_block(3)

    try:
        body()
    except _Stop:
        pass

    if DBG_STOP is not None or not state.get("out_done"):
        for c in range(8):
            sc.dma("sp", "out", oT_d[c * 128:(c + 1) * 128, :], X[:, c * S:(c + 1) * S], reads=[("X", c * S, (c + 1) * S)])
    sc.final_wait("sp", ["out"])

    with nc.Block() as block:
        sc.emit(block)
    return nc


def _bias_tables(rel_bias):
    p = np.arange(128)[:, None]
    q = np.arange(128)[None, :]
    out = np.empty((DEPTH, 128, 8, 3, 128), np.float32)
    for bi, i in enumerate((0, 3, 4)):
        diff = 128 * (4 - i) + q - p
        idx = np.clip(diff, -63, 128) + 63
        tab = rel_bias[:, :, idx]
        if i == 0:
            mask = (p < 64) & (q >= 64)
        elif i == 4:
            mask = (p >= 64) & (q < 64)
        else:
            mask = np.zeros((128, 128), bool)
        tab = np.where(mask[None, None], np.float32(MASKV), tab)
        out[:, :, :, bi, :] = np.transpose(tab, (0, 2, 1, 3))
    return out.reshape(DEPTH, 128, 8 * 3 * 128)


def _col(v):
    return np.ascontiguousarray(v.reshape(-1, 128).T)


_NC_CACHE = {}


def kernel(x, c, w_ada, b_ada, w_in, w_pool, pool_scale, rel_bias, w_out,
           ln1_g, ln1_b, w_ff1, w_ff2, ln2_g, ln2_b):
    f = lambda a: np.ascontiguousarray(np.asarray(a, dtype=np.float32))
    x, c, w_ada, b_ada, w_in, w_pool, pool_scale, rel_bias, w_out = map(
        f, (x, c, w_ada, b_ada, w_in, w_pool, pool_scale, rel_bias, w_out))
    ln1_g, ln1_b, w_ff1, w_ff2, ln2_g, ln2_b = map(f, (ln1_g, ln1_b, w_ff1, w_ff2, ln2_g, ln2_b))
    pp = np.zeros((128, DEPTH * NPP), np.float32)
    for l in range(DEPTH):
        o = l * NPP
        pp[:, o:o + 48] = _col(b_ada[l])
        pp[:, o + 48:o + 52] = _col(pool_scale[l])
        pp[:, o + 52:o + 60] = _col(ln1_g[l])
        pp[:, o + 60:o + 68] = _col(ln1_b[l])
        pp[:, o + 68:o + 76] = _col(ln2_g[l])
        pp[:, o + 76:o + 84] = _col(ln2_b[l])
        pp[:, o + 84:o + 92] = np.broadcast_to(rel_bias[l, :, 191][None, :], (128, 8))
    bt = _bias_tables(rel_bias)
    L = _NC_CACHE.get("L", DEPTH)
    if ("nc", L) not in _NC_CACHE:
        _NC_CACHE[("nc", L)] = build_nc(L)
    nc = _NC_CACHE[("nc", L)]
    in_maps = []
    for b in range(NCORES):
        in_maps.append({
            "xT": np.ascontiguousarray(x[b].T), "cT": _col(c[b]), "pp": pp, "bt": bt, "ident": np.eye(128, dtype=np.float32),
            "w_ada": w_ada, "w_in": w_in, "w_pool": w_pool, "w_out": w_out, "w_ff1": w_ff1, "w_ff2": w_ff2,
        })
    res = run_bass_kernel_spmd(nc, in_maps, core_ids=list(range(NCORES)))
    out = np.empty((NCORES, S, D), np.float32)
    for b in range(NCORES):
        out[b] = np.asarray(res.results[b]["oT"]).T
    return out
```
